# Optimizing a Trainium2 kernel written in Bass

```python
import math
import jax
import jax.numpy as jnp
from jax import lax
import numpy as np

D_MODEL = 1024
BATCH = 4
SEQ = 8192
DEPTH = 2

GRID_W = 64
CTX_LEN = 256
CHUNK = GRID_W
N_EVEN = (DEPTH + 1) // 2
N_ODD = DEPTH // 2
EPS = 1e-6

GDN_HEADS = 4
GDN_DK = 128
GDN_DV = 128
GDN_QK = GDN_HEADS * GDN_DK
GDN_V = GDN_HEADS * GDN_DV
SHORT_CONV = 3

GLA_HEADS = 4
GLA_DK = 64
GLA_DV = 128
GLA_QK = GLA_HEADS * GLA_DK
GLA_V = GLA_HEADS * GLA_DV
GLA_GATE_RANK = 16
GLA_GATE_TAU = 16.0

AB_SIZES = (GDN_QK, GDN_QK, GDN_V, GDN_V, 2 * GDN_HEADS, 2 * GDN_HEADS, GLA_QK, GLA_QK, GLA_V, GLA_V, 2 * GLA_GATE_RANK)
AB_SPLITS = tuple(sum(AB_SIZES[: i + 1]) for i in range(len(AB_SIZES) - 1))
AB_WIDTH = sum(AB_SIZES)
MIX_WIDTH = GDN_V + GLA_V

HY_ORDER = 2
HY_SHORT = 3
HY_EMB = 33
HY_BANDS = (HY_EMB - 1) // 2
HY_HIDDEN = 64
HY_MIN_DECAY = math.log(1e-2) / 1.5
HY_MAX_DECAY = math.log(1e-2) / 0.3
HY_SHIFT = 0.05

D_FF = -(-8 * D_MODEL // (3 * 256)) * 256

kernel_name = 'hybrid_gdn_gla_hyena_prefix_dit'


def rmsnorm(x, gain):
    xf = x.astype(jnp.float32)
    y = xf * lax.rsqrt(jnp.mean(xf * xf, axis=-1, keepdims=True) + EPS)
    return (y * gain.astype(jnp.float32)).astype(x.dtype)


def modulate(x, gain, shift, scale):
    return rmsnorm(x, gain) * (1 + scale) + shift


def head_rmsnorm(o, gain):
    return o * lax.rsqrt(jnp.mean(o * o, axis=-1, keepdims=True) + EPS) * gain.astype(jnp.float32)


def l2norm(u):
    return u * lax.rsqrt(jnp.sum(u * u, axis=-1, keepdims=True) + EPS)


def short_conv(x, w):
    k, l = w.shape[0], x.shape[1]
    xp = jnp.pad(x, ((0, 0), (k // 2, k // 2), (0, 0)))
    return sum(xp[:, j:j + l] * w[j] for j in range(k))


def swiglu(u, w1, w3, w2):
    return (jax.nn.silu(u @ w1) * (u @ w3)) @ w2


def to_chunks(t):
    b, l, hh = t.shape[:3]
    rows = l // GRID_W
    t = t.reshape((b, rows, CHUNK, hh) + t.shape[3:])
    return jnp.moveaxis(t, 3, 1)


def from_chunks(t):
    b, hh, n, cl, d = t.shape
    return jnp.moveaxis(t, 1, 3).reshape(b, n * cl, hh, d)


def run_gdn(q, k, v, g, beta, s0, emit):
    q, k, v, g, beta = map(to_chunks, (q, k, v, g, beta))
    dv = v.shape[-1]
    lower = jnp.tril(jnp.ones((CHUNK, CHUNK), dtype=bool))
    strict = jnp.tril(jnp.ones((CHUNK, CHUNK), dtype=bool), -1)
    gc = jnp.cumsum(g, axis=-1)
    decay = jnp.exp(jnp.where(lower, gc[..., :, None] - gc[..., None, :], -jnp.inf))
    kb = k * beta[..., None]
    lmat = jnp.where(strict, jnp.einsum('bhncd,bhnsd->bhncs', kb, k) * decay, 0.0) + jnp.eye(CHUNK, dtype=k.dtype)
    rhs = jnp.concatenate([v * beta[..., None], kb * jnp.exp(gc)[..., None]], axis=-1)
    sol = lax.linalg.triangular_solve(lmat, rhs, left_side=True, lower=True, unit_diagonal=True)
    u0, kcd = sol[..., :dv], sol[..., dv:]
    kd = k * jnp.exp(gc[..., -1:] - gc)[..., None]
    gl = jnp.exp(gc[..., -1])
    xs = (u0, kcd, kd, gl)
    if emit:
        attn = jnp.einsum('bhncd,bhnsd->bhncs', q, k) * decay
        xs = xs + (attn, q * jnp.exp(gc)[..., None])
    xs = tuple(jnp.moveaxis(t, 2, 0) for t in xs)

    def step(s, inp):
        u0_i, kcd_i, kd_i, gl_i = inp[:4]
        u = u0_i - jnp.einsum('bhcd,bhde->bhce', kcd_i, s)
        s_new = s * gl_i[..., None, None] + jnp.einsum('bhcd,bhce->bhde', kd_i, u)
        if not emit:
            return s_new, None
        attn_i, qd_i = inp[4:]
        o = jnp.einsum('bhcd,bhde->bhce', qd_i, s) + jnp.einsum('bhcs,bhse->bhce', attn_i, u)
        return s_new, o

    s_fin, out = lax.scan(step, s0, xs)
    out = from_chunks(jnp.moveaxis(out, 0, 2)) if emit else None
    return out, s_fin


def run_gla(q, k, v, log_a, s0, emit):
    q, k, v, log_a = map(to_chunks, (q, k, v, log_a))
    bc = jnp.cumsum(log_a, axis=-2)
    kd = k * jnp.exp(bc[..., -1:, :] - bc)
    gl = jnp.exp(bc[..., -1, :])
    xs = (kd, v, gl)
    if emit:
        lower = jnp.tril(jnp.ones((CHUNK, CHUNK), dtype=bool))
        ref = bc[..., CHUNK // 2:CHUNK // 2 + 1, :]
        attn = jnp.einsum('bhncd,bhnsd->bhncs', q * jnp.exp(bc - ref), k * jnp.exp(ref - bc))
        attn = jnp.where(lower, attn, 0.0)
        intra = jnp.einsum('bhncs,bhnse->bhnce', attn, v)
        xs = xs + (q * jnp.exp(bc),)
    xs = tuple(jnp.moveaxis(t, 2, 0) for t in xs)

    def step(s, inp):
        kd_i, v_i, gl_i = inp[:3]
        s_new = s * gl_i[..., :, None] + jnp.einsum('bhcd,bhce->bhde', kd_i, v_i)
        if not emit:
            return s_new, None
        o = jnp.einsum('bhcd,bhde->bhce', inp[3], s)
        return s_new, o

    s_fin, inter = lax.scan(step, s0, xs)
    out = from_chunks(jnp.moveaxis(inter, 0, 2) + intra) if emit else None
    return out, s_fin


def orient(t, d):
    return t if d == 0 else jnp.flip(t, axis=1)


def bidirectional(run, ctx_dirs, lat_dirs, s0, ctx_out):
    lat_outs, ctx_outs = [], []
    for d in range(2):
        oc, s_ctx = run(*[orient(t, d) for t in ctx_dirs[d]], s0, ctx_out)
        ol, _ = run(*[orient(t, d) for t in lat_dirs[d]], s_ctx, True)
        lat_outs.append(orient(ol, d))
        if ctx_out:
            ctx_outs.append(orient(oc, d))
    return lat_outs[0] + lat_outs[1], (ctx_outs[0] + ctx_outs[1] if ctx_out else None)


def ab_prepare(t, w_in, conv_w, a_log, dt_bias, gla_gate_w, gla_gate_b):
    b, l, _ = t.shape
    f32 = jnp.float32
    gq, gk, gv, gz, ga, gb, lq, lk, lv, lg, llr = jnp.split(t @ w_in, AB_SPLITS, axis=-1)
    qkv = jax.nn.silu(short_conv(jnp.concatenate([gq, gk, gv], axis=-1), conv_w)).astype(f32)
    gq, gk, gv = jnp.split(qkv, (GDN_QK, 2 * GDN_QK), axis=-1)

    def heads(u, h, d):
        return u.astype(f32).reshape(b, l, h, d)

    gate = jnp.einsum('bldr,drk->bldk', heads(llr, 2, GLA_GATE_RANK), gla_gate_w.astype(f32)) + gla_gate_b.astype(f32)
    return dict(
        gdn_q=l2norm(heads(gq, GDN_HEADS, GDN_DK)) * GDN_DK ** -0.5,
        gdn_k=l2norm(heads(gk, GDN_HEADS, GDN_DK)),
        gdn_v=heads(gv, GDN_HEADS, GDN_DV),
        gdn_g=-jnp.exp(a_log.astype(f32)) * jax.nn.softplus(heads(ga, 2, GDN_HEADS) + dt_bias.astype(f32)),
        gdn_beta=jax.nn.sigmoid(heads(gb, 2, GDN_HEADS)),
        gdn_z=heads(gz, GDN_HEADS, GDN_DV),
        gla_q=heads(lq, GLA_HEADS, GLA_DK) * GLA_DK ** -0.5,
        gla_k=heads(lk, GLA_HEADS, GLA_DK),
        gla_v=heads(lv, GLA_HEADS, GLA_DV),
        gla_log_a=(jax.nn.log_sigmoid(gate) / GLA_GATE_TAU).reshape(b, l, 2, GLA_HEADS, GLA_DK),
        gla_g=heads(lg, GLA_HEADS, GLA_DV),
    )


def ab_mixer(a, ac, w_in, conv_w, a_log, dt_bias, gdn_norm_w, gla_gate_w, gla_gate_b, gla_norm_w, w_out, ctx_out):
    pl = ab_prepare(a, w_in, conv_w, a_log, dt_bias, gla_gate_w, gla_gate_b)
    pc = ab_prepare(ac, w_in, conv_w, a_log, dt_bias, gla_gate_w, gla_gate_b)
    b = a.shape[0]
    s0_gdn = jnp.zeros((b, GDN_HEADS, GDN_DK, GDN_DV), jnp.float32)
    s0_gla = jnp.zeros((b, GLA_HEADS, GLA_DK, GLA_DV), jnp.float32)

    def gdn_dir(p, d):
        return (p['gdn_q'], p['gdn_k'], p['gdn_v'], p['gdn_g'][:, :, d], p['gdn_beta'][:, :, d])

    def gla_dir(p, d):
        return (p['gla_q'], p['gla_k'], p['gla_v'], p['gla_log_a'][:, :, d])

    o_gdn, oc_gdn = bidirectional(run_gdn, [gdn_dir(pc, d) for d in range(2)], [gdn_dir(pl, d) for d in range(2)], s0_gdn, ctx_out)
    o_gla, oc_gla = bidirectional(run_gla, [gla_dir(pc, d) for d in range(2)], [gla_dir(pl, d) for d in range(2)], s0_gla, ctx_out)

    def merge(og, ol, p, dtype):
        bb, l = og.shape[:2]
        y_gdn = (head_rmsnorm(og, gdn_norm_w) * jax.nn.silu(p['gdn_z'])).reshape(bb, l, GDN_V)
        y_gla = (head_rmsnorm(ol, gla_norm_w) * jax.nn.silu(p['gla_g'])).reshape(bb, l, GLA_V)
        return jnp.concatenate([y_gdn, y_gla], axis=-1).astype(dtype) @ w_out

    m = merge(o_gdn, o_gla, pl, a.dtype)
    mc = merge(oc_gdn, oc_gla, pc, ac.dtype) if ctx_out else None
    return m, mc


def hyena_filters(l, w1, b1, w2, b2, w3, b3, freq, filt_out):
    f32 = jnp.float32
    t = jnp.linspace(0.0, 1.0, l, dtype=f32)[:, None]
    w = (2.0 * math.pi / l) * jnp.arange(l, dtype=f32)[:, None]
    f = jnp.linspace(1e-4, HY_BANDS - 1, HY_BANDS, dtype=f32)[None, :]
    z = jnp.concatenate([t, jnp.cos(f * w), -jnp.sin(f * w)], axis=-1)
    fr = freq.astype(f32)
    hdn = jnp.sin(fr * (z @ w1.astype(f32) + b1.astype(f32)))
    hdn = jnp.sin(fr * (hdn @ w2.astype(f32) + b2.astype(f32)))
    hdn = jnp.sin(fr * (hdn @ w3.astype(f32) + b3.astype(f32)))
    h = (hdn @ filt_out.astype(f32)).reshape(l, 2, HY_ORDER, D_MODEL)
    deltas = jnp.abs(jnp.linspace(HY_MIN_DECAY, HY_MAX_DECAY, D_MODEL, dtype=f32))
    window = jnp.exp(-t.reshape(l, 1, 1, 1) * deltas) + HY_SHIFT
    return h * window


def long_conv(z, h_fwd, h_bwd, skip):
    l = z.shape[1]
    k = jnp.concatenate([h_fwd, jnp.zeros_like(h_fwd[:1]), jnp.flip(h_bwd[1:], axis=0)], axis=0)
    y = jnp.fft.irfft(jnp.fft.rfft(z, n=2 * l, axis=1) * jnp.fft.rfft(k, axis=0)[None], n=2 * l, axis=1)[:, :l]
    return y + z * skip.astype(jnp.float32)


def hyena_mixer(a, w_in, conv_w, w1, b1, w2, b2, w3, b3, freq, filt_out, skip, w_out):
    l = a.shape[1]
    u = short_conv(a @ w_in, conv_w).astype(jnp.float32)
    v, x1, x2 = jnp.split(u, 3, axis=-1)
    filt = hyena_filters(l, w1, b1, w2, b2, w3, b3, freq, filt_out)
    z = v
    for n, gate in enumerate((x1, x2)):
        z = gate * long_conv(z, filt[:, 0, n], filt[:, 1, n], skip[n])
    return z.astype(a.dtype) @ w_out


def setup_inputs(seed: int = 0) -> dict:
    key = jax.random.key(seed)
    ks = iter(jax.random.split(key, 40))
    f32 = jnp.float32
    D = D_MODEL

    def nrm(shape, scale):
        return jax.random.normal(next(ks), shape, f32) * scale

    dt = jnp.exp(jax.random.uniform(next(ks), (N_EVEN, 2, GDN_HEADS), f32, math.log(1e-3), math.log(1e-1)))
    return {
        'x': nrm((BATCH, SEQ, D), 1.0),
        'c': nrm((BATCH, D), 1.0),
        'ctx': nrm((BATCH, CTX_LEN, D), 1.0),
        'c_ctx': nrm((D,), 1.0),
        'mod_w': nrm((DEPTH, D, 6 * D), 0.5 * D ** -0.5),
        'mod_b': nrm((DEPTH, 6 * D), 0.02),
        'norm1_w': 1.0 + nrm((DEPTH, D), 0.02),
        'norm2_w': 1.0 + nrm((DEPTH, D), 0.02),
        'ab_w_in': nrm((N_EVEN, D, AB_WIDTH), D ** -0.5),
        'ab_conv_w': nrm((N_EVEN, SHORT_CONV, 2 * GDN_QK + GDN_V), SHORT_CONV ** -0.5),
        'gdn_a_log': jnp.log(jax.random.uniform(next(ks), (N_EVEN, 2, GDN_HEADS), f32, 1.0, 16.0)),
        'gdn_dt_bias': dt + jnp.log(-jnp.expm1(-dt)),
        'gdn_norm_w': 1.0 + nrm((N_EVEN, GDN_DV), 0.02),
        'gla_gate_w': nrm((N_EVEN, 2, GLA_GATE_RANK, GLA_QK), GLA_GATE_RANK ** -0.5),
        'gla_gate_b': nrm((N_EVEN, 2, GLA_QK), 0.1),
        'gla_norm_w': 1.0 + nrm((N_EVEN, GLA_DV), 0.02),
        'ab_w_out': nrm((N_EVEN, MIX_WIDTH, D), MIX_WIDTH ** -0.5),
        'hy_w_in': nrm((N_ODD, D, 3 * D), D ** -0.5),
        'hy_conv_w': nrm((N_ODD, HY_SHORT, 3 * D), HY_SHORT ** -0.5),
        'hy_pos_w1': nrm((N_ODD, HY_EMB, HY_HIDDEN), HY_EMB ** -0.5),
        'hy_pos_b1': nrm((N_ODD, HY_HIDDEN), 0.1),
        'hy_pos_w2': nrm((N_ODD, HY_HIDDEN, HY_HIDDEN), HY_HIDDEN ** -0.5),
        'hy_pos_b2': nrm((N_ODD, HY_HIDDEN), 0.1),
        'hy_pos_w3': nrm((N_ODD, HY_HIDDEN, HY_HIDDEN), HY_HIDDEN ** -0.5),
        'hy_pos_b3': nrm((N_ODD, HY_HIDDEN), 0.1),
        'hy_freq': 1.0 + nrm((N_ODD, HY_HIDDEN), 0.02),
        'hy_filt_out': nrm((N_ODD, HY_HIDDEN, 2 * HY_ORDER * D), 0.03 * HY_HIDDEN ** -0.5),
        'hy_skip': nrm((N_ODD, HY_ORDER, D), 0.5),
        'hy_w_out': nrm((N_ODD, D, D), D ** -0.5),
        'ffn_w1': nrm((DEPTH, D, D_FF), D ** -0.5),
        'ffn_w3': nrm((DEPTH, D, D_FF), D ** -0.5),
        'ffn_w2': nrm((DEPTH, D_FF, D), D_FF ** -0.5),
        'final_norm_w': 1.0 + nrm((D,), 0.02),
    }


def reference(x, c, ctx, c_ctx, mod_w, mod_b, norm1_w, norm2_w, ab_w_in, ab_conv_w, gdn_a_log, gdn_dt_bias,
              gdn_norm_w, gla_gate_w, gla_gate_b, gla_norm_w, ab_w_out, hy_w_in, hy_conv_w, hy_pos_w1, hy_pos_b1,
              hy_pos_w2, hy_pos_b2, hy_pos_w3, hy_pos_b3, hy_freq, hy_filt_out, hy_skip, hy_w_out,
              ffn_w1, ffn_w3, ffn_w2, final_norm_w):
    silu_c = jax.nn.silu(c)
    silu_cc = jax.nn.silu(c_ctx)
    h, hc = x, ctx
    for i in range(DEPTH):
        even = i % 2 == 0
        ctx_carry = any(j % 2 == 0 for j in range(i + 1, DEPTH))
        ctx_read = even or ctx_carry
        sh1, sc1, g1, sh2, sc2, g2 = jnp.split((silu_c @ mod_w[i] + mod_b[i])[:, None, :], 6, axis=-1)
        a = modulate(h, norm1_w[i], sh1, sc1)
        if ctx_read:
            csh1, csc1, cg1, csh2, csc2, cg2 = jnp.split(silu_cc @ mod_w[i] + mod_b[i], 6, axis=-1)
            ac = modulate(hc, norm1_w[i], csh1, csc1)
        if even:
            e = i // 2
            m, mc = ab_mixer(a, ac, ab_w_in[e], ab_conv_w[e], gdn_a_log[e], gdn_dt_bias[e], gdn_norm_w[e],
                             gla_gate_w[e], gla_gate_b[e], gla_norm_w[e], ab_w_out[e], ctx_carry)
        else:
            o = i // 2
            hy = (hy_w_in[o], hy_conv_w[o], hy_pos_w1[o], hy_pos_b1[o], hy_pos_w2[o], hy_pos_b2[o],
                  hy_pos_w3[o], hy_pos_b3[o], hy_freq[o], hy_filt_out[o], hy_skip[o], hy_w_out[o])
            m = hyena_mixer(a, *hy)
            mc = hyena_mixer(ac, *hy) if ctx_carry else None
        h = h + g1 * m
        h = h + g2 * swiglu(modulate(h, norm2_w[i], sh2, sc2), ffn_w1[i], ffn_w3[i], ffn_w2[i])
        if ctx_carry:
            hc = hc + cg1 * mc
            hc = hc + cg2 * swiglu(modulate(hc, norm2_w[i], csh2, csc2), ffn_w1[i], ffn_w3[i], ffn_w2[i])
    return rmsnorm(h, final_norm_w)
```

```python
import math
import os
import numpy as np
DBG_CUT = 99
from contextlib import ExitStack
import concourse.bass as bass
import concourse.mybir as mybir
from concourse.bass_utils import run_bass_kernel_spmd

F32 = mybir.dt.float32
AF = mybir.ActivationFunctionType
ALU = mybir.AluOpType
AX = mybir.AxisListType

D = 1024
KC = 8
DFF = 2816
FC = 22
EPS = 1e-6
NCORES = 8


class Sched:
    ENG = ['pe', 'act', 'dve', 'pool', 'sp']

    def __init__(self, nc, ctx):
        self.nc = nc
        self.ctx = ctx
        self.sem = {e: ctx.enter_context(nc.semaphore('s_' + e)) for e in self.ENG if e != 'sp'}
        self.cnt = {e: 0 for e in self.ENG}
        self.waited = {e: {} for e in self.ENG}
        self.prog = {e: [] for e in self.ENG}
        self.lastw = {}
        self.readers = {}
        self.dsem = {}

    def semh(self, k):
        return self.sem[k] if isinstance(k, str) else self.dsem[k][0]

    def _deps(self, eng, reads, writes):
        need = {}

        def add(tok):
            if tok is not None and need.get(tok[0], 0) < tok[1]:
                need[tok[0]] = tok[1]
        for r in reads:
            add(self.lastw.get(r))
        for w in writes:
            add(self.lastw.get(w))
            for t in self.readers.get(w, ()):
                add(t)
        out = []
        for k, v in need.items():
            if eng == 'pe' and k == 'pe':
                continue
            if self.waited[eng].get(k, 0) >= v:
                continue
            self.waited[eng][k] = v
            out.append((k, v))
        return out

    def op(self, eng, fn, reads=(), writes=()):
        writes = list(writes) + [r for r in reads if isinstance(r, tuple) and r[0] == 'ps' and r not in writes]
        waits = self._deps(eng, reads, writes)
        self.cnt[eng] += 1
        tok = (eng, self.cnt[eng])
        self.prog[eng].append((waits, fn, self.sem[eng], 1))
        for w in writes:
            self.lastw[w] = tok
            self.readers[w] = []
        for r in reads:
            if r not in writes:
                self.readers.setdefault(r, []).append(tok)
        return tok

    def dma(self, q, out, in_, reads=(), writes=(), semkey=None, in_fn=None):
        if semkey is None:
            semkey = writes[0]
        semkey = ('dma', semkey)
        if semkey not in self.dsem:
            self.dsem[semkey] = [self.ctx.enter_context(self.nc.semaphore('d%d' % len(self.dsem))), 0]
        waits = self._deps(q, reads, writes)
        self.dsem[semkey][1] += 16
        tok = (semkey, self.dsem[semkey][1])
        self.prog[q].append((waits, (lambda e, out=out, in_=in_, in_fn=in_fn: e.dma_start(out=out, in_=(in_fn(e) if in_fn is not None else in_))), self.dsem[semkey][0], 16))
        for w in writes:
            self.lastw[w] = tok
            self.readers[w] = []
        for r in reads:
            self.readers.setdefault(r, []).append(tok)
        return tok

    def barrier(self):
        toks = [(e, self.cnt[e]) for e in self.sem if self.cnt[e] > 0] + [(k, v[1]) for k, v in self.dsem.items() if v[1] > 0]
        for eng in self.ENG:
            waits = []
            for k, v in toks:
                if self.waited[eng].get(k, 0) < v:
                    self.waited[eng][k] = v
                    waits.append((k, v))
            self.prog[eng].append((waits, None, None, 0))

    def wait_keys(self, eng, keys):
        waits = self._deps(eng, list(keys), ())
        self.prog[eng].append((waits, None, None, 0))

    def emit(self):
        engs = {'pe': 'tensor', 'act': 'scalar', 'dve': 'vector', 'pool': 'gpsimd', 'sp': 'sync'}
        with self.nc.allow_non_contiguous_dma(reason="small strided pads / per-chunk layouts"), self.nc.Block() as block:
            for e, name in engs.items():
                prog = self.prog[e]
                if not prog:
                    continue

                def body(eng, prog=prog):
                    for waits, fn, sem, inc in prog:
                        for k, v in waits:
                            eng.wait_ge(self.semh(k), v)
                        if fn is not None:
                            fn(eng).then_inc(sem, inc)
                getattr(block, name)(body)


class KB:
    def __init__(self, fused=False, arena_floats=0):
        self.nc = bass.Bass("TRN2", target_bir_lowering=False)
        self.ctx = ExitStack()
        self.S = Sched(self.nc, self.ctx)
        self.ps = []
        self.psi = 0
        self.wbufs = []
        self.wi = 0
        self.outkeys = []
        self.fused = fused
        self.prefix = ''
        self.bind = {}
        self.yoffs = (0, 256)
        self.dyn_tok = False
        self.arena = None
        self.aoff = 0
        self.amax = 0
        if arena_floats:
            self.arena = self.ctx.enter_context(self.nc.sbuf_tensor('arena_all', [128, arena_floats], F32))
            self.asize = arena_floats

    def next_stage(self, prefix, bind=None):
        self.S.barrier()
        self.aoff = 0
        self.wbufs = []
        self.wi = 0
        self.prefix = prefix
        self.bind = dict(bind or {})

    def inp(self, name, shape):
        if name in self.bind:
            return self.bind[name]
        return self.nc.dram_tensor(self.prefix + name, list(shape), F32, kind="ExternalInput").ap()

    def outp(self, name, shape):
        if name in self.bind:
            return self.bind[name]
        return self.nc.dram_tensor(self.prefix + name, list(shape), F32, kind="ExternalOutput").ap()

    def scratch(self, name, shape):
        if name in self.bind:
            return self.bind[name]
        return self.nc.dram_tensor(self.prefix + name, list(shape), F32, kind="Internal").ap()

    def sb(self, name, shape):
        if self.arena is None:
            return self.ctx.enter_context(self.nc.sbuf_tensor(name, list(shape), F32))
        n = 1
        for d in shape[1:]:
            n *= d
        assert self.aoff + n <= self.asize, ('SBUF arena overflow', name, self.aoff, n)
        ap = self.arena[0:shape[0], self.aoff:self.aoff + n]
        self.aoff += n
        self.amax = max(self.amax, self.aoff)
        if len(shape) > 2:
            names = ['d%d' % i for i in range(len(shape) - 1)]
            ap = ap.rearrange("p (%s) -> p %s" % (' '.join(names), ' '.join(names)), **{nm: shape[i + 1] for i, nm in enumerate(names[:-1])})
        return ap

    def init_psum(self, n=8):
        if self.ps:
            return
        for i in range(n):
            self.ps.append(self.ctx.enter_context(self.nc.psum_tensor('ps%d' % i, [128, 512], F32)))

    def psum(self):
        i = self.psi
        self.psi = (self.psi + 1) % len(self.ps)
        return self.ps[i], ('ps', i)

    def init_wpool(self, n, width):
        for i in range(n):
            self.wbufs.append(self.sb('wbuf%d' % i, [128, width]))

    def wload(self, src, width, q='sp'):
        i = self.wi
        self.wi = (self.wi + 1) % len(self.wbufs)
        buf = self.wbufs[i]
        self.S.dma(q, buf[:, 0:width], src, writes=[('w', i)])
        return buf, ('w', i)

    def mm(self, out, lhsT, rhs, start, stop, reads, writes):
        return self.S.op('pe', lambda e: e.matmul(out, lhsT=lhsT, rhs=rhs, start=start, stop=stop), reads, writes)

    def act(self, out, in_, func, reads, writes, bias=None, scale=None, eng='act'):
        kw = {}
        if bias is not None:
            kw['bias'] = bias
        if scale is not None:
            kw['scale'] = scale
        return self.S.op(eng, lambda e: e.activation(out=out, in_=in_, func=func, **kw), reads, writes)

    def tt(self, out, in0, in1, op, reads, writes, eng='dve'):
        return self.S.op(eng, lambda e: e.tensor_tensor(out=out, in0=in0, in1=in1, op=op), reads, writes)

    def ts(self, out, in0, s1, s2, op0, op1, reads, writes, eng='dve'):
        if op1 is None:
            return self.S.op(eng, lambda e: e.tensor_scalar(out=out, in0=in0, scalar1=s1, scalar2=None, op0=op0), reads, writes)
        return self.S.op(eng, lambda e: e.tensor_scalar(out=out, in0=in0, scalar1=s1, scalar2=s2, op0=op0, op1=op1), reads, writes)

    def stt(self, out, in0, scalar, in1, op0, op1, reads, writes, eng='dve'):
        return self.S.op(eng, lambda e: e.scalar_tensor_tensor(out=out, in0=in0, scalar=scalar, in1=in1, op0=op0, op1=op1), reads, writes)

    def cp(self, out, in_, reads, writes, eng='dve'):
        if eng == 'act':
            return self.S.op('act', lambda e: e.copy(out=out, in_=in_), reads, writes)
        return self.S.op(eng, lambda e: e.tensor_copy(out=out, in_=in_), reads, writes)

    def memset(self, ap, val, writes, eng='pool'):
        return self.S.op(eng, lambda e: e.memset(ap, val), (), writes)

    def recip(self, out, in_, reads, writes):
        return self.S.op('dve', lambda e: e.reciprocal(out=out, in_=in_), reads, writes)

    def load(self, dst, src, key, q='sp'):
        return self.S.dma(q, dst, src, writes=[key])

    def load_tok(self, dst, view, t0, T, key):
        if not self.dyn_tok:
            return self.load(dst, view[:, :, t0:t0 + T], key)
        half = self.dyn_half

        def in_fn(e):
            if getattr(self, '_rbase', None) is None:
                self._rbase = e.snap((e.partition_id() % 2) * half, min_val=0, max_val=half)
            return view[:, :, bass.ds(self._rbase + t0, T)]
        return self.S.dma('sp', dst, None, writes=[key], in_fn=in_fn)

    def store(self, dst, src, srckey, dstkey, q='pool', final=True):
        self.S.dma(q, dst, src, reads=[srckey], writes=[dstkey], semkey=srckey)
        if final:
            self.outkeys.append(dstkey)

    def finish(self, force=False):
        if self.fused and not force:
            return None
        self.S.wait_keys('pool', self.outkeys)
        self.S.emit()
        self.ctx.close()
        return self.nc


def emit_modvec(kb, cT, ckey, nv, wl, bl, nblk, out, okey):
    for b in range(nblk):
        wb, wk = kb.wload(wl[b], KC * 128)
        ps, pk = kb.psum()
        for k in range(KC):
            kb.mm(ps[:, 0:nv], wb[:, k * 128:(k + 1) * 128], cT[:, k, :], k == 0, k == KC - 1, [wk, ckey], [pk])
        kb.ts(out[:, b, :], ps[:, 0:nv], bl[:, b:b + 1], None, ALU.add, None, [pk, 'modb'], [okey])


def emit_norm(kb, X, xkey, A, akey, T, ones, gain, shift, gkey, sq, rstd):
    for k in range(KC):
        kb.act(sq[:, k, :T], X[:, k, :T], AF.Square, [xkey], [('sq', k)])
    ps, pk = kb.psum()
    for k in range(KC):
        kb.mm(ps[:, :T], ones[:, :], sq[:, k, :T], k == 0, k == KC - 1, ['ones', ('sq', k)], [pk])
    kb.act(rstd[:, :T], ps[:, :T], AF.Sqrt, [pk, 'eps'], ['rstd'], bias=kb.eps_ap, scale=1.0 / D)
    kb.recip(rstd[:, :T], rstd[:, :T], ['rstd'], ['rstd'])
    for k in range(KC):
        kb.tt(sq[:, k, :T], X[:, k, :T], rstd[:, :T], ALU.mult, [xkey, 'rstd', ('sq', k)], [('sq', k)], eng='dve' if k % 2 == 0 else 'pool')
        kb.act(A[:, k, :T], sq[:, k, :T], AF.Identity, [('sq', k), gkey], [akey], bias=shift[:, k, 0:1], scale=gain[:, k, 0:1])


def emit_linear(kb, wl, nblk, kc, rhs_fn, rkeys, T, evac):
    for b in range(nblk):
        wb, wk = kb.wload(wl[b], kc * 128)
        ps, pk = kb.psum()
        for k in range(kc):
            kb.mm(ps[:, :T], wb[:, k * 128:(k + 1) * 128], rhs_fn(k), k == 0, k == kc - 1, [wk] + rkeys, [pk])
        evac(b, ps, pk)


def emit_ffn(kb, A, akey, H, hkey, G, T, w1l, w3l, w2l, g2, gkey, tmp):
    for j in range(FC):
        w1, k1 = kb.wload(w1l[j], KC * 128)
        w3, k3 = kb.wload(w3l[j], KC * 128)
        p1, pk1 = kb.psum()
        p3, pk3 = kb.psum()
        for k in range(KC):
            kb.mm(p1[:, :T], w1[:, k * 128:(k + 1) * 128], A[:, k, :T], k == 0, k == KC - 1, [k1, akey], [pk1])
        for k in range(KC):
            kb.mm(p3[:, :T], w3[:, k * 128:(k + 1) * 128], A[:, k, :T], k == 0, k == KC - 1, [k3, akey], [pk3])
        tk = ('tmp', j % 2)
        kb.act(tmp[:, j % 2, :T], p1[:, :T], AF.Silu, [pk1], [tk])
        kb.tt(G[:, j, :T], tmp[:, j % 2, :T], p3[:, :T], ALU.mult, [tk, pk3], [('G', j)])
    for i in range(KC):
        w2, k2 = kb.wload(w2l[i], FC * 128)
        ps, pk = kb.psum()
        for j in range(FC):
            kb.mm(ps[:, :T], w2[:, j * 128:(j + 1) * 128], G[:, j, :T], j == 0, j == FC - 1, [k2, ('G', j)], [pk])
        kb.stt(H[:, i, :T], ps[:, :T], g2[:, i, 0:1], H[:, i, :T], ALU.mult, ALU.add, [pk, gkey, hkey], [hkey])


TT = 512


def build_post(ntok, last, kb=None):
    kb = kb or KB()
    nmb = 32 if last else 48
    HT = kb.inp('HT', [D, ntok]); YT = kb.inp('YT', [D, ntok])
    wo = kb.inp('wo', [KC, 128, D]); cTd = kb.inp('cT', [128, KC])
    modw = kb.inp('modw', [nmb, 128, D]); modb = kb.inp('modb', [128, nmb]); nw = kb.inp('nw', [128, KC, 2])
    w1l = kb.inp('w1', [FC, 128, D]); w3l = kb.inp('w3', [FC, 128, D]); w2l = kb.inp('w2', [KC, 128, DFF])
    HO = kb.outp('HO', [D, ntok])
    if not last:
        wn = kb.inp('wn', [24, 128, D]); UT = kb.outp('UT', [3 * D, ntok])
    kb.init_psum(8)
    kb.init_wpool(4, DFF)
    ones = kb.sb('ones', [128, 128]); epst = kb.sb('epst', [128, 1])
    kb.memset(ones[:], 1.0, ['ones']); kb.memset(epst[:], EPS, ['eps'])
    kb.eps_ap = epst[:, 0:1]
    cT = kb.sb('cTs', [128, KC, 1]); modbs = kb.sb('modbs', [128, nmb]); nws = kb.sb('nws', [128, KC, 2])
    mod = kb.sb('mod', [128, nmb, 1])
    kb.load(cT[:, :, 0], cTd[:, :], 'cT'); kb.load(modbs[:], modb[:, :], 'modb'); kb.load(nws[:], nw[:, :, :], 'nw')
    kb.act(cT[:, :, 0], cT[:, :, 0], AF.Silu, ['cT'], ['cT'])
    emit_modvec(kb, cT, 'cT', 1, modw, modbs, nmb, mod, 'mod')
    gains = kb.sb('gains', [128, KC, 2])
    kb.stt(gains[:, :, 0:1], mod[:, 16:24, :], 1.0, nws[:, :, 0:1], ALU.add, ALU.mult, ['mod', 'nw'], ['gains'])
    if not last:
        kb.stt(gains[:, :, 1:2], mod[:, 40:48, :], 1.0, nws[:, :, 1:2], ALU.add, ALU.mult, ['mod', 'nw', 'gains'], ['gains'])
    else:
        kb.cp(gains[:, :, 1:2], nws[:, :, 1:2], ['nw', 'gains'], ['gains'])
    zshift = kb.sb('zshift', [128, KC, 1])
    kb.memset(zshift[:], 0.0, ['zshift'])
    H = kb.sb('H', [128, KC, TT]); Y = kb.sb('Y', [128, KC, TT]); A = kb.sb('A', [128, KC, TT]); sq = kb.sb('sq', [128, KC, TT])
    rstd = kb.sb('rstd', [128, TT]); G = kb.sb('G', [128, FC, TT]); tmp = kb.sb('tmp', [128, 2, TT])
    HTv = HT.rearrange("(k p) t -> p k t", p=128); YTv = YT.rearrange("(k p) t -> p k t", p=128)
    HOv = HO.rearrange("(k p) t -> p k t", p=128)
    if not last:
        UTv = UT.rearrange("(k p) t -> p k t", p=128)
    for ti in range(ntok // TT):
        tsl = slice(ti * TT, (ti + 1) * TT)
        kb.load_tok(H[:], HTv, ti * TT, TT, 'H'); kb.load_tok(Y[:], YTv, ti * TT, TT, 'Y')

        def ev_o(b, ps, pk):
            kb.stt(H[:, b, :], ps[:, :TT], mod[:, b, 0:1], H[:, b, :], ALU.mult, ALU.add, [pk, 'mod', 'H'], ['H'])
        emit_linear(kb, wo, KC, KC, lambda k: Y[:, k, :], ['Y'], TT, ev_o)
        emit_norm(kb, H, 'H', A, 'A', TT, ones, gains[:, :, 0:1], mod[:, 8:16, :], 'gains', sq, rstd)
        emit_ffn(kb, A, 'A', H, 'H', G, TT, w1l, w3l, w2l, mod[:, 24:32, :], 'mod', tmp)
        if not last:
            kb.store(HOv[:, :, tsl], H[:], 'H', ('HO', ti))
            emit_norm(kb, H, 'H', A, 'A', TT, ones, gains[:, :, 1:2], mod[:, 32:40, :], 'gains', sq, rstd)

            def ev_u(b, ps, pk):
                kb.cp(tmp[:, b % 2, :], ps[:, :TT], [pk], [('tmp', b % 2)], eng='act' if b % 2 else 'dve')
                kb.store(UTv[:, b, tsl], tmp[:, b % 2, :], ('tmp', b % 2), ('UT', ti, b))
            emit_linear(kb, wn, 24, KC, lambda k: A[:, k, :], ['A'], TT, ev_u)
        else:
            emit_norm(kb, H, 'H', A, 'A', TT, ones, gains[:, :, 1:2], zshift, 'gains', sq, rstd)
            kb.store(HOv[:, :, tsl], A[:], 'A', ('HO', ti))
    return kb.finish()


def blk_w(W, kc):
    K, N = W.shape
    nb = N // 128
    return np.ascontiguousarray(W.reshape(kc, 128, nb, 128).transpose(2, 1, 0, 3).reshape(nb, 128, kc * 128))


def fm_vec(v):
    return np.ascontiguousarray(v.reshape(-1, 128).T)


CH = 64
NCTX = 4
NLAT = 128
NG = NCTX + NLAT
NEG = -30000.0
AB_SIZES = (512, 512, 512, 512, 8, 8, 256, 256, 512, 512, 32)


def mixa_consts():
    i = np.arange(64)
    P, Fq = i[:, None], i[None, :]
    tri = [(P <= Fq).astype(np.float32), (P >= Fq).astype(np.float32)]
    ident = np.eye(64, dtype=np.float32)
    v1 = [(P >= Fq), (P <= Fq)]
    v2 = [(Fq >= P), (Fq <= P)]
    s1 = [(P > Fq), (P < Fq)]
    s2 = [(Fq > P), (Fq < P)]
    c = {}
    dd = [0, 0, 1, 1]
    c['TRIS'] = np.stack([tri[d] for d in dd], 1)
    c['ID4'] = np.stack([ident for d in dd], 1)
    c['NEG1'] = np.stack([np.where(v1[d], 0.0, NEG) for d in dd], 1).astype(np.float32)
    c['NEG2'] = np.stack([np.where(v2[d], 0.0, NEG) for d in dd], 1).astype(np.float32)
    c['ST1'] = np.stack([s1[d].astype(np.float32) for d in dd], 1)
    c['ST2'] = np.stack([s2[d].astype(np.float32) for d in dd], 1)
    c['INC2'] = np.stack([v2[d].astype(np.float32) for d in dd], 1)
    names = ['TRIS', 'ID4', 'NEG1', 'NEG2', 'ST1', 'ST2', 'INC2']
    return np.ascontiguousarray(np.concatenate([c[n] for n in names], 1)), names


def build_mixa(stage=99, ntiles=99, kb=None):
    kb = kb or KB()
    S = kb.S
    L = 8192
    LC = 256
    XT = kb.inp('XT', [D, L]); CXT = kb.inp('CXT', [D, LC])
    cTd = kb.inp('cT', [128, KC, 2]); modw = kb.inp('modw', [16, 128, D]); modb = kb.inp('modb', [128, 16]); nw = kb.inp('nw', [128, KC])
    wl = kb.inp('wl', [18, 128, D]); wg = kb.inp('wg', [128, KC, 8]); taps = kb.inp('taps', [128, 6, 3])
    gconst = kb.inp('gconst', [64, 2, 32])
    gwb = kb.inp('gwb', [17, 2, 128]); hnw = kb.inp('hnw', [128, 2]); cm = kb.inp('cm', [64, 28, 64]); identd = kb.inp('ident', [128, 128])
    YT = kb.outp('YT', [512, L])
    ATl = kb.scratch('ATl', [D, L + 2]); ATc = kb.scratch('ATc', [D, LC + 2])
    G_U0 = kb.scratch('G_U0', [NG, 64, 4, 128]); G_KC = kb.scratch('G_KC', [NG, 128, 4, 64]); G_KD = kb.scratch('G_KD', [NG, 64, 4, 128])
    G_AT = kb.scratch('G_AT', [NG, 64, 4, 64]); G_QT = kb.scratch('G_QT', [NG, 128, 2, 64])
    L_KD = kb.scratch('L_KD', [NG, 64, 4, 64]); L_V = kb.scratch('L_V', [NG, 64, 2, 128]); L_AT = kb.scratch('L_AT', [NG, 64, 4, 64]); L_QD = kb.scratch('L_QD', [NG, 64, 4, 64])
    OG = kb.scratch('OG', [NLAT, 64, 4, 128]); OL = kb.scratch('OL', [NLAT, 64, 4, 128])
    ZT = kb.scratch('ZT', [4, 128, L])
    kb.init_psum(8)
    kb.init_wpool(3, D)
    ones = kb.sb('ones', [128, 128]); epst = kb.sb('epst', [128, 1]); ident = kb.sb('ident_s', [128, 128])
    kb.memset(ones[:], 1.0, ['ones']); kb.memset(epst[:], EPS, ['eps']); kb.eps_ap = epst[:, 0:1]
    kb.load(ident[:], identd[:, :], 'ident')
    cms = kb.sb('cms', [64, 28, 64]); kb.load(cms[:], cm[:, :, :], 'cm')
    TRIS, ID4, NEG1, NEG2, ST1, ST2, INC2 = [cms[:, 4 * i:4 * i + 4, :] for i in range(7)]
    cT = kb.sb('cTs', [128, KC, 2]); modbs = kb.sb('modbs', [128, 16]); nws = kb.sb('nws', [128, KC, 1]); mod = kb.sb('mod', [128, 16, 2])
    wgs = kb.sb('wgs', [128, KC, 8]); tapss = kb.sb('tapss', [128, 6, 3]); gcs = kb.sb('gcs', [64, 2, 32]); gwbs = kb.sb('gwbs', [17, 2, 128]); hnws = kb.sb('hnws', [128, 2])
    kb.load(cT[:], cTd[:, :, :], 'cT'); kb.load(modbs[:], modb[:, :], 'modb'); kb.load(nws[:, :, 0], nw[:, :], 'nw')
    kb.load(wgs[:], wg[:, :, :], 'wg'); kb.load(tapss[:], taps[:, :, :], 'taps'); kb.load(gcs[:], gconst[:, :, :], 'gcs'); kb.load(gwbs[:], gwb[:, :, :], 'gwb'); kb.load(hnws[:], hnw[:, :], 'hnw')
    kb.act(cT[:], cT[:], AF.Silu, ['cT'], ['cT'])
    emit_modvec(kb, cT, 'cT', 2, modw, modbs, 16, mod, 'mod')
    gains = kb.sb('gains', [128, KC, 2])
    for v in range(2):
        kb.stt(gains[:, :, v:v + 1], mod[:, 8:16, v:v + 1], 1.0, nws[:, :, 0:1], ALU.add, ALU.mult, ['mod', 'nw', 'gains'], ['gains'])
    negA = kb.sb('negA', [64, 32])
    kb.act(negA[:], gcs[:, 0, :], AF.Exp, ['gcs'], ['negA'])
    kb.ts(negA[:], negA[:], -1.0, None, ALU.mult, None, ['negA'], ['negA'])
    arena = kb.sb('arena', [128, 12800])
    X = arena[:, 0:4096].rearrange("p (k t) -> p k t", k=KC); A = arena[:, 4096:8192].rearrange("p (k t) -> p k t", k=KC)
    sq = arena[:, 8192:12288].rearrange("p (k t) -> p k t", k=KC); rstd = arena[:, 12288:12800]
    zt = kb.sb('zt', [128, KC, 1]); kb.memset(zt[:], 0.0, ['zt'])
    ATlv = ATl.rearrange("(k p) t -> p k t", p=128); ATcv = ATc.rearrange("(k p) t -> p k t", p=128)
    XTv = XT.rearrange("(k p) t -> p k t", p=128); CXTv = CXT.rearrange("(k p) t -> p k t", p=128)
    for (dst, n) in ((ATlv, L), (ATcv, LC)):
        kb.store(dst[:, :, 0:1], zt[:], 'zt', ('ATpad', n, 0), final=False)
        kb.store(dst[:, :, n + 1:n + 2], zt[:], 'zt', ('ATpad', n, 1), final=False)
    tiles = [('c', 0, LC, 0)] + [('l', i * TT, TT, NCTX + i * 8) for i in range(L // TT)]
    for (sq_, t0, T, g0) in tiles:
        src = CXTv if sq_ == 'c' else XTv
        dst = ATcv if sq_ == 'c' else ATlv
        v = 1 if sq_ == 'c' else 0
        kb.load(X[:, :, :T], src[:, :, t0:t0 + T], 'X')
        emit_norm(kb, X, 'X', A, 'A', T, ones, gains[:, :, v:v + 1], mod[:, 0:8, v:v + 1], 'gains', sq, rstd)
        kb.store(dst[:, :, 1 + t0:1 + t0 + T], A[:, :, :T], 'A', ('AT', sq_, t0), final=False)
    atkeys = [('AT', s_, t0) for (s_, t0, T, g0) in tiles] + [('ATpad', n, i) for n in (L, LC) for i in range(2)]
    if stage < 2:
        return kb.finish()
    tiles = tiles[:ntiles]
    S.barrier()
    AH = arena[:, 0:KC * (TT + 2)].rearrange("p (k t) -> p k t", k=KC)
    Ue = kb.sb('Ue', [128, TT + 2]); cv = kb.sb('cv', [128, TT]); sqb = kb.sb('sqb', [128, TT]); rs = kb.sb('rs', [128, TT])
    qT = kb.sb('qT', [128, 2, TT]); kT = kb.sb('kT', [128, 2, TT]); vT = kb.sb('vT', [128, TT])
    k_tm = arena[0:64, 4112:6160].rearrange("p (c h f) -> p c h f", c=8, h=2); v_tm = arena[0:64, 6160:8208].rearrange("p (c h f) -> p c h f", c=8, h=2)
    qlT = kb.sb('qlT', [64, 2, TT]); klT = kb.sb('klT', [64, 2, TT]); kl_tm = kb.sb('kl_tm', [64, 8, 2, 64]); vl_tm = arena[0:64, 8208:10256].rearrange("p (c h f) -> p c h f", c=8, h=2)
    llrT = kb.sb('llrT', [17, 2, TT]); kb.memset(llrT[:], 1.0, ['llrT'])
    zs = kb.sb('zs', [128, 2, TT])
    g_tm = kb.sb('g_tm', [64, 8, 4]); b_tm = kb.sb('b_tm', [64, 8, 4]); gx = kb.sb('gx', [64, 8, 4])
    la_tm = arena[0:64, 10256:12304].rearrange("p (c h f) -> p c h f", c=8, h=4)
    RGB = kb.sb('RGB', [64, 8, 64]); gcc = kb.sb('gcc', [64, 4]); t1 = kb.sb('t1', [64, 4, 64]); t2 = kb.sb('t2', [64, 4, 64])
    E1 = kb.sb('E1', [64, 4, 64]); E2 = kb.sb('E2', [64, 4, 64]); Wm = kb.sb('Wm', [64, 4, 64]); WmT = kb.sb('WmT', [64, 4, 64])
    Pb = [kb.sb('Pb%d' % i, [64, 4, 64]) for i in range(2)]; PTb = [kb.sb('PTb%d' % i, [64, 4, 64]) for i in range(2)]; XTb = [kb.sb('XTb%d' % i, [64, 4, 64]) for i in range(2)]
    attT = kb.sb('attT', [64, 4, 64]); egc = kb.sb('egc', [64, 4]); bege = kb.sb('bege', [64, 4]); ekd = kb.sb('ekd', [64, 4])
    VB = kb.sb('VB', [64, 4, 128]); KBG = kb.sb('KBG', [64, 4, 128]); KDg = kb.sb('KDg', [64, 4, 128]); U0s = kb.sb('U0s', [64, 4, 128]); KCs = kb.sb('KCs', [128, 4, 64])
    GGL = kb.sb('GGL', [128, NG, 4]); GEGC = kb.sb('GEGC', [64, NG, 4]); LGL = kb.sb('LGL', [64, NG, 4])
    refc = kb.sb('refc', [64, 4]); dlt = kb.sb('dlt', [64, 4, 64]); Ea = kb.sb('Ea', [64, 4, 64]); Eb = kb.sb('Eb', [64, 4, 64]); Ec = kb.sb('Ec', [64, 4, 64])
    QQ = kb.sb('QQ', [64, 4, 64]); KKl = kb.sb('KKl', [64, 4, 64]); QDl = kb.sb('QDl', [64, 4, 64]); ATl_ = kb.sb('ATTl', [64, 4, 64]); bct = kb.sb('bct', [64, 4, 64]); KDl = kb.sb('KDl', [64, 4, 64])

    def bc_last(ap, n):
        return ap.unsqueeze(2).to_broadcast([ap.shape[0], ap.shape[1], n])

    def l2norm(dst, T, qscale):
        kb.act(sqb[:, :T], dst, AF.Square, ['qkv'], ['sqb'])
        ps, pk = kb.psum()
        kb.mm(ps[:, :T], ones[:, :], sqb[:, :T], True, True, ['ones', 'sqb'], [pk])
        kb.act(rs[:, :T], ps[:, :T], AF.Sqrt, [pk, 'eps'], ['rs'], bias=kb.eps_ap, scale=1.0)
        kb.recip(rs[:, :T], rs[:, :T], ['rs'], ['rs'])
        kb.stt(dst, dst, qscale, rs[:, :T], ALU.mult, ALU.mult, ['qkv', 'rs'], ['qkv'])

    def transp(src_fn, nch, K_, M_, dst_fn, keys_r, key_w):
        per = 512 // M_
        for c0 in range(0, nch, per):
            ps, pk = kb.psum()
            n = min(per, nch - c0)
            for c in range(c0, c0 + n):
                kb.mm(ps[0:64, (c - c0) * M_:(c - c0 + 1) * M_], src_fn(c), ident[0:K_, 0:M_], True, True, keys_r + ['ident'], [pk])
            for c in range(c0, c0 + n):
                kb.cp(dst_fn(c), ps[0:64, (c - c0) * M_:(c - c0 + 1) * M_], [pk], [key_w], eng='act' if c % 2 else 'dve')

    for (sq_, t0, T, g0) in tiles:
        nch = T // CH
        src = ATcv if sq_ == 'c' else ATlv
        S.dma('sp', AH[:, :, 0:T + 2], src[:, :, t0:t0 + T + 2], reads=atkeys, writes=['AH'])

        def proj(b, M_, T=T):
            wb, wk = kb.wload(wl[b], D)
            ps, pk = kb.psum()
            for k in range(KC):
                kb.mm(ps[0:M_, :T], wb[:, k * 128:k * 128 + M_], AH[:, k, 1:T + 1], k == 0, k == KC - 1, [wk, 'AH'], [pk])
            return wb, wk, ps, pk
        for b in range(6):
            wb, wk, ps, pk = proj(b, 128)
            ph, phk = kb.psum()
            for k in range(KC):
                kb.mm(ph[:, 0:2], wb[:, k * 128:(k + 1) * 128], AH[:, k, 0:T + 2:T + 1], k == 0, k == KC - 1, [wk, 'AH'], [phk])
            kb.cp(Ue[:, 1:T + 1], ps[:, :T], [pk], ['Ue'], eng='act')
            kb.cp(Ue[:, 0:T + 2:T + 1], ph[:, 0:2], [phk, 'Ue'], ['Ue'])
            kb.ts(cv[:, :T], Ue[:, 1:T + 1], tapss[:, b, 1:2], None, ALU.mult, None, ['Ue', 'taps'], ['cv'])
            kb.stt(cv[:, :T], Ue[:, 0:T], tapss[:, b, 0:1], cv[:, :T], ALU.mult, ALU.add, ['Ue', 'taps', 'cv'], ['cv'])
            kb.stt(cv[:, :T], Ue[:, 2:T + 2], tapss[:, b, 2:3], cv[:, :T], ALU.mult, ALU.add, ['Ue', 'taps', 'cv'], ['cv'])
            seg, hl = b // 2, b % 2
            dst = (qT[:, hl, :T], kT[:, hl, :T], vT[:, :T])[seg]
            kb.act(dst, cv[:, :T], AF.Silu, ['cv'], ['qkv'])
            if seg < 2:
                l2norm(dst, T, 128 ** -0.5 if seg == 0 else 1.0)
            if seg == 0:
                kb.store(G_QT[g0:g0 + nch, :, hl, :].rearrange("g d i -> d g i"), qT[:, hl, :T].rearrange("d (g i) -> d g i", i=CH), 'qkv', ('G_QT', g0, hl), final=False)
            if seg == 1:
                transp(lambda c, hl=hl: kT[:, hl, c * CH:(c + 1) * CH], nch, 128, 128, lambda c, hl=hl: k_tm[:, c, hl, :], ['qkv'], 'k_tm')
            if seg == 2:
                transp(lambda c: vT[:, c * CH:(c + 1) * CH], nch, 128, 128, lambda c, hl=hl: v_tm[:, c, hl, :], ['qkv'], 'v_tm')
        for bi, b in enumerate((6, 7, 14, 15)):
            wb, wk, ps, pk = proj(b, 128)
            kb.act(zs[:, bi % 2, :T], ps[:, :T], AF.Silu, [pk], [('zs', bi % 2)])
            if sq_ == 'l':
                kb.store(ZT[bi, :, t0:t0 + T], zs[:, bi % 2, :T], ('zs', bi % 2), ('ZT', bi, t0), final=False)
        for b in (8, 9, 10, 11):
            wb, wk, ps, pk = proj(b, 64)
            hl = b % 2
            if b < 10:
                kb.ts(qlT[:, hl, :T], ps[0:64, :T], 64 ** -0.5, None, ALU.mult, None, [pk], ['qlT'])
            else:
                kb.cp(klT[:, hl, :T], ps[0:64, :T], [pk], ['klT'], eng='act')
                transp(lambda c, hl=hl: klT[:, hl, c * CH:(c + 1) * CH], nch, 64, 64, lambda c, hl=hl: kl_tm[:, c, hl, :], ['klT'], 'kl_tm')
        for b in (12, 13):
            wb, wk, ps, pk = proj(b, 128)
            hl = b % 2
            kb.cp(vT[:, :T], ps[:, :T], [pk, 'qkv'], ['qkv'], eng='act')
            transp(lambda c: vT[:, c * CH:(c + 1) * CH], nch, 128, 128, lambda c, hl=hl: vl_tm[:, c, hl, :], ['qkv'], 'vl_tm')
        for d in range(2):
            wb, wk, ps, pk = proj(16 + d, 16)
            kb.cp(llrT[0:16, d, :T], ps[0:16, :T], [pk, 'llrT'], ['llrT'])
        psg, pgk = kb.psum()
        for c in range(nch):
            for k in range(KC):
                kb.mm(psg[0:64, c * 8:(c + 1) * 8], AH[:, k, 1 + c * CH:1 + (c + 1) * CH], wgs[:, k, :], k == 0, k == KC - 1, ['AH', 'wg'], [pgk])
        psgv = psg[0:64, 0:nch * 8].rearrange("p (c e) -> p c e", e=8)
        gcv = gcs[:, 1, :].rearrange("p (c e) -> p c e", e=4)
        kb.tt(gx[:, :nch, :], psgv[:, :, 0:4], gcv[:, :nch, :], ALU.add, [pgk, 'gcs'], ['gx'])
        kb.act(gx[:, :nch, :], gx[:, :nch, :], AF.Exp, ['gx'], ['gx'])
        kb.act(gx[:, :nch, :], gx[:, :nch, :], AF.Ln, ['gx'], ['gx'], bias=1.0)
        kb.tt(g_tm[:, :nch, :], gx[:, :nch, :], negA[:].rearrange("p (c e) -> p c e", e=4)[:, :nch, :], ALU.mult, ['gx', 'negA'], ['g_tm'])
        kb.act(b_tm[:, :nch, :], psgv[:, :, 4:8], AF.Sigmoid, [pgk], ['b_tm'])
        for d in range(2):
            for c0 in range(0, nch, 4):
                ps, pk = kb.psum()
                n = min(4, nch - c0)
                for c in range(c0, c0 + n):
                    kb.mm(ps[0:64, (c - c0) * 128:(c - c0 + 1) * 128], llrT[0:17, d, c * CH:(c + 1) * CH], gwbs[0:17, d, :], True, True, ['llrT', 'gwb'], [pk])
                dstv = la_tm[:, c0:c0 + n, 2 * d:2 * d + 2, :]
                kb.act(dstv, ps[0:64, 0:n * 128].rearrange("p (c h f) -> p c h f", h=2, f=64), AF.Exp, [pk, 'la_tm'], ['la_tm'], scale=-1.0)
                kb.act(dstv, dstv, AF.Ln, ['la_tm'], ['la_tm'], bias=1.0)
                kb.ts(dstv, dstv, -1.0 / 16.0, None, ALU.mult, None, ['la_tm'], ['la_tm'])
        for c in range(nch if stage >= 3 else 0):
            g = g0 + c
            csl = slice(c * CH, (c + 1) * CH)
            kb.tt(RGB[:, 0:4, :], TRIS, bc_last(g_tm[:, c, :], 64), ALU.mult, ['cm', 'g_tm'], ['RGB'])
            kb.tt(RGB[:, 4:8, :], ID4, bc_last(b_tm[:, c, :], 64), ALU.mult, ['cm', 'b_tm', 'RGB'], ['RGB'], eng='pool')
            psR, pRk = kb.psum()
            kb.mm(psR[0:64, :], ones[0:64, 0:64], RGB[:].rearrange("p a b -> p (a b)"), True, True, ['ones', 'RGB'], [pRk])
            R = psR[0:64, 0:256].rearrange("p (a b) -> p a b", b=64); Bb = psR[0:64, 256:512].rearrange("p (a b) -> p a b", b=64)
            psC, pCk = kb.psum()
            for d in range(2):
                kb.mm(psC[0:64, 2 * d:2 * d + 2], cms[:, 2 * d, :], g_tm[:, c, 2 * d:2 * d + 2], True, True, ['cm', 'g_tm'], [pCk])
            kb.mm(psC[0:128, 4:8], ones[0:64, 0:128], g_tm[:, c, :], True, True, ['ones', 'g_tm'], [pCk])
            kb.cp(gcc[:], psC[0:64, 0:4], [pCk], ['gcc'], eng='act')
            kb.stt(t1[:], R, -1.0, NEG1, ALU.mult, ALU.add, [pRk, 'cm'], ['t1'])
            kb.tt(t1[:], t1[:], bc_last(gcc[:], 64), ALU.add, ['t1', 'gcc'], ['t1'])
            kb.act(E1[:], t1[:], AF.Exp, ['t1'], ['E1'])
            kb.tt(t2[:], R, NEG2, ALU.add, [pRk, 'cm'], ['t2'])
            kb.tt(t2[:], t2[:], bc_last(gcc[:], 64), ALU.subtract, ['t2', 'gcc'], ['t2'])
            kb.act(E2[:], t2[:], AF.Exp, ['t2'], ['E2'])
            kb.tt(Wm[:], E1[:], bc_last(b_tm[:, c, :], 64), ALU.mult, ['E1', 'b_tm'], ['Wm'])
            kb.tt(Wm[:], Wm[:], ST1, ALU.mult, ['Wm', 'cm'], ['Wm'], eng='pool')
            kb.tt(WmT[:], Bb, E2[:], ALU.mult, [pRk, 'E2'], ['WmT'])
            kb.tt(WmT[:], WmT[:], ST2, ALU.mult, ['WmT', 'cm'], ['WmT'], eng='pool')
            kb.act(egc[:], gcc[:], AF.Exp, ['gcc'], ['egc'])
            kb.tt(bege[:], egc[:], b_tm[:, c, :], ALU.mult, ['egc', 'b_tm'], ['bege'])
            kb.tt(ekd[:], psC[0:64, 4:8], gcc[:], ALU.subtract, [pCk, 'gcc'], ['ekd'])
            kb.act(ekd[:], ekd[:], AF.Exp, ['ekd'], ['ekd'])
            kb.act(GGL[:, g, :], psC[0:128, 4:8], AF.Exp, [pCk], ['GGL'])
            kb.cp(GEGC[:, g, :], egc[:], ['egc'], ['GEGC'], eng='pool')
            if DBG_CUT < 1:
                continue
            psK, pKk = kb.psum()
            for hl in range(2):
                kb.mm(psK[0:64, hl * 64:(hl + 1) * 64], kT[:, hl, csl], kT[:, hl, csl], True, True, ['qkv'], [pKk])
                kb.mm(psK[0:64, 128 + hl * 64:128 + (hl + 1) * 64], kT[:, hl, csl], qT[:, hl, csl], True, True, ['qkv'], [pKk])
            KK = psK[0:64, 0:128].rearrange("p (a b) -> p a b", b=64); QK = psK[0:64, 128:256].rearrange("p (a b) -> p a b", b=64)
            P, PT, XTm = Pb[0], PTb[0], XTb[0]
            for d in range(2):
                kb.tt(P[:, 2 * d:2 * d + 2, :], KK, Wm[:, 2 * d:2 * d + 2, :], ALU.mult, [pKk, 'Wm', 'P0'], ['P0'])
                kb.tt(PT[:, 2 * d:2 * d + 2, :], KK, WmT[:, 2 * d:2 * d + 2, :], ALU.mult, [pKk, 'WmT', 'PT0'], ['PT0'])
                kb.tt(attT[:, 2 * d:2 * d + 2, :], QK, E2[:, 2 * d:2 * d + 2, :], ALU.mult, [pKk, 'E2', 'attT'], ['attT'])
            kb.tt(XTm[:], ID4, PT[:], ALU.subtract, ['cm', 'PT0', 'X0'], ['X0'])
            if DBG_CUT < 2:
                continue
            cur = 0
            for lvl in range(5):
                nxt = 1 - cur
                psP, pPk = kb.psum()
                for i in range(4):
                    kb.mm(psP[0:64, i * 64:(i + 1) * 64], PTb[cur][:, i, :], Pb[cur][:, i, :], True, True, ['P%d' % cur, 'PT%d' % cur], [pPk])
                    if lvl < 4:
                        kb.mm(psP[0:64, 256 + i * 64:256 + (i + 1) * 64], Pb[cur][:, i, :], PTb[cur][:, i, :], True, True, ['P%d' % cur, 'PT%d' % cur], [pPk])
                kb.cp(Pb[nxt][:].rearrange("p a b -> p (a b)"), psP[0:64, 0:256], [pPk, 'P%d' % nxt], ['P%d' % nxt], eng='act')
                if lvl < 4:
                    kb.cp(PTb[nxt][:].rearrange("p a b -> p (a b)"), psP[0:64, 256:512], [pPk, 'PT%d' % nxt], ['PT%d' % nxt])
                psX, pXk = kb.psum()
                for i in range(4):
                    kb.mm(psX[0:64, i * 64:(i + 1) * 64], Pb[nxt][:, i, :], XTb[cur][:, i, :], True, True, ['P%d' % nxt, 'X%d' % cur], [pXk])
                kb.tt(XTb[nxt][:].rearrange("p a b -> p (a b)"), psX[0:64, 0:256], XTb[cur][:].rearrange("p a b -> p (a b)"), ALU.add, [pXk, 'X%d' % cur, 'X%d' % nxt], ['X%d' % nxt])
                cur = nxt
            XTf, xk = XTb[cur], 'X%d' % cur
            if DBG_CUT < 3:
                continue
            for d in range(2):
                dsl = slice(2 * d, 2 * d + 2)
                kb.tt(VB[:, dsl, :], v_tm[:, c, :, :], bc_last(b_tm[:, c, dsl], 128), ALU.mult, ['v_tm', 'b_tm', 'VB'], ['VB'], eng='pool')
                kb.tt(KBG[:, dsl, :], k_tm[:, c, :, :], bc_last(bege[:, dsl], 128), ALU.mult, ['k_tm', 'bege', 'KBG'], ['KBG'])
                kb.tt(KDg[:, dsl, :], k_tm[:, c, :, :], bc_last(ekd[:, dsl], 128), ALU.mult, ['k_tm', 'ekd', 'KDg'], ['KDg'], eng='pool')
            psU, pUk = kb.psum()
            psKc, pKck = kb.psum()
            for i in range(4):
                kb.mm(psU[0:64, i * 128:(i + 1) * 128], XTf[:, i, :], VB[:, i, :], True, True, [xk, 'VB'], [pUk])
                kb.mm(psKc[0:128, i * 64:(i + 1) * 64], KBG[:, i, :], XTf[:, i, :], True, True, [xk, 'KBG'], [pKck])
            kb.cp(U0s[:].rearrange("p a b -> p (a b)"), psU[0:64, :], [pUk, 'U0s'], ['U0s'], eng='act')
            kb.cp(KCs[:].rearrange("p a b -> p (a b)"), psKc[0:128, 0:256], [pKck, 'KCs'], ['KCs'])
            kb.store(G_U0[g], U0s[:], 'U0s', ('G_U0', g), final=False)
            kb.store(G_KC[g], KCs[:], 'KCs', ('G_KC', g), final=False)
            kb.store(G_KD[g], KDg[:], 'KDg', ('G_KD', g), final=False)
            kb.store(G_AT[g], attT[:], 'attT', ('G_AT', g), final=False)
            if DBG_CUT < 4:
                continue
            LA = la_tm[:, c, :, :]
            psB, pBk = kb.psum()
            for i in range(4):
                kb.mm(psB[0:64, i * 64:(i + 1) * 64], LA[:, i, :], cms[:, 2 * (i // 2), :], True, True, ['la_tm', 'cm'], [pBk])
            for d in range(2):
                kb.mm(psB[0:64, 256 + d * 128:256 + (d + 1) * 128], cms[:, 2 * d, :], LA[:, 2 * d:2 * d + 2, :].rearrange("p a b -> p (a b)"), True, True, ['la_tm', 'cm'], [pBk])
            psT, pTk = kb.psum()
            kb.mm(psT[0:64, 0:256], ones[0:64, 0:64], LA.rearrange("p a b -> p (a b)"), True, True, ['la_tm', 'ones'], [pTk])
            for i in range(4):
                kb.mm(psT[0:64, 256 + i:257 + i], LA[:, i, :], ones[0:64, 0:1], True, True, ['la_tm', 'ones'], [pTk])
            bcT = psB[0:64, 0:256].rearrange("p (a b) -> p a b", b=64)
            for d in range(2):
                ridx = 32 if d == 0 else 31
                kb.cp(refc[:, 2 * d:2 * d + 2], bcT[:, 2 * d:2 * d + 2, ridx], [pBk, 'refc'], ['refc'])
            kb.tt(dlt[:], bcT, bc_last(refc[:], 64), ALU.subtract, [pBk, 'refc'], ['dlt'])
            kb.act(Ea[:], dlt[:], AF.Exp, ['dlt'], ['Ea'])
            kb.act(Eb[:], dlt[:], AF.Exp, ['dlt'], ['Eb'], scale=-1.0)
            kb.act(Ec[:], bcT, AF.Exp, [pBk], ['Ec'])
            for d in range(2):
                dsl = slice(2 * d, 2 * d + 2)
                kb.tt(QQ[:, dsl, :], qlT[:, :, csl], Ea[:, dsl, :], ALU.mult, ['qlT', 'Ea', 'QQ'], ['QQ'])
                kb.tt(KKl[:, dsl, :], klT[:, :, csl], Eb[:, dsl, :], ALU.mult, ['klT', 'Eb', 'KKl'], ['KKl'], eng='pool')
                kb.tt(QDl[:, dsl, :], qlT[:, :, csl], Ec[:, dsl, :], ALU.mult, ['qlT', 'Ec', 'QDl'], ['QDl'])
            psA, pAk = kb.psum()
            for i in range(4):
                kb.mm(psA[0:64, i * 64:(i + 1) * 64], KKl[:, i, :], QQ[:, i, :], True, True, ['KKl', 'QQ'], [pAk])
            kb.tt(ATl_[:], psA[0:64, 0:256].rearrange("p (a b) -> p a b", b=64), INC2, ALU.mult, [pAk, 'cm'], ['ATTl'])
            kb.cp(bct[:].rearrange("p a b -> p (a b)"), psB[0:64, 256:512], [pBk], ['bct'], eng='act')
            kb.tt(bct[:].rearrange("p a b -> p (a b)"), psT[0:64, 0:256], bct[:].rearrange("p a b -> p (a b)"), ALU.subtract, [pTk, 'bct'], ['bct'])
            kb.act(bct[:], bct[:], AF.Exp, ['bct'], ['bct'])
            for d in range(2):
                dsl = slice(2 * d, 2 * d + 2)
                kb.tt(KDl[:, dsl, :], kl_tm[:, c, :, :], bct[:, dsl, :], ALU.mult, ['kl_tm', 'bct', 'KDl'], ['KDl'])
            kb.act(LGL[:, g, :], psT[0:64, 256:260], AF.Exp, [pTk], ['LGL'])
            kb.store(L_KD[g], KDl[:], 'KDl', ('L_KD', g), final=False)
            kb.store(L_AT[g], ATl_[:], 'ATTl', ('L_AT', g), final=False)
            kb.store(L_QD[g], QDl[:], 'QDl', ('L_QD', g), final=False)
            kb.store(L_V[g], vl_tm[:, c, :, :], 'vl_tm', ('L_V', g), final=False)
    kb.mixa_state = dict(G_U0=G_U0, G_KC=G_KC, G_KD=G_KD, G_AT=G_AT, G_QT=G_QT, L_KD=L_KD, L_V=L_V, L_AT=L_AT, L_QD=L_QD, OG=OG, OL=OL, ZT=ZT,
                         GGL=GGL, GEGC=GEGC, LGL=LGL, YT=YT, ident=ident, hnws=hnws, tiles=tiles,
                         g0of=(lambda g: 0 if g < NCTX else NCTX + ((g - NCTX) // 8) * 8))
    if stage >= 4:
        emit_mixa_scan(kb)
    return kb.finish()


def emit_mixa_scan(kb):
    st = kb.mixa_state
    S = kb.S
    G_U0, G_KC, G_KD, G_AT, G_QT = st['G_U0'], st['G_KC'], st['G_KD'], st['G_AT'], st['G_QT']
    L_KD, L_V, L_AT, L_QD, OG, OL, ZT = st['L_KD'], st['L_V'], st['L_AT'], st['L_QD'], st['OG'], st['OL'], st['ZT']
    GGL, GEGC, LGL, YT, ident, hnws = st['GGL'], st['GEGC'], st['LGL'], st['YT'], st['ident'], st['hnws']
    Sg = kb.sb('Sg', [128, 4, 128]); Sl = kb.sb('Sl', [64, 4, 128])
    kb.memset(Sg[:], 0.0, [('Sg', i) for i in range(4)]); kb.memset(Sl[:], 0.0, [('Sl', i) for i in range(4)])
    NB = 2
    U0b = [[kb.sb('U0b%d%d' % (d, s), [64, 2, 128]) for s in range(NB)] for d in range(2)]
    KCb = [[kb.sb('KCb%d%d' % (d, s), [128, 2, 64]) for s in range(NB)] for d in range(2)]
    KDb = [[kb.sb('KDb%d%d' % (d, s), [64, 2, 128]) for s in range(NB)] for d in range(2)]
    ATb = [[kb.sb('ATb%d%d' % (d, s), [64, 2, 64]) for s in range(NB)] for d in range(2)]
    QTb = [[kb.sb('QTb%d%d' % (d, s), [128, 2, 64]) for s in range(NB)] for d in range(2)]
    LKDb = [[kb.sb('LKDb%d%d' % (d, s), [64, 2, 64]) for s in range(NB)] for d in range(2)]
    LVb = [[kb.sb('LVb%d%d' % (d, s), [64, 2, 128]) for s in range(NB)] for d in range(2)]
    LATb = [[kb.sb('LATb%d%d' % (d, s), [64, 2, 64]) for s in range(NB)] for d in range(2)]
    LQDb = [[kb.sb('LQDb%d%d' % (d, s), [64, 2, 64]) for s in range(NB)] for d in range(2)]
    ub = [kb.sb('ub%d' % d, [64, 2, 128]) for d in range(2)]
    qs = [kb.sb('qs%d' % d, [64, 2, 128]) for d in range(2)]
    ob = [kb.sb('ob%d' % d, [64, 2, 128]) for d in range(2)]
    obl = [kb.sb('obl%d' % d, [64, 2, 128]) for d in range(2)]
    order = [list(range(NG)), list(range(NCTX - 1, -1, -1)) + list(range(NG - 1, NCTX - 1, -1))]
    for s in range(NG):
        slot = s % NB
        for d in range(2):
            g = order[d][s]
            emit = g >= NCTX
            dsl = slice(2 * d, 2 * d + 2)
            gk = ('gop', d, slot); lk = ('lop', d, slot)
            S.dma('sp', U0b[d][slot][:], G_U0[g][:, dsl, :], reads=[('G_U0', g)], writes=[gk])
            S.dma('sp', KCb[d][slot][:], G_KC[g][:, dsl, :], reads=[('G_KC', g)], writes=[gk])
            S.dma('sp', KDb[d][slot][:], G_KD[g][:, dsl, :], reads=[('G_KD', g)], writes=[gk])
            if emit:
                S.dma('sp', ATb[d][slot][:], G_AT[g][:, dsl, :], reads=[('G_AT', g)], writes=[gk])
                S.dma('sp', QTb[d][slot][:], G_QT[g], reads=[('G_QT', st['g0of'](g), 0), ('G_QT', st['g0of'](g), 1)], writes=[gk])
            S.dma('sp', LKDb[d][slot][:], L_KD[g][:, dsl, :], reads=[('L_KD', g)], writes=[lk])
            S.dma('sp', LVb[d][slot][:], L_V[g], reads=[('L_V', g)], writes=[lk])
            if emit:
                S.dma('sp', LATb[d][slot][:], L_AT[g][:, dsl, :], reads=[('L_AT', g)], writes=[lk])
                S.dma('sp', LQDb[d][slot][:], L_QD[g][:, dsl, :], reads=[('L_QD', g)], writes=[lk])
            ps1, p1k = kb.psum()
            for hl in range(2):
                i = 2 * d + hl
                kb.mm(ps1[0:64, hl * 128:(hl + 1) * 128], KCb[d][slot][:, hl, :], Sg[:, i, :], True, True, [gk, ('Sg', i)], [p1k])
            if emit:
                pso, pok = kb.psum()
                for hl in range(2):
                    i = 2 * d + hl
                    kb.mm(pso[0:64, hl * 128:(hl + 1) * 128], QTb[d][slot][:, hl, :], Sg[:, i, :], True, True, [gk, ('Sg', i)], [pok])
            kb.tt(ub[d][:].rearrange("p a b -> p (a b)"), U0b[d][slot][:].rearrange("p a b -> p (a b)"), ps1[0:64, 0:256], ALU.subtract, [gk, p1k, ('ub', d)], [('ub', d)])
            if emit:
                for hl in range(2):
                    i = 2 * d + hl
                    kb.ts(qs[d][:, hl, :], pso[0:64, hl * 128:(hl + 1) * 128], GEGC[:, g, i:i + 1], None, ALU.mult, None, [pok, 'GEGC', ('qs', d)], [('qs', d)], eng='pool' if False else 'dve')
                pso2, po2k = kb.psum()
                for hl in range(2):
                    kb.mm(pso2[0:64, hl * 128:(hl + 1) * 128], ATb[d][slot][:, hl, :], ub[d][:, hl, :], True, True, [gk, ('ub', d)], [po2k])
                kb.tt(ob[d][:].rearrange("p a b -> p (a b)"), qs[d][:].rearrange("p a b -> p (a b)"), pso2[0:64, 0:256], ALU.add, [('qs', d), po2k, ('ob', d)], [('ob', d)])
                kb.store(OG[g - NCTX][:, dsl, :], ob[d][:], ('ob', d), ('OG', g, d), final=False)
            pss, psk = kb.psum()
            for hl in range(2):
                kb.mm(pss[0:128, hl * 128:(hl + 1) * 128], KDb[d][slot][:, hl, :], ub[d][:, hl, :], True, True, [gk, ('ub', d)], [psk])
            for hl in range(2):
                i = 2 * d + hl
                kb.stt(Sg[:, i, :], Sg[:, i, :], GGL[:, g, i:i + 1], pss[0:128, hl * 128:(hl + 1) * 128], ALU.mult, ALU.add, [psk, 'GGL', ('Sg', i)], [('Sg', i)])
            if emit:
                pso, pok = kb.psum()
                for hl in range(2):
                    i = 2 * d + hl
                    kb.mm(pso[0:64, hl * 128:(hl + 1) * 128], LQDb[d][slot][:, hl, :], Sl[:, i, :], True, False, [lk, ('Sl', i)], [pok])
                    kb.mm(pso[0:64, hl * 128:(hl + 1) * 128], LATb[d][slot][:, hl, :], LVb[d][slot][:, hl, :], False, True, [lk], [pok])
                kb.cp(obl[d][:].rearrange("p a b -> p (a b)"), pso[0:64, 0:256], [pok, ('obl', d)], [('obl', d)], eng='act')
                kb.store(OL[g - NCTX][:, dsl, :], obl[d][:], ('obl', d), ('OL', g, d), final=False)
            pss, psk = kb.psum()
            for hl in range(2):
                kb.mm(pss[0:64, hl * 128:(hl + 1) * 128], LKDb[d][slot][:, hl, :], LVb[d][slot][:, hl, :], True, True, [lk], [psk])
            for hl in range(2):
                i = 2 * d + hl
                kb.stt(Sl[:, i, :], Sl[:, i, :], LGL[:, g, i:i + 1], pss[0:64, hl * 128:(hl + 1) * 128], ALU.mult, ALU.add, [psk, 'LGL', ('Sl', i)], [('Sl', i)])
    Ob = kb.sb('Ob', [128, 4, 128]); osum = kb.sb('osum', [128, 2, 128]); osq = kb.sb('osq', [128, 2, 128]); ss = kb.sb('ss', [128, 2])
    Zb = kb.sb('Zb', [128, 2, 128]); Yb = kb.sb('Yb', [128, 2, 128])
    for which, (OD, zoff, yoff, ncol) in enumerate(((OG, 0, kb.yoffs[0], 0), (OL, 2, kb.yoffs[1], 1))):
        nm = 'OG' if which == 0 else 'OL'
        for tg in range(NLAT // 2):
            toks = slice(tg * 128, (tg + 1) * 128)
            rk = [(nm, NCTX + 2 * tg + cc, d) for cc in range(2) for d in range(2)]
            S.dma('sp', Ob[:], OD[2 * tg:2 * tg + 2].rearrange("c p i e -> (c p) i e"), reads=rk, writes=['Ob'])
            S.dma('sp', Zb[:], ZT[zoff:zoff + 2, :, toks].rearrange("h e t -> e h t"), reads=[('ZT', zoff + h, (tg * 128 // TT) * TT) for h in range(2)], writes=['Zb'])
            kb.tt(osum[:], Ob[:, 0:2, :], Ob[:, 2:4, :], ALU.add, ['Ob', 'osum'], ['osum'])
            kb.tt(osq[:], osum[:], osum[:], ALU.mult, ['osum', 'osq'], ['osq'], eng='pool')
            S.op('dve', lambda e: e.tensor_reduce(out=ss[:], in_=osq[:], axis=AX.X, op=ALU.add), ['osq', 'ss'], ['ss'])
            kb.act(ss[:], ss[:], AF.Sqrt, ['ss', 'eps'], ['ss'], bias=kb.eps_ap, scale=1.0 / 128)
            kb.recip(ss[:], ss[:], ['ss'], ['ss'])
            kb.tt(osum[:], osum[:], ss[:].unsqueeze(2).to_broadcast([128, 2, 128]), ALU.mult, ['osum', 'ss'], ['osum'])
            psY, pYk = kb.psum()
            for hl in range(2):
                kb.mm(psY[0:128, hl * 128:(hl + 1) * 128], osum[:, hl, :], ident[:, :], True, True, ['osum', 'ident'], [pYk])
            kb.stt(Yb[:].rearrange("p a b -> p (a b)"), psY[0:128, 0:256], hnws[:, ncol:ncol + 1], Zb[:].rearrange("p a b -> p (a b)"), ALU.mult, ALU.mult, [pYk, 'hnw', 'Zb', 'Yb'], ['Yb'])
            kb.store(YT[yoff:yoff + 256, toks].rearrange("(h e) t -> e h t", e=128), Yb[:], 'Yb', ('YT', which, tg))


def mixa_inputs(inp, b, hh):
    e = 0
    W = inp['ab_w_in'][e]
    offs = np.concatenate([[0], np.cumsum(AB_SIZES)])

    def cols(seg, start, n):
        return list(range(offs[seg] + start, offs[seg] + start + n))
    blocks = []
    for seg in (0, 1, 2):
        for hl in range(2):
            blocks.append(cols(seg, (2 * hh + hl) * 128, 128))
    for hl in range(2):
        blocks.append(cols(3, (2 * hh + hl) * 128, 128))
    for seg in (6, 7):
        for hl in range(2):
            blocks.append(cols(seg, (2 * hh + hl) * 64, 64))
    for seg in (8, 9):
        for hl in range(2):
            blocks.append(cols(seg, (2 * hh + hl) * 128, 128))
    for d in range(2):
        blocks.append(cols(10, d * 16, 16))
    Wp = np.zeros((D, 18 * 128), np.float32)
    for bi, cl in enumerate(blocks):
        Wp[:, bi * 128:bi * 128 + len(cl)] = W[:, cl]
    gcols = [offs[4] + d * 4 + 2 * hh + hl for d in range(2) for hl in range(2)] + [offs[5] + d * 4 + 2 * hh + hl for d in range(2) for hl in range(2)]
    wg = np.ascontiguousarray(W[:, gcols].reshape(KC, 128, 8).transpose(1, 0, 2))
    cw = inp['ab_conv_w'][e]
    taps = np.zeros((128, 6, 3), np.float32)
    for seg in range(3):
        for hl in range(2):
            ch0 = seg * 512 + (2 * hh + hl) * 128
            taps[:, seg * 2 + hl, :] = cw[:, ch0:ch0 + 128].T
    al = np.array([inp['gdn_a_log'][e][d, 2 * hh + hl] for d in range(2) for hl in range(2)], np.float32)
    dtb = np.array([inp['gdn_dt_bias'][e][d, 2 * hh + hl] for d in range(2) for hl in range(2)], np.float32)
    gconst = np.zeros((64, 2, 32), np.float32)
    gconst[:, 0, :] = np.tile(al, 8)[None, :]
    gconst[:, 1, :] = np.tile(dtb, 8)[None, :]
    gwb = np.zeros((17, 2, 128), np.float32)
    hs = slice(2 * hh * 64, (2 * hh + 2) * 64)
    for d in range(2):
        gwb[0:16, d, :] = inp['gla_gate_w'][e][d][:, hs]
        gwb[16, d, :] = inp['gla_gate_b'][e][d][hs]
    cm, _ = mixa_consts()
    return dict(
        XT=np.ascontiguousarray(inp['x'][b].T), CXT=np.ascontiguousarray(inp['ctx'][b].T),
        cT=np.ascontiguousarray(np.stack([fm_vec(inp['c'][b]), fm_vec(inp['c_ctx'])], axis=-1)),
        modw=blk_w(np.ascontiguousarray(inp['mod_w'][0][:, 0:2048]), 8), modb=fm_vec(inp['mod_b'][0][0:2048]), nw=fm_vec(inp['norm1_w'][0]),
        wl=blk_w(Wp, 8), wg=wg, taps=taps, gconst=gconst, gwb=gwb,
        hnw=np.ascontiguousarray(np.stack([inp['gdn_norm_w'][e], inp['gla_norm_w'][e]], axis=-1)), cm=cm, ident=np.eye(128, dtype=np.float32))


LSEQ = 8192
NFFT = 16384
CG = 8
MAGIC = 12582912.0


def hy_consts():
    n = np.arange(128, dtype=np.float64)
    th = 2 * np.pi * np.outer(n, n) / 128.0
    Fre, Fim = np.cos(th), -np.sin(th)
    tw = 2 * np.pi * np.outer(n, n) / NFFT
    Tre, Tim = np.cos(tw), -np.sin(tw)
    F64c = np.zeros((128, 256)); F64c[:64, :128] = Fre[:64]; F64c[:64, 128:] = Fim[:64]
    G1 = np.concatenate([Fre, -Fim], 1); G2 = np.concatenate([Fim, Fre], 1)
    fc = np.concatenate([F64c, Fre, Fim, G1, G2, Tre, Tim], 1).astype(np.float32)
    l = LSEQ
    t = np.linspace(0.0, 1.0, l, dtype=np.float32)[:, None]
    w = (np.float32(2.0 * math.pi / l) * np.arange(l, dtype=np.float32))[:, None]
    f = np.linspace(1e-4, 15, 16, dtype=np.float32)[None, :]
    zp = np.concatenate([t, np.cos(f * w), -np.sin(f * w)], axis=-1).astype(np.float32).T
    deltas = np.abs(np.linspace(math.log(1e-2) / 1.5, math.log(1e-2) / 0.3, D, dtype=np.float32))
    win = (np.exp(-t * deltas[None, :]) + np.float32(0.05)).astype(np.float32).T
    return np.ascontiguousarray(fc), np.ascontiguousarray(zp), np.ascontiguousarray(win)


def build_hyc(stage=99, ngroups=None, nb=4, nblk=1, kb=None):
    kb = kb or KB()
    S = kb.S
    L = LSEQ
    NCH = nblk * 128
    if ngroups is None:
        ngroups = NCH // CG
    UT3 = kb.inp('UT3', [3, NCH, nb, L]); tapsd = kb.inp('taps', [128, nblk, 3, 3]); zpd = kb.inp('zp', [33, L]); wind = kb.inp('win', [NCH, L])
    w1d = kb.inp('w1', [33, 64]); w23d = kb.inp('w23', [64, 2, 64]); bfrd = kb.inp('bfr', [64, 4]); fod = kb.inp('fo', [64, 4, NCH]); skipd = kb.inp('skip', [128, nblk, 2])
    fcd = kb.inp('fc', [128, 1280])
    Z2T = kb.outp('Z2T', [NCH, nb, L])
    UC = kb.scratch('UC', [3, nb, NCH, L]); HFs = kb.scratch('HFs', [4, NCH, L])
    kb.init_psum(8)
    fc = kb.sb('fc_s', [128, 1280]); kb.load(fc[:], fcd[:, :], 'fc')
    F64c = fc[0:64, 0:256]; F_re = fc[:, 256:384]; F_im = fc[:, 384:512]; G1 = fc[:, 512:768]; G2 = fc[:, 768:1024]; T_re = fc[:, 1024:1152]; T_im = fc[:, 1152:1280]
    taps = kb.sb('taps_s', [128, nblk, 3, 3]); w1 = kb.sb('w1s', [33, 64]); w23 = kb.sb('w23s', [64, 2, 64]); bfr = kb.sb('bfrs', [64, 4]); fo = kb.sb('fos', [64, 4, NCH]); skip = kb.sb('skips', [128, nblk, 2])
    kb.load(taps[:], tapsd[:, :, :, :], 'taps'); kb.load(w1[:], w1d[:, :], 'w1'); kb.load(w23[:], w23d[:, :, :], 'w23'); kb.load(bfr[:], bfrd[:, :], 'bfr'); kb.load(fo[:], fod[:, :, :], 'fo'); kb.load(skip[:], skipd[:, :, :], 'skip')
    frb = kb.sb('frb', [64, 3])
    kb.tt(frb[:], bfr[:, 0:3], bfr[:, 3:4].to_broadcast([64, 3]), ALU.mult, ['bfr'], ['frb'])
    aoff_p3 = kb.aoff
    CW = 2048
    Uin = kb.sb('Uin', [128, CW + 2]); Uout = kb.sb('Uout', [128, CW])
    uckeys = []
    for seg in range(3):
        for b in range(nb):
            for cb in range(nblk):
                chs = slice(cb * 128, (cb + 1) * 128)
                for t0 in range(0, L, CW):
                    lo = max(t0 - 1, 0); hi = min(t0 + CW + 1, L)
                    if t0 == 0:
                        kb.memset(Uin[:, 0:1], 0.0, ['Uin'], eng='dve')
                    if t0 + CW == L:
                        kb.memset(Uin[:, CW + 1:CW + 2], 0.0, ['Uin'], eng='dve')
                    S.dma('sp', Uin[:, lo - (t0 - 1):hi - (t0 - 1)], UT3[seg, chs, b, lo:hi], writes=['Uin'])
                    kb.ts(Uout[:], Uin[:, 1:CW + 1], taps[:, cb, seg, 1:2], None, ALU.mult, None, ['Uin', 'taps', 'Uout'], ['Uout'])
                    kb.stt(Uout[:], Uin[:, 0:CW], taps[:, cb, seg, 0:1], Uout[:], ALU.mult, ALU.add, ['Uin', 'taps', 'Uout'], ['Uout'])
                    kb.stt(Uout[:], Uin[:, 2:CW + 2], taps[:, cb, seg, 2:3], Uout[:], ALU.mult, ALU.add, ['Uin', 'taps', 'Uout'], ['Uout'])
                    kb.store(UC[seg, b, chs, t0:t0 + CW], Uout[:], 'Uout', ('UC', seg, b, cb, t0), final=False)
                    uckeys.append(('UC', seg, b, cb, t0))
    zp = kb.sb('zp_s', [33, TT]); aa = kb.sb('aa', [64, TT]); tq = kb.sb('tq', [64, TT]); hd = [kb.sb('hd%d' % i, [64, TT]) for i in range(2)]
    hft = kb.sb('hft', [128, 2, TT]); wt = kb.sb('wt', [128, TT])

    def sin_layer(ps, pk, li, out, okey):
        kb.act(aa[:], ps[0:64, :TT], AF.Identity, [pk, 'bfr', 'frb'], ['aa'], bias=frb[:, li:li + 1], scale=bfr[:, 3:4])
        kb.ts(tq[:], aa[:], 1.0 / (2 * math.pi), MAGIC, ALU.mult, ALU.add, ['aa'], ['tq'])
        kb.ts(tq[:], tq[:], MAGIC, -2 * math.pi, ALU.subtract, ALU.mult, ['tq'], ['tq'])
        kb.tt(aa[:], aa[:], tq[:], ALU.add, ['aa', 'tq'], ['aa'])
        kb.ts(aa[:], aa[:], 3.141592, -3.141592, ALU.min, ALU.max, ['aa'], ['aa'])
        kb.act(out, aa[:], AF.Sin, ['aa'], [okey])
    hfkeys = []
    for ti in range(L // TT):
        tsl = slice(ti * TT, (ti + 1) * TT)
        kb.load(zp[:], zpd[:, tsl], 'zp')
        ps, pk = kb.psum()
        kb.mm(ps[0:64, :TT], w1[:, :], zp[:, :], True, True, ['w1', 'zp'], [pk])
        sin_layer(ps, pk, 0, hd[0][:], 'hd0')
        ps, pk = kb.psum()
        kb.mm(ps[0:64, :TT], w23[:, 0, :], hd[0][:], True, True, ['w23', 'hd0'], [pk])
        sin_layer(ps, pk, 1, hd[1][:], 'hd1')
        ps, pk = kb.psum()
        kb.mm(ps[0:64, :TT], w23[:, 1, :], hd[1][:], True, True, ['w23', 'hd1'], [pk])
        sin_layer(ps, pk, 2, hd[0][:], 'hd0')
        for cb in range(nblk):
            chs = slice(cb * 128, (cb + 1) * 128)
            kb.load(wt[:], wind[chs, tsl], 'wt')
            for f in range(4):
                ps, pk = kb.psum()
                kb.mm(ps[:, :TT], fo[:, f, chs], hd[0][:], True, True, ['fo', 'hd0'], [pk])
                hk = ('hft', f % 2)
                kb.tt(hft[:, f % 2, :], ps[:, :TT], wt[:], ALU.mult, [pk, 'wt', hk], [hk])
                if ti == 0:
                    if f < 2:
                        kb.tt(hft[:, f % 2, 0:1], hft[:, f % 2, 0:1], skip[:, cb, f:f + 1], ALU.add, [hk, 'skip'], [hk])
                    else:
                        kb.memset(hft[:, f % 2, 0:1], 0.0, [hk], eng='dve')
                kb.store(HFs[f, chs, tsl], hft[:, f % 2, :], hk, ('HFs', f, cb, ti), final=False)
                hfkeys.append(('HFs', f, cb, ti))
    S.barrier()
    kb.aoff = aoff_p3
    Tre_b = T_re.unsqueeze(1).to_broadcast([128, CG, 128]); Tim_b = T_im.unsqueeze(1).to_broadcast([128, CG, 128])

    def mk_lane(tag):
        return dict(tag=tag, As=kb.sb('As' + tag, [128, CG, 2, 128]), Bt=kb.sb('Bt' + tag, [128, CG, 3, 128]),
                    tw=[kb.sb('tw%d%s' % (i, tag), [128, CG, 128]) for i in range(4)])
    LF, LD = mk_lane('F'), mk_lane('D')
    Xg = kb.sb('Xg', [64, 3, CG, 128]); Ys = kb.sb('Ys', [128, CG, 2, 128])
    Hx = [kb.sb('Hx%d' % i, [64, CG, 128]) for i in range(2)]
    Htmp = kb.sb('Htmp', [128, CG, 2, 128])
    KSp = [kb.sb('KSp%d' % i, [128, 2, CG, 2, 128]) for i in range(2)]

    def fwd_fft(ln, Xv, xkey, dst, dkey):
        t = ln['tag']; As = ln['As']; Bt = ln['Bt']; tw = ln['tw']
        ak, bk = 'As' + t, 'Bt' + t
        tk = ['tw%d%s' % (i, t) for i in range(4)]
        for c0 in range(0, CG, 2):
            ps, pk = kb.psum()
            for c in (c0, c0 + 1):
                kb.mm(ps[:, (c - c0) * 256:(c - c0 + 1) * 256], Xv[:, c, :], F64c, True, True, [xkey, 'fc'], [pk])
            kb.cp(As[:, c0:c0 + 2, :, :].rearrange("p a b c -> p (a b c)"), ps[:, :], [pk, ak], [ak], eng='act')
            yield
        Are, Aim = As[:, :, 0, :], As[:, :, 1, :]
        kb.tt(tw[0][:], Are, Tre_b, ALU.mult, [ak, 'fc', tk[0]], [tk[0]])
        kb.tt(tw[1][:], Aim, Tim_b, ALU.mult, [ak, 'fc', tk[1]], [tk[1]], eng='pool')
        kb.tt(tw[2][:], Are, Tim_b, ALU.mult, [ak, 'fc', tk[2]], [tk[2]], eng='pool')
        kb.tt(tw[3][:], Aim, Tre_b, ALU.mult, [ak, 'fc', tk[3]], [tk[3]])
        yield
        kb.tt(Bt[:, :, 1, :], tw[0][:], tw[1][:], ALU.subtract, [tk[0], tk[1], bk], [bk])
        kb.tt(Bt[:, :, 2, :], tw[2][:], tw[3][:], ALU.add, [tk[2], tk[3], bk], [bk], eng='pool')
        yield
        kb.ts(Bt[:, :, 0, :], Bt[:, :, 2, :], -1.0, None, ALU.mult, None, [bk], [bk])
        yield
        for c0 in range(0, CG, 2):
            ps, pk = kb.psum()
            o = ps[:, :].rearrange("p (c x) -> p c x", c=2)
            kb.mm(o, F_re, Bt[:, c0:c0 + 2, 1:3, :].rearrange("p c a b -> p c (a b)"), True, False, [bk, 'fc'], [pk])
            kb.mm(o, F_im, Bt[:, c0:c0 + 2, 0:2, :].rearrange("p c a b -> p c (a b)"), False, True, [bk, 'fc'], [pk])
            kb.cp(dst[:, c0:c0 + 2, :, :].rearrange("p a b c -> p (a b c)"), ps[:, :], [pk, dkey], [dkey], eng='act')
            yield

    def conv(ln, zidx, gidx, Kt, kkey, order):
        t = ln['tag']; As = ln['As']; Bt = ln['Bt']; tw = ln['tw']
        ak, bk = 'As' + t, 'Bt' + t
        tk = ['tw%d%s' % (i, t) for i in range(4)]
        yield from fwd_fft(ln, Xg[:, zidx, :, :], ('Xg', zidx), As, ak)
        Xre, Xim = As[:, :, 0, :], As[:, :, 1, :]
        Kre, Kim = Kt[:, order, :, 0, :], Kt[:, order, :, 1, :]
        kb.tt(tw[0][:], Xre, Kre, ALU.mult, [ak, kkey, tk[0]], [tk[0]])
        kb.tt(tw[1][:], Xim, Kim, ALU.mult, [ak, kkey, tk[1]], [tk[1]], eng='pool')
        kb.tt(tw[2][:], Xre, Kim, ALU.mult, [ak, kkey, tk[2]], [tk[2]], eng='pool')
        kb.tt(tw[3][:], Xim, Kre, ALU.mult, [ak, kkey, tk[3]], [tk[3]])
        yield
        kb.tt(Ys[:, :, 0, :], tw[0][:], tw[1][:], ALU.subtract, [tk[0], tk[1], 'Ys'], ['Ys'])
        kb.tt(Ys[:, :, 1, :], tw[2][:], tw[3][:], ALU.add, [tk[2], tk[3], 'Ys'], ['Ys'], eng='pool')
        yield
        Cs = As
        for c0 in range(0, CG, 2):
            ps, pk = kb.psum()
            for c in (c0, c0 + 1):
                o = ps[:, (c - c0) * 256:(c - c0 + 1) * 256]
                kb.mm(o, Ys[:, c, 0, :], G1, True, False, ['Ys', 'fc'], [pk])
                kb.mm(o, Ys[:, c, 1, :], G2, False, True, ['Ys', 'fc'], [pk])
            kb.cp(Cs[:, c0:c0 + 2, :, :].rearrange("p a b c -> p (a b c)"), ps[:, :], [pk, ak], [ak], eng='act')
            yield
        Cre, Cim = Cs[:, :, 0, :], Cs[:, :, 1, :]
        Cp = Bt
        kb.tt(tw[0][:], Cre, Tre_b, ALU.mult, [ak, 'fc', tk[0]], [tk[0]])
        kb.tt(tw[1][:], Cim, Tim_b, ALU.mult, [ak, 'fc', tk[1]], [tk[1]], eng='pool')
        kb.tt(tw[2][:], Cim, Tre_b, ALU.mult, [ak, 'fc', tk[2]], [tk[2]], eng='pool')
        kb.tt(tw[3][:], Cre, Tim_b, ALU.mult, [ak, 'fc', tk[3]], [tk[3]])
        yield
        kb.tt(Cp[:, :, 0, :], tw[0][:], tw[1][:], ALU.add, [tk[0], tk[1], bk], [bk])
        kb.tt(Cp[:, :, 1, :], tw[2][:], tw[3][:], ALU.subtract, [tk[2], tk[3], bk], [bk], eng='pool')
        yield
        for c0 in range(0, CG, 4):
            ps, pk = kb.psum()
            o = ps[0:64, :].rearrange("p (c x) -> p c x", c=4)
            kb.mm(o, F_re[:, 0:64], Cp[:, c0:c0 + 4, 0, :], True, False, [bk, 'fc'], [pk])
            kb.mm(o, F_im[:, 0:64], Cp[:, c0:c0 + 4, 1, :], False, True, [bk, 'fc'], [pk])
            kb.tt(Xg[:, zidx, c0:c0 + 4, :].rearrange("p a b -> p (a b)"), ps[0:64, :], Xg[:, gidx, c0:c0 + 4, :].rearrange("p a b -> p (a b)"), ALU.mult, [pk, ('Xg', gidx), ('Xg', zidx)], [('Xg', zidx)])
            yield

    def filter_lane(gi):
        ch = slice(gi * CG, (gi + 1) * CG)
        Kt = KSp[gi % 2]; kkey = ('KSp', gi % 2)
        for order in range(2):
            for di in range(2):
                f = di * 2 + order
                hb = Hx[f % 2]; hk = ('Hx', f % 2)
                S.dma('sp', hb[:], HFs[f, ch, :].rearrange("c (a b) -> a c b", b=128), reads=hfkeys, writes=[hk])
                if di == 0:
                    yield from fwd_fft(LF, hb, hk, Kt[:, order, :, :, :], kkey)
                else:
                    yield from fwd_fft(LF, hb, hk, Htmp, 'Htmp')
                    kb.tt(Kt[:, order, :, 0, :], Kt[:, order, :, 0, :], Htmp[:, :, 0, :], ALU.add, [kkey, 'Htmp'], [kkey])
                    kb.tt(Kt[:, order, :, 1, :], Kt[:, order, :, 1, :], Htmp[:, :, 1, :], ALU.subtract, [kkey, 'Htmp'], [kkey], eng='pool')
                    yield
        kb.ts(Kt[:].rearrange("p a b c d -> p (a b c d)"), Kt[:].rearrange("p a b c d -> p (a b c d)"), 1.0 / NFFT, None, ALU.mult, None, [kkey], [kkey])
        yield

    def data_lane(gi):
        ch = slice(gi * CG, (gi + 1) * CG)
        Kt = KSp[gi % 2]; kkey = ('KSp', gi % 2)
        for b in range(nb):
            for seg in range(3):
                S.dma('sp', Xg[:, seg, :, :], UC[seg, b, ch, :].rearrange("c (a b) -> a c b", b=128), reads=uckeys, writes=[('Xg', seg)])
            yield from conv(LD, 0, 1, Kt, kkey, 0)
            yield from conv(LD, 0, 2, Kt, kkey, 1)
            kb.store(Z2T[ch, b, :].rearrange("c (a b) -> a c b", b=128), Xg[:, 0, :, :], ('Xg', 0), ('Z2T', gi, b))
            yield

    for step in range(ngroups + 1):
        lanes = []
        if step < ngroups:
            lanes.append(filter_lane(step))
        if step >= 1:
            lanes.append(data_lane(step - 1))
        while lanes:
            for g in list(lanes):
                try:
                    next(g)
                except StopIteration:
                    lanes.remove(g)
    return kb.finish()


def hyc_inputs(inp, c0, nblk):
    o = 0
    nch = nblk * 128
    fc, zp, win = hy_consts()
    cw = inp['hy_conv_w'][o]
    taps = np.zeros((128, nblk, 3, 3), np.float32)
    for cb in range(nblk):
        for seg in range(3):
            ch0 = seg * D + c0 + cb * 128
            taps[:, cb, seg, :] = cw[:, ch0:ch0 + 128].T
    fo_full = inp['hy_filt_out'][o].reshape(64, 2, 2, D)
    fo = np.ascontiguousarray(fo_full[:, :, :, c0:c0 + nch].reshape(64, 4, nch))
    sk = inp['hy_skip'][o][:, c0:c0 + nch]
    skip = np.ascontiguousarray(sk.reshape(2, nblk, 128).transpose(2, 1, 0))
    return dict(
        taps=taps, zp=zp, win=np.ascontiguousarray(win[c0:c0 + nch]),
        w1=np.ascontiguousarray(inp['hy_pos_w1'][o]), w23=np.ascontiguousarray(np.stack([inp['hy_pos_w2'][o], inp['hy_pos_w3'][o]], axis=1)),
        bfr=np.ascontiguousarray(np.stack([inp['hy_pos_b1'][o], inp['hy_pos_b2'][o], inp['hy_pos_b3'][o], inp['hy_freq'][o]], axis=1)),
        fo=fo, skip=skip, fc=fc)


ARENA = 53200


def build_fused():
    kb = KB(fused=True, arena_floats=ARENA)
    nc = kb.nc
    L = LSEQ
    XT = nc.dram_tensor('XT', [D, L], F32, kind="ExternalInput").ap()
    CXT = nc.dram_tensor('CXT', [D, 256], F32, kind="ExternalInput").ap()
    Yint = nc.dram_tensor('Yint', [D, L], F32, kind="Internal").ap()
    H0 = nc.dram_tensor('H0int', [D, L], F32, kind="Internal").ap()
    U = nc.dram_tensor('Uint', [3 * D, L], F32, kind="Internal").ap()
    Z2 = nc.dram_tensor('Z2int', [D, 1, L], F32, kind="Internal").ap()
    kb.init_psum(8)
    for hh in range(2):
        kb.next_stage('a%d_' % hh, {'XT': XT, 'CXT': CXT, 'YT': Yint})
        kb.yoffs = (hh * 256, 512 + hh * 256)
        build_mixa(kb=kb)
    kb.next_stage('b_', {'HT': XT, 'YT': Yint, 'HO': H0, 'UT': U})
    build_post(L, last=False, kb=kb)
    kb.next_stage('c_', {'UT3': U.rearrange("(s c) (b t) -> s c b t", s=3, b=1), 'Z2T': Z2})
    build_hyc(nb=1, nblk=8, kb=kb)
    kb.next_stage('d_', {'HT': H0, 'YT': Z2.rearrange("c b t -> c (b t)")})
    kb.dyn_tok = True
    kb.dyn_half = L // 2
    build_post(L // 2, last=True, kb=kb)
    kb.stage_amax = kb.amax
    return kb.finish(force=True)


def fused_inputs(inp, core):
    b, r = core // 2, core % 2
    m = {}
    for hh in range(2):
        a = mixa_inputs(inp, b, hh)
        m['XT'] = a.pop('XT'); m['CXT'] = a.pop('CXT')
        for k, v in a.items():
            m['a%d_%s' % (hh, k)] = v
    mw, mb = inp['mod_w'], inp['mod_b']
    modw = np.concatenate([mw[0][:, 2 * D:6 * D], mw[1][:, 0:2 * D]], axis=1)
    modb = np.concatenate([mb[0][2 * D:6 * D], mb[1][0:2 * D]])
    bd = dict(wo=blk_w(inp['ab_w_out'][0], 8), modw=blk_w(modw, 8), modb=fm_vec(modb), cT=fm_vec(inp['c'][b]),
              nw=np.ascontiguousarray(np.stack([fm_vec(inp['norm2_w'][0]), fm_vec(inp['norm1_w'][1])], axis=-1)),
              w1=blk_w(inp['ffn_w1'][0], 8), w3=blk_w(inp['ffn_w3'][0], 8), w2=blk_w(inp['ffn_w2'][0], 22),
              wn=blk_w(inp['hy_w_in'][0], 8))
    for k, v in bd.items():
        m['b_' + k] = v
    for k, v in hyc_inputs(inp, 0, 8).items():
        m['c_' + k] = v
    dd = dict(wo=blk_w(inp['hy_w_out'][0], 8), modw=blk_w(np.ascontiguousarray(mw[1][:, 2 * D:6 * D]), 8), modb=fm_vec(mb[1][2 * D:6 * D]), cT=fm_vec(inp['c'][b]),
              nw=np.ascontiguousarray(np.stack([fm_vec(inp['norm2_w'][1]), fm_vec(inp['final_norm_w'])], axis=-1)),
              w1=blk_w(inp['ffn_w1'][1], 8), w3=blk_w(inp['ffn_w3'][1], 8), w2=blk_w(inp['ffn_w2'][1], 22))
    for k, v in dd.items():
        m['d_' + k] = v
    return m


def kernel(**inputs):
    inp = {k: np.asarray(v, dtype=np.float32) for k, v in inputs.items()}
    B, L = 4, LSEQ
    cores = list(range(NCORES))
    nc = build_fused()
    shared = {}
    maps = []
    for c in cores:
        m = fused_inputs(inp, c)
        for k in list(m.keys()):
            if k in shared and shared[k].shape == m[k].shape and k not in ('XT', 'CXT') and not k.endswith('cT') and not k.startswith('a'):
                m[k] = shared[k]
            else:
                shared.setdefault(k, m[k])
        maps.append(m)
    res = run_bass_kernel_spmd(nc, maps, core_ids=cores)
    out = np.zeros((B, L, D), np.float32)
    NT = L // 2
    for c in cores:
        b, r = c // 2, c % 2
        out[b, r * NT:(r + 1) * NT, :] = res.results[c]['d_HO'].T
    return out
```

```python
import math
import os
import numpy as np
DBG_CUT = 99
from contextlib import ExitStack
import concourse.bass as bass
import concourse.mybir as mybir
from concourse.bass_utils import run_bass_kernel_spmd

F32 = mybir.dt.float32
AF = mybir.ActivationFunctionType
ALU = mybir.AluOpType
AX = mybir.AxisListType

D = 1024
KC = 8
DFF = 2816
FC = 22
EPS = 1e-6
NCORES = 8


class Sched:
    ENG = ['pe', 'act', 'dve', 'pool', 'sp']

    def __init__(self, nc, ctx):
        self.nc = nc
        self.ctx = ctx
        self.sem = {e: ctx.enter_context(nc.semaphore('s_' + e)) for e in self.ENG if e != 'sp'}
        self.cnt = {e: 0 for e in self.ENG}
        self.waited = {e: {} for e in self.ENG}
        self.prog = {e: [] for e in self.ENG}
        self.lastw = {}
        self.readers = {}
        self.dsem = {}

    def semh(self, k):
        return self.sem[k] if isinstance(k, str) else self.dsem[k][0]

    def _deps(self, eng, reads, writes):
        need = {}

        def add(tok):
            if tok is not None and need.get(tok[0], 0) < tok[1]:
                need[tok[0]] = tok[1]
        for r in reads:
            add(self.lastw.get(r))
        for w in writes:
            add(self.lastw.get(w))
            for t in self.readers.get(w, ()):
                add(t)
        out = []
        for k, v in need.items():
            if eng == 'pe' and k == 'pe':
                continue
            if self.waited[eng].get(k, 0) >= v:
                continue
            self.waited[eng][k] = v
            out.append((k, v))
        return out

    def op(self, eng, fn, reads=(), writes=()):
        writes = list(writes) + [r for r in reads if isinstance(r, tuple) and r[0] == 'ps' and r not in writes]
        waits = self._deps(eng, reads, writes)
        self.cnt[eng] += 1
        tok = (eng, self.cnt[eng])
        self.prog[eng].append((waits, fn, self.sem[eng], 1))
        for w in writes:
            self.lastw[w] = tok
            self.readers[w] = []
        for r in reads:
            if r not in writes:
                self.readers.setdefault(r, []).append(tok)
        return tok

    def dma(self, q, out, in_, reads=(), writes=(), semkey=None, in_fn=None):
        if semkey is None:
            semkey = writes[0]
        semkey = ('dma', semkey)
        if semkey not in self.dsem:
            self.dsem[semkey] = [self.ctx.enter_context(self.nc.semaphore('d%d' % len(self.dsem))), 0]
        waits = self._deps(q, reads, writes)
        self.dsem[semkey][1] += 16
        tok = (semkey, self.dsem[semkey][1])
        self.prog[q].append((waits, (lambda e, out=out, in_=in_, in_fn=in_fn: e.dma_start(out=out, in_=(in_fn(e) if in_fn is not None else in_))), self.dsem[semkey][0], 16))
        for w in writes:
            self.lastw[w] = tok
            self.readers[w] = []
        for r in reads:
            self.readers.setdefault(r, []).append(tok)
        return tok

    def barrier(self):
        toks = [(e, self.cnt[e]) for e in self.sem if self.cnt[e] > 0] + [(k, v[1]) for k, v in self.dsem.items() if v[1] > 0]
        for eng in self.ENG:
            waits = []
            for k, v in toks:
                if self.waited[eng].get(k, 0) < v:
                    self.waited[eng][k] = v
                    waits.append((k, v))
            self.prog[eng].append((waits, None, None, 0))

    def wait_keys(self, eng, keys):
        waits = self._deps(eng, list(keys), ())
        self.prog[eng].append((waits, None, None, 0))

    def emit(self):
        engs = {'pe': 'tensor', 'act': 'scalar', 'dve': 'vector', 'pool': 'gpsimd', 'sp': 'sync'}
        with self.nc.allow_non_contiguous_dma(reason="small strided pads / per-chunk layouts"), self.nc.Block() as block:
            for e, name in engs.items():
                prog = self.prog[e]
                if not prog:
                    continue

                def body(eng, prog=prog):
                    for waits, fn, sem, inc in prog:
                        for k, v in waits:
                            eng.wait_ge(self.semh(k), v)
                        if fn is not None:
                            fn(eng).then_inc(sem, inc)
                getattr(block, name)(body)


class KB:
    def __init__(self, fused=False, arena_floats=0):
        self.nc = bass.Bass("TRN2", target_bir_lowering=False)
        self.ctx = ExitStack()
        self.S = Sched(self.nc, self.ctx)
        self.ps = []
        self.psi = 0
        self.wbufs = []
        self.wi = 0
        self.outkeys = []
        self.fused = fused
        self.prefix = ''
        self.bind = {}
        self.yoffs = (0, 256)
        self.dyn_tok = False
        self.arena = None
        self.aoff = 0
        self.amax = 0
        if arena_floats:
            self.arena = self.ctx.enter_context(self.nc.sbuf_tensor('arena_all', [128, arena_floats], F32))
            self.asize = arena_floats

    def next_stage(self, prefix, bind=None):
        self.S.barrier()
        self.aoff = 0
        self.wbufs = []
        self.wi = 0
        self.prefix = prefix
        self.bind = dict(bind or {})

    def inp(self, name, shape):
        if name in self.bind:
            return self.bind[name]
        return self.nc.dram_tensor(self.prefix + name, list(shape), F32, kind="ExternalInput").ap()

    def outp(self, name, shape):
        if name in self.bind:
            return self.bind[name]
        return self.nc.dram_tensor(self.prefix + name, list(shape), F32, kind="ExternalOutput").ap()

    def scratch(self, name, shape):
        if name in self.bind:
            return self.bind[name]
        return self.nc.dram_tensor(self.prefix + name, list(shape), F32, kind="Internal").ap()

    def sb(self, name, shape):
        if self.arena is None:
            return self.ctx.enter_context(self.nc.sbuf_tensor(name, list(shape), F32))
        n = 1
        for d in shape[1:]:
            n *= d
        assert self.aoff + n <= self.asize, ('SBUF arena overflow', name, self.aoff, n)
        ap = self.arena[0:shape[0], self.aoff:self.aoff + n]
        self.aoff += n
        self.amax = max(self.amax, self.aoff)
        if len(shape) > 2:
            names = ['d%d' % i for i in range(len(shape) - 1)]
            ap = ap.rearrange("p (%s) -> p %s" % (' '.join(names), ' '.join(names)), **{nm: shape[i + 1] for i, nm in enumerate(names[:-1])})
        return ap

    def init_psum(self, n=8):
        if self.ps:
            return
        for i in range(n):
            self.ps.append(self.ctx.enter_context(self.nc.psum_tensor('ps%d' % i, [128, 512], F32)))

    def psum(self):
        i = self.psi
        self.psi = (self.psi + 1) % len(self.ps)
        return self.ps[i], ('ps', i)

    def init_wpool(self, n, width):
        for i in range(n):
            self.wbufs.append(self.sb('wbuf%d' % i, [128, width]))

    def wload(self, src, width, q='sp'):
        i = self.wi
        self.wi = (self.wi + 1) % len(self.wbufs)
        buf = self.wbufs[i]
        self.S.dma(q, buf[:, 0:width], src, writes=[('w', i)])
        return buf, ('w', i)

    def mm(self, out, lhsT, rhs, start, stop, reads, writes):
        return self.S.op('pe', lambda e: e.matmul(out, lhsT=lhsT, rhs=rhs, start=start, stop=stop), reads, writes)

    def act(self, out, in_, func, reads, writes, bias=None, scale=None, eng='act'):
        kw = {}
        if bias is not None:
            kw['bias'] = bias
        if scale is not None:
            kw['scale'] = scale
        return self.S.op(eng, lambda e: e.activation(out=out, in_=in_, func=func, **kw), reads, writes)

    def tt(self, out, in0, in1, op, reads, writes, eng='dve'):
        return self.S.op(eng, lambda e: e.tensor_tensor(out=out, in0=in0, in1=in1, op=op), reads, writes)

    def ts(self, out, in0, s1, s2, op0, op1, reads, writes, eng='dve'):
        if op1 is None:
            return self.S.op(eng, lambda e: e.tensor_scalar(out=out, in0=in0, scalar1=s1, scalar2=None, op0=op0), reads, writes)
        return self.S.op(eng, lambda e: e.tensor_scalar(out=out, in0=in0, scalar1=s1, scalar2=s2, op0=op0, op1=op1), reads, writes)

    def stt(self, out, in0, scalar, in1, op0, op1, reads, writes, eng='dve'):
        return self.S.op(eng, lambda e: e.scalar_tensor_tensor(out=out, in0=in0, scalar=scalar, in1=in1, op0=op0, op1=op1), reads, writes)

    def cp(self, out, in_, reads, writes, eng='dve'):
        if eng == 'act':
            return self.S.op('act', lambda e: e.copy(out=out, in_=in_), reads, writes)
        return self.S.op(eng, lambda e: e.tensor_copy(out=out, in_=in_), reads, writes)

    def memset(self, ap, val, writes, eng='pool'):
        return self.S.op(eng, lambda e: e.memset(ap, val), (), writes)

    def recip(self, out, in_, reads, writes):
        return self.S.op('dve', lambda e: e.reciprocal(out=out, in_=in_), reads, writes)

    def load(self, dst, src, key, q='sp'):
        return self.S.dma(q, dst, src, writes=[key])

    def load_tok(self, dst, view, t0, T, key):
        if not self.dyn_tok:
            return self.load(dst, view[:, :, t0:t0 + T], key)
        half = self.dyn_half

        def in_fn(e):
            if getattr(self, '_rbase', None) is None:
                self._rbase = e.snap((e.partition_id() % 2) * half, min_val=0, max_val=half)
            return view[:, :, bass.ds(self._rbase + t0, T)]
        return self.S.dma('sp', dst, None, writes=[key], in_fn=in_fn)

    def store(self, dst, src, srckey, dstkey, q='pool', final=True):
        self.S.dma(q, dst, src, reads=[srckey], writes=[dstkey], semkey=srckey)
        if final:
            self.outkeys.append(dstkey)

    def finish(self, force=False):
        if self.fused and not force:
            return None
        self.S.wait_keys('pool', self.outkeys)
        self.S.emit()
        self.ctx.close()
        return self.nc


def emit_modvec(kb, cT, ckey, nv, wl, bl, nblk, out, okey):
    for b in range(nblk):
        wb, wk = kb.wload(wl[b], KC * 128)
        ps, pk = kb.psum()
        for k in range(KC):
            kb.mm(ps[:, 0:nv], wb[:, k * 128:(k + 1) * 128], cT[:, k, :], k == 0, k == KC - 1, [wk, ckey], [pk])
        kb.ts(out[:, b, :], ps[:, 0:nv], bl[:, b:b + 1], None, ALU.add, None, [pk, 'modb'], [okey])


def emit_norm(kb, X, xkey, A, akey, T, ones, gain, shift, gkey, sq, rstd):
    for k in range(KC):
        kb.act(sq[:, k, :T], X[:, k, :T], AF.Square, [xkey], [('sq', k)])
    ps, pk = kb.psum()
    for k in range(KC):
        kb.mm(ps[:, :T], ones[:, :], sq[:, k, :T], k == 0, k == KC - 1, ['ones', ('sq', k)], [pk])
    kb.act(rstd[:, :T], ps[:, :T], AF.Sqrt, [pk, 'eps'], ['rstd'], bias=kb.eps_ap, scale=1.0 / D)
    kb.recip(rstd[:, :T], rstd[:, :T], ['rstd'], ['rstd'])
    for k in range(KC):
        kb.tt(sq[:, k, :T], X[:, k, :T], rstd[:, :T], ALU.mult, [xkey, 'rstd', ('sq', k)], [('sq', k)], eng='dve' if k % 2 == 0 else 'pool')
        kb.act(A[:, k, :T], sq[:, k, :T], AF.Identity, [('sq', k), gkey], [akey], bias=shift[:, k, 0:1], scale=gain[:, k, 0:1])


def emit_linear(kb, wl, nblk, kc, rhs_fn, rkeys, T, evac):
    for b in range(nblk):
        wb, wk = kb.wload(wl[b], kc * 128)
        ps, pk = kb.psum()
        for k in range(kc):
            kb.mm(ps[:, :T], wb[:, k * 128:(k + 1) * 128], rhs_fn(k), k == 0, k == kc - 1, [wk] + rkeys, [pk])
        evac(b, ps, pk)


def emit_ffn(kb, A, akey, H, hkey, G, T, w1l, w3l, w2l, g2, gkey, tmp):
    for j in range(FC):
        w1, k1 = kb.wload(w1l[j], KC * 128)
        w3, k3 = kb.wload(w3l[j], KC * 128)
        p1, pk1 = kb.psum()
        p3, pk3 = kb.psum()
        for k in range(KC):
            kb.mm(p1[:, :T], w1[:, k * 128:(k + 1) * 128], A[:, k, :T], k == 0, k == KC - 1, [k1, akey], [pk1])
        for k in range(KC):
            kb.mm(p3[:, :T], w3[:, k * 128:(k + 1) * 128], A[:, k, :T], k == 0, k == KC - 1, [k3, akey], [pk3])
        tk = ('tmp', j % 2)
        kb.act(tmp[:, j % 2, :T], p1[:, :T], AF.Silu, [pk1], [tk])
        kb.tt(G[:, j, :T], tmp[:, j % 2, :T], p3[:, :T], ALU.mult, [tk, pk3], [('G', j)])
    for i in range(KC):
        w2, k2 = kb.wload(w2l[i], FC * 128)
        ps, pk = kb.psum()
        for j in range(FC):
            kb.mm(ps[:, :T], w2[:, j * 128:(j + 1) * 128], G[:, j, :T], j == 0, j == FC - 1, [k2, ('G', j)], [pk])
        kb.stt(H[:, i, :T], ps[:, :T], g2[:, i, 0:1], H[:, i, :T], ALU.mult, ALU.add, [pk, gkey, hkey], [hkey])


TT = 512


def build_post(ntok, last, kb=None):
    kb = kb or KB()
    nmb = 32 if last else 48
    HT = kb.inp('HT', [D, ntok]); YT = kb.inp('YT', [D, ntok])
    wo = kb.inp('wo', [KC, 128, D]); cTd = kb.inp('cT', [128, KC])
    modw = kb.inp('modw', [nmb, 128, D]); modb = kb.inp('modb', [128, nmb]); nw = kb.inp('nw', [128, KC, 2])
    w1l = kb.inp('w1', [FC, 128, D]); w3l = kb.inp('w3', [FC, 128, D]); w2l = kb.inp('w2', [KC, 128, DFF])
    HO = kb.outp('HO', [D, ntok])
    if not last:
        wn = kb.inp('wn', [24, 128, D]); UT = kb.outp('UT', [3 * D, ntok])
    kb.init_psum(8)
    kb.init_wpool(4, DFF)
    ones = kb.sb('ones', [128, 128]); epst = kb.sb('epst', [128, 1])
    kb.memset(ones[:], 1.0, ['ones']); kb.memset(epst[:], EPS, ['eps'])
    kb.eps_ap = epst[:, 0:1]
    cT = kb.sb('cTs', [128, KC, 1]); modbs = kb.sb('modbs', [128, nmb]); nws = kb.sb('nws', [128, KC, 2])
    mod = kb.sb('mod', [128, nmb, 1])
    kb.load(cT[:, :, 0], cTd[:, :], 'cT'); kb.load(modbs[:], modb[:, :], 'modb'); kb.load(nws[:], nw[:, :, :], 'nw')
    kb.act(cT[:, :, 0], cT[:, :, 0], AF.Silu, ['cT'], ['cT'])
    emit_modvec(kb, cT, 'cT', 1, modw, modbs, nmb, mod, 'mod')
    gains = kb.sb('gains', [128, KC, 2])
    kb.stt(gains[:, :, 0:1], mod[:, 16:24, :], 1.0, nws[:, :, 0:1], ALU.add, ALU.mult, ['mod', 'nw'], ['gains'])
    if not last:
        kb.stt(gains[:, :, 1:2], mod[:, 40:48, :], 1.0, nws[:, :, 1:2], ALU.add, ALU.mult, ['mod', 'nw', 'gains'], ['gains'])
    else:
        kb.cp(gains[:, :, 1:2], nws[:, :, 1:2], ['nw', 'gains'], ['gains'])
    zshift = kb.sb('zshift', [128, KC, 1])
    kb.memset(zshift[:], 0.0, ['zshift'])
    H = kb.sb('H', [128, KC, TT]); Y = kb.sb('Y', [128, KC, TT]); A = kb.sb('A', [128, KC, TT]); sq = kb.sb('sq', [128, KC, TT])
    rstd = kb.sb('rstd', [128, TT]); G = kb.sb('G', [128, FC, TT]); tmp = kb.sb('tmp', [128, 2, TT])
    HTv = HT.rearrange("(k p) t -> p k t", p=128); YTv = YT.rearrange("(k p) t -> p k t", p=128)
    HOv = HO.rearrange("(k p) t -> p k t", p=128)
    if not last:
        UTv = UT.rearrange("(k p) t -> p k t", p=128)
    for ti in range(ntok // TT):
        tsl = slice(ti * TT, (ti + 1) * TT)
        kb.load_tok(H[:], HTv, ti * TT, TT, 'H'); kb.load_tok(Y[:], YTv, ti * TT, TT, 'Y')

        def ev_o(b, ps, pk):
            kb.stt(H[:, b, :], ps[:, :TT], mod[:, b, 0:1], H[:, b, :], ALU.mult, ALU.add, [pk, 'mod', 'H'], ['H'])
        emit_linear(kb, wo, KC, KC, lambda k: Y[:, k, :], ['Y'], TT, ev_o)
        emit_norm(kb, H, 'H', A, 'A', TT, ones, gains[:, :, 0:1], mod[:, 8:16, :], 'gains', sq, rstd)
        emit_ffn(kb, A, 'A', H, 'H', G, TT, w1l, w3l, w2l, mod[:, 24:32, :], 'mod', tmp)
        if not last:
            kb.store(HOv[:, :, tsl], H[:], 'H', ('HO', ti))
            emit_norm(kb, H, 'H', A, 'A', TT, ones, gains[:, :, 1:2], mod[:, 32:40, :], 'gains', sq, rstd)

            def ev_u(b, ps, pk):
                kb.cp(tmp[:, b % 2, :], ps[:, :TT], [pk], [('tmp', b % 2)], eng='act' if b % 2 else 'dve')
                kb.store(UTv[:, b, tsl], tmp[:, b % 2, :], ('tmp', b % 2), ('UT', ti, b))
            emit_linear(kb, wn, 24, KC, lambda k: A[:, k, :], ['A'], TT, ev_u)
        else:
            emit_norm(kb, H, 'H', A, 'A', TT, ones, gains[:, :, 1:2], zshift, 'gains', sq, rstd)
            kb.store(HOv[:, :, tsl], A[:], 'A', ('HO', ti))
    return kb.finish()


def blk_w(W, kc):
    K, N = W.shape
    nb = N // 128
    return np.ascontiguousarray(W.reshape(kc, 128, nb, 128).transpose(2, 1, 0, 3).reshape(nb, 128, kc * 128))


def fm_vec(v):
    return np.ascontiguousarray(v.reshape(-1, 128).T)


CH = 64
NCTX = 4
NLAT = 128
NG = NCTX + NLAT
NEG = -30000.0
AB_SIZES = (512, 512, 512, 512, 8, 8, 256, 256, 512, 512, 32)


def mixa_consts():
    i = np.arange(64)
    P, Fq = i[:, None], i[None, :]
    tri = [(P <= Fq).astype(np.float32), (P >= Fq).astype(np.float32)]
    ident = np.eye(64, dtype=np.float32)
    v1 = [(P >= Fq), (P <= Fq)]
    v2 = [(Fq >= P), (Fq <= P)]
    s1 = [(P > Fq), (P < Fq)]
    s2 = [(Fq > P), (Fq < P)]
    c = {}
    dd = [0, 0, 1, 1]
    c['TRIS'] = np.stack([tri[d] for d in dd], 1)
    c['ID4'] = np.stack([ident for d in dd], 1)
    c['NEG1'] = np.stack([np.where(v1[d], 0.0, NEG) for d in dd], 1).astype(np.float32)
    c['NEG2'] = np.stack([np.where(v2[d], 0.0, NEG) for d in dd], 1).astype(np.float32)
    c['ST1'] = np.stack([s1[d].astype(np.float32) for d in dd], 1)
    c['ST2'] = np.stack([s2[d].astype(np.float32) for d in dd], 1)
    c['INC2'] = np.stack([v2[d].astype(np.float32) for d in dd], 1)
    names = ['TRIS', 'ID4', 'NEG1', 'NEG2', 'ST1', 'ST2', 'INC2']
    return np.ascontiguousarray(np.concatenate([c[n] for n in names], 1)), names


def build_mixa(stage=99, ntiles=99, kb=None):
    kb = kb or KB()
    S = kb.S
    L = 8192
    LC = 256
    XT = kb.inp('XT', [D, L]); CXT = kb.inp('CXT', [D, LC])
    cTd = kb.inp('cT', [128, KC, 2]); modw = kb.inp('modw', [16, 128, D]); modb = kb.inp('modb', [128, 16]); nw = kb.inp('nw', [128, KC])
    wl = kb.inp('wl', [18, 128, D]); wg = kb.inp('wg', [128, KC, 8]); taps = kb.inp('taps', [128, 6, 3])
    gconst = kb.inp('gconst', [64, 2, 32])
    gwb = kb.inp('gwb', [17, 2, 128]); hnw = kb.inp('hnw', [128, 2]); cm = kb.inp('cm', [64, 28, 64]); identd = kb.inp('ident', [128, 128])
    YT = kb.outp('YT', [512, L])
    ATl = kb.scratch('ATl', [D, L + 2]); ATc = kb.scratch('ATc', [D, LC + 2])
    G_U0 = kb.scratch('G_U0', [NG, 64, 4, 128]); G_KC = kb.scratch('G_KC', [NG, 128, 4, 64]); G_KD = kb.scratch('G_KD', [NG, 64, 4, 128])
    G_AT = kb.scratch('G_AT', [NG, 64, 4, 64]); G_QT = kb.scratch('G_QT', [NG, 128, 2, 64])
    L_KD = kb.scratch('L_KD', [NG, 64, 4, 64]); L_V = kb.scratch('L_V', [NG, 64, 2, 128]); L_AT = kb.scratch('L_AT', [NG, 64, 4, 64]); L_QD = kb.scratch('L_QD', [NG, 64, 4, 64])
    OG = kb.scratch('OG', [NLAT, 64, 4, 128]); OL = kb.scratch('OL', [NLAT, 64, 4, 128])
    ZT = kb.scratch('ZT', [4, 128, L])
    kb.init_psum(8)
    kb.init_wpool(3, D)
    ones = kb.sb('ones', [128, 128]); epst = kb.sb('epst', [128, 1]); ident = kb.sb('ident_s', [128, 128])
    kb.memset(ones[:], 1.0, ['ones']); kb.memset(epst[:], EPS, ['eps']); kb.eps_ap = epst[:, 0:1]
    kb.load(ident[:], identd[:, :], 'ident')
    cms = kb.sb('cms', [64, 28, 64]); kb.load(cms[:], cm[:, :, :], 'cm')
    TRIS, ID4, NEG1, NEG2, ST1, ST2, INC2 = [cms[:, 4 * i:4 * i + 4, :] for i in range(7)]
    cT = kb.sb('cTs', [128, KC, 2]); modbs = kb.sb('modbs', [128, 16]); nws = kb.sb('nws', [128, KC, 1]); mod = kb.sb('mod', [128, 16, 2])
    wgs = kb.sb('wgs', [128, KC, 8]); tapss = kb.sb('tapss', [128, 6, 3]); gcs = kb.sb('gcs', [64, 2, 32]); gwbs = kb.sb('gwbs', [17, 2, 128]); hnws = kb.sb('hnws', [128, 2])
    kb.load(cT[:], cTd[:, :, :], 'cT'); kb.load(modbs[:], modb[:, :], 'modb'); kb.load(nws[:, :, 0], nw[:, :], 'nw')
    kb.load(wgs[:], wg[:, :, :], 'wg'); kb.load(tapss[:], taps[:, :, :], 'taps'); kb.load(gcs[:], gconst[:, :, :], 'gcs'); kb.load(gwbs[:], gwb[:, :, :], 'gwb'); kb.load(hnws[:], hnw[:, :], 'hnw')
    kb.act(cT[:], cT[:], AF.Silu, ['cT'], ['cT'])
    emit_modvec(kb, cT, 'cT', 2, modw, modbs, 16, mod, 'mod')
    gains = kb.sb('gains', [128, KC, 2])
    for v in range(2):
        kb.stt(gains[:, :, v:v + 1], mod[:, 8:16, v:v + 1], 1.0, nws[:, :, 0:1], ALU.add, ALU.mult, ['mod', 'nw', 'gains'], ['gains'])
    negA = kb.sb('negA', [64, 32])
    kb.act(negA[:], gcs[:, 0, :], AF.Exp, ['gcs'], ['negA'])
    kb.ts(negA[:], negA[:], -1.0, None, ALU.mult, None, ['negA'], ['negA'])
    GGL = kb.sb('GGL', [128, NG, 4]); GEGC = kb.sb('GEGC', [64, NG, 4]); LGL = kb.sb('LGL', [64, NG, 4])
    kb.mixa_aoff_scan = kb.aoff
    arena = kb.sb('arena', [128, 12800])
    X = arena[:, 0:4096].rearrange("p (k t) -> p k t", k=KC); A = arena[:, 4096:8192].rearrange("p (k t) -> p k t", k=KC)
    sq = arena[:, 8192:12288].rearrange("p (k t) -> p k t", k=KC); rstd = arena[:, 12288:12800]
    zt = kb.sb('zt', [128, KC, 1]); kb.memset(zt[:], 0.0, ['zt'])
    ATlv = ATl.rearrange("(k p) t -> p k t", p=128); ATcv = ATc.rearrange("(k p) t -> p k t", p=128)
    XTv = XT.rearrange("(k p) t -> p k t", p=128); CXTv = CXT.rearrange("(k p) t -> p k t", p=128)
    for (dst, n) in ((ATlv, L), (ATcv, LC)):
        kb.store(dst[:, :, 0:1], zt[:], 'zt', ('ATpad', n, 0), final=False)
        kb.store(dst[:, :, n + 1:n + 2], zt[:], 'zt', ('ATpad', n, 1), final=False)
    tiles = [('c', 0, LC, 0)] + [('l', i * TT, TT, NCTX + i * 8) for i in range(L // TT)]
    for (sq_, t0, T, g0) in tiles:
        src = CXTv if sq_ == 'c' else XTv
        dst = ATcv if sq_ == 'c' else ATlv
        v = 1 if sq_ == 'c' else 0
        kb.load(X[:, :, :T], src[:, :, t0:t0 + T], 'X')
        emit_norm(kb, X, 'X', A, 'A', T, ones, gains[:, :, v:v + 1], mod[:, 0:8, v:v + 1], 'gains', sq, rstd)
        kb.store(dst[:, :, 1 + t0:1 + t0 + T], A[:, :, :T], 'A', ('AT', sq_, t0), final=False)
    atkeys = [('AT', s_, t0) for (s_, t0, T, g0) in tiles] + [('ATpad', n, i) for n in (L, LC) for i in range(2)]
    if stage < 2:
        return kb.finish()
    tiles = tiles[:ntiles]
    S.barrier()
    AH = arena[:, 0:KC * (TT + 2)].rearrange("p (k t) -> p k t", k=KC)
    Ue = kb.sb('Ue', [128, TT + 2]); cv = kb.sb('cv', [128, TT]); sqb = kb.sb('sqb', [128, TT]); rs = kb.sb('rs', [128, TT])
    qT = kb.sb('qT', [128, 2, TT]); kT = kb.sb('kT', [128, 2, TT]); vT = kb.sb('vT', [128, TT])
    k_tm = arena[0:64, 4112:6160].rearrange("p (c h f) -> p c h f", c=8, h=2); v_tm = arena[0:64, 6160:8208].rearrange("p (c h f) -> p c h f", c=8, h=2)
    qlT = kb.sb('qlT', [64, 2, TT]); klT = kb.sb('klT', [64, 2, TT]); kl_tm = kb.sb('kl_tm', [64, 8, 2, 64]); vl_tm = arena[0:64, 8208:10256].rearrange("p (c h f) -> p c h f", c=8, h=2)
    llrT = kb.sb('llrT', [17, 2, TT]); kb.memset(llrT[:], 1.0, ['llrT'])
    zs = kb.sb('zs', [128, 2, TT])
    g_tm = kb.sb('g_tm', [64, 8, 4]); b_tm = kb.sb('b_tm', [64, 8, 4]); gx = kb.sb('gx', [64, 8, 4])
    la_tm = arena[0:64, 10256:12304].rearrange("p (c h f) -> p c h f", c=8, h=4)
    def mk_lane(tag):
        sb = lambda n, shp: kb.sb(n + tag, shp)
        d = dict(tag=tag)
        d['RGB'] = sb('RGB', [64, 8, 64]); d['gcc'] = sb('gcc', [64, 4]); d['t1'] = sb('t1', [64, 4, 64]); d['t2'] = sb('t2', [64, 4, 64])
        d['E1'] = sb('E1', [64, 4, 64]); d['E2'] = sb('E2', [64, 4, 64]); d['Wm'] = sb('Wm', [64, 4, 64]); d['WmT'] = sb('WmT', [64, 4, 64])
        d['Pb'] = [sb('Pb%d' % i, [64, 4, 64]) for i in range(2)]; d['PTb'] = [sb('PTb%d' % i, [64, 4, 64]) for i in range(2)]; d['XTb'] = [sb('XTb%d' % i, [64, 4, 64]) for i in range(2)]
        d['attT'] = sb('attT', [64, 4, 64]); d['egc'] = sb('egc', [64, 4]); d['bege'] = sb('bege', [64, 4]); d['ekd'] = sb('ekd', [64, 4])
        d['VB'] = sb('VB', [64, 4, 128]); d['KBG'] = sb('KBG', [64, 4, 128]); d['KDg'] = sb('KDg', [64, 4, 128]); d['U0s'] = sb('U0s', [64, 4, 128]); d['KCs'] = sb('KCs', [128, 4, 64])
        d['refc'] = sb('refc', [64, 4]); d['dlt'] = sb('dlt', [64, 4, 64]); d['Ea'] = sb('Ea', [64, 4, 64]); d['Eb'] = sb('Eb', [64, 4, 64]); d['Ec'] = sb('Ec', [64, 4, 64])
        d['QQ'] = sb('QQ', [64, 4, 64]); d['KKl'] = sb('KKl', [64, 4, 64]); d['QDl'] = sb('QDl', [64, 4, 64]); d['ATl_'] = sb('ATTl', [64, 4, 64]); d['bct'] = sb('bct', [64, 4, 64]); d['KDl'] = sb('KDl', [64, 4, 64])
        return d
    prep_lanes = [mk_lane('L0'), mk_lane('L1')]

    def bc_last(ap, n):
        return ap.unsqueeze(2).to_broadcast([ap.shape[0], ap.shape[1], n])

    def l2norm(dst, T, qscale):
        kb.act(sqb[:, :T], dst, AF.Square, ['qkv'], ['sqb'])
        ps, pk = kb.psum()
        kb.mm(ps[:, :T], ones[:, :], sqb[:, :T], True, True, ['ones', 'sqb'], [pk])
        kb.act(rs[:, :T], ps[:, :T], AF.Sqrt, [pk, 'eps'], ['rs'], bias=kb.eps_ap, scale=1.0)
        kb.recip(rs[:, :T], rs[:, :T], ['rs'], ['rs'])
        kb.stt(dst, dst, qscale, rs[:, :T], ALU.mult, ALU.mult, ['qkv', 'rs'], ['qkv'])

    def transp(src_fn, nch, K_, M_, dst_fn, keys_r, key_w):
        per = 512 // M_
        for c0 in range(0, nch, per):
            ps, pk = kb.psum()
            n = min(per, nch - c0)
            for c in range(c0, c0 + n):
                kb.mm(ps[0:64, (c - c0) * M_:(c - c0 + 1) * M_], src_fn(c), ident[0:K_, 0:M_], True, True, keys_r + ['ident'], [pk])
            for c in range(c0, c0 + n):
                kb.cp(dst_fn(c), ps[0:64, (c - c0) * M_:(c - c0 + 1) * M_], [pk], [key_w], eng='act' if c % 2 else 'dve')

    for (sq_, t0, T, g0) in tiles:
        nch = T // CH
        src = ATcv if sq_ == 'c' else ATlv
        S.dma('sp', AH[:, :, 0:T + 2], src[:, :, t0:t0 + T + 2], reads=atkeys, writes=['AH'])

        def proj(b, M_, T=T):
            wb, wk = kb.wload(wl[b], D)
            ps, pk = kb.psum()
            for k in range(KC):
                kb.mm(ps[0:M_, :T], wb[:, k * 128:k * 128 + M_], AH[:, k, 1:T + 1], k == 0, k == KC - 1, [wk, 'AH'], [pk])
            return wb, wk, ps, pk
        for b in range(6):
            wb, wk, ps, pk = proj(b, 128)
            ph, phk = kb.psum()
            for k in range(KC):
                kb.mm(ph[:, 0:2], wb[:, k * 128:(k + 1) * 128], AH[:, k, 0:T + 2:T + 1], k == 0, k == KC - 1, [wk, 'AH'], [phk])
            kb.cp(Ue[:, 1:T + 1], ps[:, :T], [pk], ['Ue'], eng='act')
            kb.cp(Ue[:, 0:T + 2:T + 1], ph[:, 0:2], [phk, 'Ue'], ['Ue'])
            kb.ts(cv[:, :T], Ue[:, 1:T + 1], tapss[:, b, 1:2], None, ALU.mult, None, ['Ue', 'taps'], ['cv'])
            kb.stt(cv[:, :T], Ue[:, 0:T], tapss[:, b, 0:1], cv[:, :T], ALU.mult, ALU.add, ['Ue', 'taps', 'cv'], ['cv'])
            kb.stt(cv[:, :T], Ue[:, 2:T + 2], tapss[:, b, 2:3], cv[:, :T], ALU.mult, ALU.add, ['Ue', 'taps', 'cv'], ['cv'])
            seg, hl = b // 2, b % 2
            dst = (qT[:, hl, :T], kT[:, hl, :T], vT[:, :T])[seg]
            kb.act(dst, cv[:, :T], AF.Silu, ['cv'], ['qkv'])
            if seg < 2:
                l2norm(dst, T, 128 ** -0.5 if seg == 0 else 1.0)
            if seg == 0:
                kb.store(G_QT[g0:g0 + nch, :, hl, :].rearrange("g d i -> d g i"), qT[:, hl, :T].rearrange("d (g i) -> d g i", i=CH), 'qkv', ('G_QT', g0, hl), final=False)
            if seg == 1:
                transp(lambda c, hl=hl: kT[:, hl, c * CH:(c + 1) * CH], nch, 128, 128, lambda c, hl=hl: k_tm[:, c, hl, :], ['qkv'], 'k_tm')
            if seg == 2:
                transp(lambda c: vT[:, c * CH:(c + 1) * CH], nch, 128, 128, lambda c, hl=hl: v_tm[:, c, hl, :], ['qkv'], 'v_tm')
        for bi, b in enumerate((6, 7, 14, 15)):
            wb, wk, ps, pk = proj(b, 128)
            kb.act(zs[:, bi % 2, :T], ps[:, :T], AF.Silu, [pk], [('zs', bi % 2)])
            if sq_ == 'l':
                kb.store(ZT[bi, :, t0:t0 + T], zs[:, bi % 2, :T], ('zs', bi % 2), ('ZT', bi, t0), final=False)
        for b in (8, 9, 10, 11):
            wb, wk, ps, pk = proj(b, 64)
            hl = b % 2
            if b < 10:
                kb.ts(qlT[:, hl, :T], ps[0:64, :T], 64 ** -0.5, None, ALU.mult, None, [pk], ['qlT'])
            else:
                kb.cp(klT[:, hl, :T], ps[0:64, :T], [pk], ['klT'], eng='act')
                transp(lambda c, hl=hl: klT[:, hl, c * CH:(c + 1) * CH], nch, 64, 64, lambda c, hl=hl: kl_tm[:, c, hl, :], ['klT'], 'kl_tm')
        for b in (12, 13):
            wb, wk, ps, pk = proj(b, 128)
            hl = b % 2
            kb.cp(vT[:, :T], ps[:, :T], [pk, 'qkv'], ['qkv'], eng='act')
            transp(lambda c: vT[:, c * CH:(c + 1) * CH], nch, 128, 128, lambda c, hl=hl: vl_tm[:, c, hl, :], ['qkv'], 'vl_tm')
        for d in range(2):
            wb, wk, ps, pk = proj(16 + d, 16)
            kb.cp(llrT[0:16, d, :T], ps[0:16, :T], [pk, 'llrT'], ['llrT'])
        psg, pgk = kb.psum()
        for c in range(nch):
            for k in range(KC):
                kb.mm(psg[0:64, c * 8:(c + 1) * 8], AH[:, k, 1 + c * CH:1 + (c + 1) * CH], wgs[:, k, :], k == 0, k == KC - 1, ['AH', 'wg'], [pgk])
        psgv = psg[0:64, 0:nch * 8].rearrange("p (c e) -> p c e", e=8)
        gcv = gcs[:, 1, :].rearrange("p (c e) -> p c e", e=4)
        kb.tt(gx[:, :nch, :], psgv[:, :, 0:4], gcv[:, :nch, :], ALU.add, [pgk, 'gcs'], ['gx'])
        kb.act(gx[:, :nch, :], gx[:, :nch, :], AF.Exp, ['gx'], ['gx'])
        kb.act(gx[:, :nch, :], gx[:, :nch, :], AF.Ln, ['gx'], ['gx'], bias=1.0)
        kb.tt(g_tm[:, :nch, :], gx[:, :nch, :], negA[:].rearrange("p (c e) -> p c e", e=4)[:, :nch, :], ALU.mult, ['gx', 'negA'], ['g_tm'])
        kb.act(b_tm[:, :nch, :], psgv[:, :, 4:8], AF.Sigmoid, [pgk], ['b_tm'])
        for d in range(2):
            for c0 in range(0, nch, 4):
                ps, pk = kb.psum()
                n = min(4, nch - c0)
                for c in range(c0, c0 + n):
                    kb.mm(ps[0:64, (c - c0) * 128:(c - c0 + 1) * 128], llrT[0:17, d, c * CH:(c + 1) * CH], gwbs[0:17, d, :], True, True, ['llrT', 'gwb'], [pk])
                dstv = la_tm[:, c0:c0 + n, 2 * d:2 * d + 2, :]
                kb.act(dstv, ps[0:64, 0:n * 128].rearrange("p (c h f) -> p c h f", h=2, f=64), AF.Exp, [pk, 'la_tm'], ['la_tm'], scale=-1.0)
                kb.act(dstv, dstv, AF.Ln, ['la_tm'], ['la_tm'], bias=1.0)
                kb.ts(dstv, dstv, -1.0 / 16.0, None, ALU.mult, None, ['la_tm'], ['la_tm'])
        def prep(c, LN, g0=g0):
            g = g0 + c
            sfx = LN['tag']
            RGB, gcc, t1, t2, E1, E2, Wm, WmT = LN['RGB'], LN['gcc'], LN['t1'], LN['t2'], LN['E1'], LN['E2'], LN['Wm'], LN['WmT']
            Pb, PTb, XTb, attT, egc, bege, ekd = LN['Pb'], LN['PTb'], LN['XTb'], LN['attT'], LN['egc'], LN['bege'], LN['ekd']
            VB, KBG, KDg, U0s, KCs = LN['VB'], LN['KBG'], LN['KDg'], LN['U0s'], LN['KCs']
            refc, dlt, Ea, Eb, Ec, QQ, KKl, QDl, ATl_, bct, KDl = LN['refc'], LN['dlt'], LN['Ea'], LN['Eb'], LN['Ec'], LN['QQ'], LN['KKl'], LN['QDl'], LN['ATl_'], LN['bct'], LN['KDl']
            csl = slice(c * CH, (c + 1) * CH)
            kb.tt(RGB[:, 0:4, :], TRIS, bc_last(g_tm[:, c, :], 64), ALU.mult, ['cm', 'g_tm'], [sfx + 'RGB'])
            kb.tt(RGB[:, 4:8, :], ID4, bc_last(b_tm[:, c, :], 64), ALU.mult, ['cm', 'b_tm', sfx + 'RGB'], [sfx + 'RGB'], eng='pool')
            psR, pRk = kb.psum()
            kb.mm(psR[0:64, :], ones[0:64, 0:64], RGB[:].rearrange("p a b -> p (a b)"), True, True, ['ones', sfx + 'RGB'], [pRk])
            R = psR[0:64, 0:256].rearrange("p (a b) -> p a b", b=64); Bb = psR[0:64, 256:512].rearrange("p (a b) -> p a b", b=64)
            psC, pCk = kb.psum()
            for d in range(2):
                kb.mm(psC[0:64, 2 * d:2 * d + 2], cms[:, 2 * d, :], g_tm[:, c, 2 * d:2 * d + 2], True, True, ['cm', 'g_tm'], [pCk])
            kb.mm(psC[0:128, 4:8], ones[0:64, 0:128], g_tm[:, c, :], True, True, ['ones', 'g_tm'], [pCk])
            kb.cp(gcc[:], psC[0:64, 0:4], [pCk], [sfx + 'gcc'], eng='act')
            kb.stt(t1[:], R, -1.0, NEG1, ALU.mult, ALU.add, [pRk, 'cm'], [sfx + 't1'])
            kb.tt(t1[:], t1[:], bc_last(gcc[:], 64), ALU.add, [sfx + 't1', sfx + 'gcc'], [sfx + 't1'])
            kb.act(E1[:], t1[:], AF.Exp, [sfx + 't1'], [sfx + 'E1'])
            kb.tt(t2[:], R, NEG2, ALU.add, [pRk, 'cm'], [sfx + 't2'])
            kb.tt(t2[:], t2[:], bc_last(gcc[:], 64), ALU.subtract, [sfx + 't2', sfx + 'gcc'], [sfx + 't2'])
            kb.act(E2[:], t2[:], AF.Exp, [sfx + 't2'], [sfx + 'E2'])
            kb.tt(Wm[:], E1[:], bc_last(b_tm[:, c, :], 64), ALU.mult, [sfx + 'E1', 'b_tm'], [sfx + 'Wm'])
            kb.tt(Wm[:], Wm[:], ST1, ALU.mult, [sfx + 'Wm', 'cm'], [sfx + 'Wm'], eng='pool')
            kb.tt(WmT[:], Bb, E2[:], ALU.mult, [pRk, sfx + 'E2'], [sfx + 'WmT'])
            kb.tt(WmT[:], WmT[:], ST2, ALU.mult, [sfx + 'WmT', 'cm'], [sfx + 'WmT'], eng='pool')
            kb.act(egc[:], gcc[:], AF.Exp, [sfx + 'gcc'], [sfx + 'egc'])
            kb.tt(bege[:], egc[:], b_tm[:, c, :], ALU.mult, [sfx + 'egc', 'b_tm'], [sfx + 'bege'])
            kb.tt(ekd[:], psC[0:64, 4:8], gcc[:], ALU.subtract, [pCk, sfx + 'gcc'], [sfx + 'ekd'])
            kb.act(ekd[:], ekd[:], AF.Exp, [sfx + 'ekd'], [sfx + 'ekd'])
            kb.act(GGL[:, g, :], psC[0:128, 4:8], AF.Exp, [pCk], ['GGL'])
            kb.cp(GEGC[:, g, :], egc[:], [sfx + 'egc'], ['GEGC'], eng='pool')
            yield
            psK, pKk = kb.psum()
            for hl in range(2):
                kb.mm(psK[0:64, hl * 64:(hl + 1) * 64], kT[:, hl, csl], kT[:, hl, csl], True, True, ['qkv'], [pKk])
                kb.mm(psK[0:64, 128 + hl * 64:128 + (hl + 1) * 64], kT[:, hl, csl], qT[:, hl, csl], True, True, ['qkv'], [pKk])
            KK = psK[0:64, 0:128].rearrange("p (a b) -> p a b", b=64); QK = psK[0:64, 128:256].rearrange("p (a b) -> p a b", b=64)
            P, PT, XTm = Pb[0], PTb[0], XTb[0]
            for d in range(2):
                kb.tt(P[:, 2 * d:2 * d + 2, :], KK, Wm[:, 2 * d:2 * d + 2, :], ALU.mult, [pKk, sfx + 'Wm', sfx + 'P0'], [sfx + 'P0'])
                kb.tt(PT[:, 2 * d:2 * d + 2, :], KK, WmT[:, 2 * d:2 * d + 2, :], ALU.mult, [pKk, sfx + 'WmT', sfx + 'PT0'], [sfx + 'PT0'])
                kb.tt(attT[:, 2 * d:2 * d + 2, :], QK, E2[:, 2 * d:2 * d + 2, :], ALU.mult, [pKk, sfx + 'E2', sfx + 'attT'], [sfx + 'attT'])
            kb.tt(XTm[:], ID4, PT[:], ALU.subtract, ['cm', sfx + 'PT0', sfx + 'X0'], [sfx + 'X0'])
            yield
            cur = 0
            for lvl in range(5):
                nxt = 1 - cur
                psP, pPk = kb.psum()
                for i in range(4):
                    kb.mm(psP[0:64, i * 64:(i + 1) * 64], PTb[cur][:, i, :], Pb[cur][:, i, :], True, True, [sfx + 'P%d' % cur, sfx + 'PT%d' % cur], [pPk])
                    if lvl < 4:
                        kb.mm(psP[0:64, 256 + i * 64:256 + (i + 1) * 64], Pb[cur][:, i, :], PTb[cur][:, i, :], True, True, [sfx + 'P%d' % cur, sfx + 'PT%d' % cur], [pPk])
                kb.cp(Pb[nxt][:].rearrange("p a b -> p (a b)"), psP[0:64, 0:256], [pPk, sfx + 'P%d' % nxt], [sfx + 'P%d' % nxt], eng='act')
                if lvl < 4:
                    kb.cp(PTb[nxt][:].rearrange("p a b -> p (a b)"), psP[0:64, 256:512], [pPk, sfx + 'PT%d' % nxt], [sfx + 'PT%d' % nxt])
                psX, pXk = kb.psum()
                for i in range(4):
                    kb.mm(psX[0:64, i * 64:(i + 1) * 64], Pb[nxt][:, i, :], XTb[cur][:, i, :], True, True, [sfx + 'P%d' % nxt, sfx + 'X%d' % cur], [pXk])
                kb.tt(XTb[nxt][:].rearrange("p a b -> p (a b)"), psX[0:64, 0:256], XTb[cur][:].rearrange("p a b -> p (a b)"), ALU.add, [pXk, sfx + 'X%d' % cur, sfx + 'X%d' % nxt], [sfx + 'X%d' % nxt])
                cur = nxt
                yield
            XTf, xk = XTb[cur], sfx + 'X%d' % cur
            for d in range(2):
                dsl = slice(2 * d, 2 * d + 2)
                kb.tt(VB[:, dsl, :], v_tm[:, c, :, :], bc_last(b_tm[:, c, dsl], 128), ALU.mult, ['v_tm', 'b_tm', sfx + 'VB'], [sfx + 'VB'], eng='pool')
                kb.tt(KBG[:, dsl, :], k_tm[:, c, :, :], bc_last(bege[:, dsl], 128), ALU.mult, ['k_tm', sfx + 'bege', sfx + 'KBG'], [sfx + 'KBG'])
                kb.tt(KDg[:, dsl, :], k_tm[:, c, :, :], bc_last(ekd[:, dsl], 128), ALU.mult, ['k_tm', sfx + 'ekd', sfx + 'KDg'], [sfx + 'KDg'], eng='pool')
            psU, pUk = kb.psum()
            psKc, pKck = kb.psum()
            for i in range(4):
                kb.mm(psU[0:64, i * 128:(i + 1) * 128], XTf[:, i, :], VB[:, i, :], True, True, [xk, sfx + 'VB'], [pUk])
                kb.mm(psKc[0:128, i * 64:(i + 1) * 64], KBG[:, i, :], XTf[:, i, :], True, True, [xk, sfx + 'KBG'], [pKck])
            kb.cp(U0s[:].rearrange("p a b -> p (a b)"), psU[0:64, :], [pUk, sfx + 'U0s'], [sfx + 'U0s'], eng='act')
            kb.cp(KCs[:].rearrange("p a b -> p (a b)"), psKc[0:128, 0:256], [pKck, sfx + 'KCs'], [sfx + 'KCs'])
            kb.store(G_U0[g], U0s[:], sfx + 'U0s', ('G_U0', g), final=False)
            kb.store(G_KC[g], KCs[:], sfx + 'KCs', ('G_KC', g), final=False)
            kb.store(G_KD[g], KDg[:], sfx + 'KDg', ('G_KD', g), final=False)
            kb.store(G_AT[g], attT[:], sfx + 'attT', ('G_AT', g), final=False)
            yield
            LA = la_tm[:, c, :, :]
            psB, pBk = kb.psum()
            for i in range(4):
                kb.mm(psB[0:64, i * 64:(i + 1) * 64], LA[:, i, :], cms[:, 2 * (i // 2), :], True, True, ['la_tm', 'cm'], [pBk])
            for d in range(2):
                kb.mm(psB[0:64, 256 + d * 128:256 + (d + 1) * 128], cms[:, 2 * d, :], LA[:, 2 * d:2 * d + 2, :].rearrange("p a b -> p (a b)"), True, True, ['la_tm', 'cm'], [pBk])
            psT, pTk = kb.psum()
            kb.mm(psT[0:64, 0:256], ones[0:64, 0:64], LA.rearrange("p a b -> p (a b)"), True, True, ['la_tm', 'ones'], [pTk])
            for i in range(4):
                kb.mm(psT[0:64, 256 + i:257 + i], LA[:, i, :], ones[0:64, 0:1], True, True, ['la_tm', 'ones'], [pTk])
            bcT = psB[0:64, 0:256].rearrange("p (a b) -> p a b", b=64)
            for d in range(2):
                ridx = 32 if d == 0 else 31
                kb.cp(refc[:, 2 * d:2 * d + 2], bcT[:, 2 * d:2 * d + 2, ridx], [pBk, sfx + 'refc'], [sfx + 'refc'])
            kb.tt(dlt[:], bcT, bc_last(refc[:], 64), ALU.subtract, [pBk, sfx + 'refc'], [sfx + 'dlt'])
            kb.act(Ea[:], dlt[:], AF.Exp, [sfx + 'dlt'], [sfx + 'Ea'])
            kb.act(Eb[:], dlt[:], AF.Exp, [sfx + 'dlt'], [sfx + 'Eb'], scale=-1.0)
            kb.act(Ec[:], bcT, AF.Exp, [pBk], [sfx + 'Ec'])
            kb.cp(bct[:].rearrange("p a b -> p (a b)"), psB[0:64, 256:512], [pBk], [sfx + 'bct'], eng='act')
            kb.tt(bct[:].rearrange("p a b -> p (a b)"), psT[0:64, 0:256], bct[:].rearrange("p a b -> p (a b)"), ALU.subtract, [pTk, sfx + 'bct'], [sfx + 'bct'])
            kb.act(bct[:], bct[:], AF.Exp, [sfx + 'bct'], [sfx + 'bct'])
            kb.act(LGL[:, g, :], psT[0:64, 256:260], AF.Exp, [pTk], ['LGL'])
            yield
            for d in range(2):
                dsl = slice(2 * d, 2 * d + 2)
                kb.tt(QQ[:, dsl, :], qlT[:, :, csl], Ea[:, dsl, :], ALU.mult, ['qlT', sfx + 'Ea', sfx + 'QQ'], [sfx + 'QQ'])
                kb.tt(KKl[:, dsl, :], klT[:, :, csl], Eb[:, dsl, :], ALU.mult, ['klT', sfx + 'Eb', sfx + 'KKl'], [sfx + 'KKl'], eng='pool')
                kb.tt(QDl[:, dsl, :], qlT[:, :, csl], Ec[:, dsl, :], ALU.mult, ['qlT', sfx + 'Ec', sfx + 'QDl'], [sfx + 'QDl'])
            psA, pAk = kb.psum()
            for i in range(4):
                kb.mm(psA[0:64, i * 64:(i + 1) * 64], KKl[:, i, :], QQ[:, i, :], True, True, [sfx + 'KKl', sfx + 'QQ'], [pAk])
            kb.tt(ATl_[:], psA[0:64, 0:256].rearrange("p (a b) -> p a b", b=64), INC2, ALU.mult, [pAk, 'cm'], [sfx + 'ATTl'])
            yield
            for d in range(2):
                dsl = slice(2 * d, 2 * d + 2)
                kb.tt(KDl[:, dsl, :], kl_tm[:, c, :, :], bct[:, dsl, :], ALU.mult, ['kl_tm', sfx + 'bct', sfx + 'KDl'], [sfx + 'KDl'])
            kb.store(L_KD[g], KDl[:], sfx + 'KDl', ('L_KD', g), final=False)
            kb.store(L_AT[g], ATl_[:], sfx + 'ATTl', ('L_AT', g), final=False)
            kb.store(L_QD[g], QDl[:], sfx + 'QDl', ('L_QD', g), final=False)
            kb.store(L_V[g], vl_tm[:, c, :, :], 'vl_tm', ('L_V', g), final=False)
        if stage >= 3:
            for c0 in range(0, nch, 2):
                gens = [prep(c0, prep_lanes[0]), prep(c0 + 1, prep_lanes[1])]
                while gens:
                    for gnr in list(gens):
                        try:
                            next(gnr)
                        except StopIteration:
                            gens.remove(gnr)
    kb.mixa_state = dict(G_U0=G_U0, G_KC=G_KC, G_KD=G_KD, G_AT=G_AT, G_QT=G_QT, L_KD=L_KD, L_V=L_V, L_AT=L_AT, L_QD=L_QD, OG=OG, OL=OL, ZT=ZT,
                         GGL=GGL, GEGC=GEGC, LGL=LGL, YT=YT, ident=ident, hnws=hnws, tiles=tiles,
                         g0of=(lambda g: 0 if g < NCTX else NCTX + ((g - NCTX) // 8) * 8))
    if stage >= 4:
        emit_mixa_scan(kb)
    return kb.finish()


def emit_mixa_scan(kb):
    st = kb.mixa_state
    S = kb.S
    G_U0, G_KC, G_KD, G_AT, G_QT = st['G_U0'], st['G_KC'], st['G_KD'], st['G_AT'], st['G_QT']
    L_KD, L_V, L_AT, L_QD, OG, OL, ZT = st['L_KD'], st['L_V'], st['L_AT'], st['L_QD'], st['OG'], st['OL'], st['ZT']
    GGL, GEGC, LGL, YT, ident, hnws = st['GGL'], st['GEGC'], st['LGL'], st['YT'], st['ident'], st['hnws']
    if kb.arena is not None:
        S.barrier()
        kb.aoff = kb.mixa_aoff_scan
    Sg = kb.sb('Sg', [128, 4, 128]); Sl = kb.sb('Sl', [64, 4, 128])
    kb.memset(Sg[:], 0.0, [('Sg', i) for i in range(4)]); kb.memset(Sl[:], 0.0, [('Sl', i) for i in range(4)])
    NB = 2
    U0b = [[kb.sb('U0b%d%d' % (d, s), [64, 2, 128]) for s in range(NB)] for d in range(2)]
    KCb = [[kb.sb('KCb%d%d' % (d, s), [128, 2, 64]) for s in range(NB)] for d in range(2)]
    KDb = [[kb.sb('KDb%d%d' % (d, s), [64, 2, 128]) for s in range(NB)] for d in range(2)]
    ATb = [[kb.sb('ATb%d%d' % (d, s), [64, 2, 64]) for s in range(NB)] for d in range(2)]
    QTb = [[kb.sb('QTb%d%d' % (d, s), [128, 2, 64]) for s in range(NB)] for d in range(2)]
    LKDb = [[kb.sb('LKDb%d%d' % (d, s), [64, 2, 64]) for s in range(NB)] for d in range(2)]
    LVb = [[kb.sb('LVb%d%d' % (d, s), [64, 2, 128]) for s in range(NB)] for d in range(2)]
    LATb = [[kb.sb('LATb%d%d' % (d, s), [64, 2, 64]) for s in range(NB)] for d in range(2)]
    LQDb = [[kb.sb('LQDb%d%d' % (d, s), [64, 2, 64]) for s in range(NB)] for d in range(2)]
    ub = [kb.sb('ub%d' % d, [64, 2, 128]) for d in range(2)]
    qs = [kb.sb('qs%d' % d, [64, 2, 128]) for d in range(2)]
    ob = [kb.sb('ob%d' % d, [64, 2, 128]) for d in range(2)]
    obl = [kb.sb('obl%d' % d, [64, 2, 128]) for d in range(2)]
    order = [list(range(NG)), list(range(NCTX - 1, -1, -1)) + list(range(NG - 1, NCTX - 1, -1))]
    for s in range(NG):
        slot = s % NB
        for d in range(2):
            g = order[d][s]
            emit = g >= NCTX
            dsl = slice(2 * d, 2 * d + 2)
            gk = ('gop', d, slot); lk = ('lop', d, slot)
            S.dma('sp', U0b[d][slot][:], G_U0[g][:, dsl, :], reads=[('G_U0', g)], writes=[gk])
            S.dma('sp', KCb[d][slot][:], G_KC[g][:, dsl, :], reads=[('G_KC', g)], writes=[gk])
            S.dma('sp', KDb[d][slot][:], G_KD[g][:, dsl, :], reads=[('G_KD', g)], writes=[gk])
            if emit:
                S.dma('sp', ATb[d][slot][:], G_AT[g][:, dsl, :], reads=[('G_AT', g)], writes=[gk])
                S.dma('sp', QTb[d][slot][:], G_QT[g], reads=[('G_QT', st['g0of'](g), 0), ('G_QT', st['g0of'](g), 1)], writes=[gk])
            S.dma('sp', LKDb[d][slot][:], L_KD[g][:, dsl, :], reads=[('L_KD', g)], writes=[lk])
            S.dma('sp', LVb[d][slot][:], L_V[g], reads=[('L_V', g)], writes=[lk])
            if emit:
                S.dma('sp', LATb[d][slot][:], L_AT[g][:, dsl, :], reads=[('L_AT', g)], writes=[lk])
                S.dma('sp', LQDb[d][slot][:], L_QD[g][:, dsl, :], reads=[('L_QD', g)], writes=[lk])
            ps1, p1k = kb.psum()
            for hl in range(2):
                i = 2 * d + hl
                kb.mm(ps1[0:64, hl * 128:(hl + 1) * 128], KCb[d][slot][:, hl, :], Sg[:, i, :], True, True, [gk, ('Sg', i)], [p1k])
            if emit:
                pso, pok = kb.psum()
                for hl in range(2):
                    i = 2 * d + hl
                    kb.mm(pso[0:64, hl * 128:(hl + 1) * 128], QTb[d][slot][:, hl, :], Sg[:, i, :], True, True, [gk, ('Sg', i)], [pok])
            kb.tt(ub[d][:].rearrange("p a b -> p (a b)"), U0b[d][slot][:].rearrange("p a b -> p (a b)"), ps1[0:64, 0:256], ALU.subtract, [gk, p1k, ('ub', d)], [('ub', d)])
            if emit:
                for hl in range(2):
                    i = 2 * d + hl
                    kb.ts(qs[d][:, hl, :], pso[0:64, hl * 128:(hl + 1) * 128], GEGC[:, g, i:i + 1], None, ALU.mult, None, [pok, 'GEGC', ('qs', d)], [('qs', d)], eng='pool' if False else 'dve')
                pso2, po2k = kb.psum()
                for hl in range(2):
                    kb.mm(pso2[0:64, hl * 128:(hl + 1) * 128], ATb[d][slot][:, hl, :], ub[d][:, hl, :], True, True, [gk, ('ub', d)], [po2k])
                kb.tt(ob[d][:].rearrange("p a b -> p (a b)"), qs[d][:].rearrange("p a b -> p (a b)"), pso2[0:64, 0:256], ALU.add, [('qs', d), po2k, ('ob', d)], [('ob', d)])
                kb.store(OG[g - NCTX][:, dsl, :], ob[d][:], ('ob', d), ('OG', g, d), final=False)
            pss, psk = kb.psum()
            for hl in range(2):
                kb.mm(pss[0:128, hl * 128:(hl + 1) * 128], KDb[d][slot][:, hl, :], ub[d][:, hl, :], True, True, [gk, ('ub', d)], [psk])
            for hl in range(2):
                i = 2 * d + hl
                kb.stt(Sg[:, i, :], Sg[:, i, :], GGL[:, g, i:i + 1], pss[0:128, hl * 128:(hl + 1) * 128], ALU.mult, ALU.add, [psk, 'GGL', ('Sg', i)], [('Sg', i)])
            if emit:
                pso, pok = kb.psum()
                for hl in range(2):
                    i = 2 * d + hl
                    kb.mm(pso[0:64, hl * 128:(hl + 1) * 128], LQDb[d][slot][:, hl, :], Sl[:, i, :], True, False, [lk, ('Sl', i)], [pok])
                    kb.mm(pso[0:64, hl * 128:(hl + 1) * 128], LATb[d][slot][:, hl, :], LVb[d][slot][:, hl, :], False, True, [lk], [pok])
                kb.cp(obl[d][:].rearrange("p a b -> p (a b)"), pso[0:64, 0:256], [pok, ('obl', d)], [('obl', d)], eng='act')
                kb.store(OL[g - NCTX][:, dsl, :], obl[d][:], ('obl', d), ('OL', g, d), final=False)
            pss, psk = kb.psum()
            for hl in range(2):
                kb.mm(pss[0:64, hl * 128:(hl + 1) * 128], LKDb[d][slot][:, hl, :], LVb[d][slot][:, hl, :], True, True, [lk], [psk])
            for hl in range(2):
                i = 2 * d + hl
                kb.stt(Sl[:, i, :], Sl[:, i, :], LGL[:, g, i:i + 1], pss[0:64, hl * 128:(hl + 1) * 128], ALU.mult, ALU.add, [psk, 'LGL', ('Sl', i)], [('Sl', i)])
    Ob = kb.sb('Ob', [128, 4, 128]); osum = kb.sb('osum', [128, 2, 128]); osq = kb.sb('osq', [128, 2, 128]); ss = kb.sb('ss', [128, 2])
    Zb = kb.sb('Zb', [128, 2, 128]); Yb = kb.sb('Yb', [128, 2, 128])
    for which, (OD, zoff, yoff, ncol) in enumerate(((OG, 0, kb.yoffs[0], 0), (OL, 2, kb.yoffs[1], 1))):
        nm = 'OG' if which == 0 else 'OL'
        for tg in range(NLAT // 2):
            toks = slice(tg * 128, (tg + 1) * 128)
            rk = [(nm, NCTX + 2 * tg + cc, d) for cc in range(2) for d in range(2)]
            S.dma('sp', Ob[:], OD[2 * tg:2 * tg + 2].rearrange("c p i e -> (c p) i e"), reads=rk, writes=['Ob'])
            S.dma('sp', Zb[:], ZT[zoff:zoff + 2, :, toks].rearrange("h e t -> e h t"), reads=[('ZT', zoff + h, (tg * 128 // TT) * TT) for h in range(2)], writes=['Zb'])
            kb.tt(osum[:], Ob[:, 0:2, :], Ob[:, 2:4, :], ALU.add, ['Ob', 'osum'], ['osum'])
            kb.tt(osq[:], osum[:], osum[:], ALU.mult, ['osum', 'osq'], ['osq'], eng='pool')
            S.op('dve', lambda e: e.tensor_reduce(out=ss[:], in_=osq[:], axis=AX.X, op=ALU.add), ['osq', 'ss'], ['ss'])
            kb.act(ss[:], ss[:], AF.Sqrt, ['ss', 'eps'], ['ss'], bias=kb.eps_ap, scale=1.0 / 128)
            kb.recip(ss[:], ss[:], ['ss'], ['ss'])
            kb.tt(osum[:], osum[:], ss[:].unsqueeze(2).to_broadcast([128, 2, 128]), ALU.mult, ['osum', 'ss'], ['osum'])
            psY, pYk = kb.psum()
            for hl in range(2):
                kb.mm(psY[0:128, hl * 128:(hl + 1) * 128], osum[:, hl, :], ident[:, :], True, True, ['osum', 'ident'], [pYk])
            kb.stt(Yb[:].rearrange("p a b -> p (a b)"), psY[0:128, 0:256], hnws[:, ncol:ncol + 1], Zb[:].rearrange("p a b -> p (a b)"), ALU.mult, ALU.mult, [pYk, 'hnw', 'Zb', 'Yb'], ['Yb'])
            kb.store(YT[yoff:yoff + 256, toks].rearrange("(h e) t -> e h t", e=128), Yb[:], 'Yb', ('YT', which, tg))


def mixa_inputs(inp, b, hh):
    e = 0
    W = inp['ab_w_in'][e]
    offs = np.concatenate([[0], np.cumsum(AB_SIZES)])

    def cols(seg, start, n):
        return list(range(offs[seg] + start, offs[seg] + start + n))
    blocks = []
    for seg in (0, 1, 2):
        for hl in range(2):
            blocks.append(cols(seg, (2 * hh + hl) * 128, 128))
    for hl in range(2):
        blocks.append(cols(3, (2 * hh + hl) * 128, 128))
    for seg in (6, 7):
        for hl in range(2):
            blocks.append(cols(seg, (2 * hh + hl) * 64, 64))
    for seg in (8, 9):
        for hl in range(2):
            blocks.append(cols(seg, (2 * hh + hl) * 128, 128))
    for d in range(2):
        blocks.append(cols(10, d * 16, 16))
    Wp = np.zeros((D, 18 * 128), np.float32)
    for bi, cl in enumerate(blocks):
        Wp[:, bi * 128:bi * 128 + len(cl)] = W[:, cl]
    gcols = [offs[4] + d * 4 + 2 * hh + hl for d in range(2) for hl in range(2)] + [offs[5] + d * 4 + 2 * hh + hl for d in range(2) for hl in range(2)]
    wg = np.ascontiguousarray(W[:, gcols].reshape(KC, 128, 8).transpose(1, 0, 2))
    cw = inp['ab_conv_w'][e]
    taps = np.zeros((128, 6, 3), np.float32)
    for seg in range(3):
        for hl in range(2):
            ch0 = seg * 512 + (2 * hh + hl) * 128
            taps[:, seg * 2 + hl, :] = cw[:, ch0:ch0 + 128].T
    al = np.array([inp['gdn_a_log'][e][d, 2 * hh + hl] for d in range(2) for hl in range(2)], np.float32)
    dtb = np.array([inp['gdn_dt_bias'][e][d, 2 * hh + hl] for d in range(2) for hl in range(2)], np.float32)
    gconst = np.zeros((64, 2, 32), np.float32)
    gconst[:, 0, :] = np.tile(al, 8)[None, :]
    gconst[:, 1, :] = np.tile(dtb, 8)[None, :]
    gwb = np.zeros((17, 2, 128), np.float32)
    hs = slice(2 * hh * 64, (2 * hh + 2) * 64)
    for d in range(2):
        gwb[0:16, d, :] = inp['gla_gate_w'][e][d][:, hs]
        gwb[16, d, :] = inp['gla_gate_b'][e][d][hs]
    cm, _ = mixa_consts()
    return dict(
        XT=np.ascontiguousarray(inp['x'][b].T), CXT=np.ascontiguousarray(inp['ctx'][b].T),
        cT=np.ascontiguousarray(np.stack([fm_vec(inp['c'][b]), fm_vec(inp['c_ctx'])], axis=-1)),
        modw=blk_w(np.ascontiguousarray(inp['mod_w'][0][:, 0:2048]), 8), modb=fm_vec(inp['mod_b'][0][0:2048]), nw=fm_vec(inp['norm1_w'][0]),
        wl=blk_w(Wp, 8), wg=wg, taps=taps, gconst=gconst, gwb=gwb,
        hnw=np.ascontiguousarray(np.stack([inp['gdn_norm_w'][e], inp['gla_norm_w'][e]], axis=-1)), cm=cm, ident=np.eye(128, dtype=np.float32))


LSEQ = 8192
NFFT = 16384
CG = 8
MAGIC = 12582912.0


def hy_consts():
    n = np.arange(128, dtype=np.float64)
    th = 2 * np.pi * np.outer(n, n) / 128.0
    Fre, Fim = np.cos(th), -np.sin(th)
    tw = 2 * np.pi * np.outer(n, n) / NFFT
    Tre, Tim = np.cos(tw), -np.sin(tw)
    F64c = np.zeros((128, 256)); F64c[:64, :128] = Fre[:64]; F64c[:64, 128:] = Fim[:64]
    G1 = np.concatenate([Fre, -Fim], 1); G2 = np.concatenate([Fim, Fre], 1)
    fc = np.concatenate([F64c, Fre, Fim, G1, G2, Tre, Tim], 1).astype(np.float32)
    l = LSEQ
    t = np.linspace(0.0, 1.0, l, dtype=np.float32)[:, None]
    w = (np.float32(2.0 * math.pi / l) * np.arange(l, dtype=np.float32))[:, None]
    f = np.linspace(1e-4, 15, 16, dtype=np.float32)[None, :]
    zp = np.concatenate([t, np.cos(f * w), -np.sin(f * w)], axis=-1).astype(np.float32).T
    deltas = np.abs(np.linspace(math.log(1e-2) / 1.5, math.log(1e-2) / 0.3, D, dtype=np.float32))
    win = (np.exp(-t * deltas[None, :]) + np.float32(0.05)).astype(np.float32).T
    return np.ascontiguousarray(fc), np.ascontiguousarray(zp), np.ascontiguousarray(win)


def build_hyc(stage=99, ngroups=None, nb=4, nblk=1, kb=None):
    kb = kb or KB()
    S = kb.S
    L = LSEQ
    NCH = nblk * 128
    if ngroups is None:
        ngroups = NCH // CG
    UT3 = kb.inp('UT3', [3, NCH, nb, L]); tapsd = kb.inp('taps', [128, nblk, 3, 3]); zpd = kb.inp('zp', [33, L]); wind = kb.inp('win', [NCH, L])
    w1d = kb.inp('w1', [33, 64]); w23d = kb.inp('w23', [64, 2, 64]); bfrd = kb.inp('bfr', [64, 4]); fod = kb.inp('fo', [64, 4, NCH]); skipd = kb.inp('skip', [128, nblk, 2])
    fcd = kb.inp('fc', [128, 1280])
    Z2T = kb.outp('Z2T', [NCH, nb, L])
    UC = kb.scratch('UC', [3, nb, NCH, L]); HFs = kb.scratch('HFs', [4, NCH, L])
    kb.init_psum(8)
    fc = kb.sb('fc_s', [128, 1280]); kb.load(fc[:], fcd[:, :], 'fc')
    F64c = fc[0:64, 0:256]; F_re = fc[:, 256:384]; F_im = fc[:, 384:512]; G1 = fc[:, 512:768]; G2 = fc[:, 768:1024]; T_re = fc[:, 1024:1152]; T_im = fc[:, 1152:1280]
    taps = kb.sb('taps_s', [128, nblk, 3, 3]); w1 = kb.sb('w1s', [33, 64]); w23 = kb.sb('w23s', [64, 2, 64]); bfr = kb.sb('bfrs', [64, 4]); fo = kb.sb('fos', [64, 4, NCH]); skip = kb.sb('skips', [128, nblk, 2])
    kb.load(taps[:], tapsd[:, :, :, :], 'taps'); kb.load(w1[:], w1d[:, :], 'w1'); kb.load(w23[:], w23d[:, :, :], 'w23'); kb.load(bfr[:], bfrd[:, :], 'bfr'); kb.load(fo[:], fod[:, :, :], 'fo'); kb.load(skip[:], skipd[:, :, :], 'skip')
    frb = kb.sb('frb', [64, 3])
    kb.tt(frb[:], bfr[:, 0:3], bfr[:, 3:4].to_broadcast([64, 3]), ALU.mult, ['bfr'], ['frb'])
    aoff_p3 = kb.aoff
    CW = 2048
    Uin = kb.sb('Uin', [128, CW + 2]); Uout = kb.sb('Uout', [128, CW])
    uckeys = []
    for seg in range(3):
        for b in range(nb):
            for cb in range(nblk):
                chs = slice(cb * 128, (cb + 1) * 128)
                for t0 in range(0, L, CW):
                    lo = max(t0 - 1, 0); hi = min(t0 + CW + 1, L)
                    if t0 == 0:
                        kb.memset(Uin[:, 0:1], 0.0, ['Uin'], eng='dve')
                    if t0 + CW == L:
                        kb.memset(Uin[:, CW + 1:CW + 2], 0.0, ['Uin'], eng='dve')
                    S.dma('sp', Uin[:, lo - (t0 - 1):hi - (t0 - 1)], UT3[seg, chs, b, lo:hi], writes=['Uin'])
                    kb.ts(Uout[:], Uin[:, 1:CW + 1], taps[:, cb, seg, 1:2], None, ALU.mult, None, ['Uin', 'taps', 'Uout'], ['Uout'])
                    kb.stt(Uout[:], Uin[:, 0:CW], taps[:, cb, seg, 0:1], Uout[:], ALU.mult, ALU.add, ['Uin', 'taps', 'Uout'], ['Uout'])
                    kb.stt(Uout[:], Uin[:, 2:CW + 2], taps[:, cb, seg, 2:3], Uout[:], ALU.mult, ALU.add, ['Uin', 'taps', 'Uout'], ['Uout'])
                    kb.store(UC[seg, b, chs, t0:t0 + CW], Uout[:], 'Uout', ('UC', seg, b, cb, t0), final=False)
                    uckeys.append(('UC', seg, b, cb, t0))
    zp = kb.sb('zp_s', [33, TT]); aa = kb.sb('aa', [64, TT]); tq = kb.sb('tq', [64, TT]); hd = [kb.sb('hd%d' % i, [64, TT]) for i in range(2)]
    hft = kb.sb('hft', [128, 2, TT]); wt = kb.sb('wt', [128, TT])

    def sin_layer(ps, pk, li, out, okey):
        kb.act(aa[:], ps[0:64, :TT], AF.Identity, [pk, 'bfr', 'frb'], ['aa'], bias=frb[:, li:li + 1], scale=bfr[:, 3:4])
        kb.ts(tq[:], aa[:], 1.0 / (2 * math.pi), MAGIC, ALU.mult, ALU.add, ['aa'], ['tq'])
        kb.ts(tq[:], tq[:], MAGIC, -2 * math.pi, ALU.subtract, ALU.mult, ['tq'], ['tq'])
        kb.tt(aa[:], aa[:], tq[:], ALU.add, ['aa', 'tq'], ['aa'])
        kb.ts(aa[:], aa[:], 3.141592, -3.141592, ALU.min, ALU.max, ['aa'], ['aa'])
        kb.act(out, aa[:], AF.Sin, ['aa'], [okey])
    hfkeys = []
    for ti in range(L // TT):
        tsl = slice(ti * TT, (ti + 1) * TT)
        kb.load(zp[:], zpd[:, tsl], 'zp')
        ps, pk = kb.psum()
        kb.mm(ps[0:64, :TT], w1[:, :], zp[:, :], True, True, ['w1', 'zp'], [pk])
        sin_layer(ps, pk, 0, hd[0][:], 'hd0')
        ps, pk = kb.psum()
        kb.mm(ps[0:64, :TT], w23[:, 0, :], hd[0][:], True, True, ['w23', 'hd0'], [pk])
        sin_layer(ps, pk, 1, hd[1][:], 'hd1')
        ps, pk = kb.psum()
        kb.mm(ps[0:64, :TT], w23[:, 1, :], hd[1][:], True, True, ['w23', 'hd1'], [pk])
        sin_layer(ps, pk, 2, hd[0][:], 'hd0')
        for cb in range(nblk):
            chs = slice(cb * 128, (cb + 1) * 128)
            kb.load(wt[:], wind[chs, tsl], 'wt')
            for f in range(4):
                ps, pk = kb.psum()
                kb.mm(ps[:, :TT], fo[:, f, chs], hd[0][:], True, True, ['fo', 'hd0'], [pk])
                hk = ('hft', f % 2)
                kb.tt(hft[:, f % 2, :], ps[:, :TT], wt[:], ALU.mult, [pk, 'wt', hk], [hk])
                if ti == 0:
                    if f < 2:
                        kb.tt(hft[:, f % 2, 0:1], hft[:, f % 2, 0:1], skip[:, cb, f:f + 1], ALU.add, [hk, 'skip'], [hk])
                    else:
                        kb.memset(hft[:, f % 2, 0:1], 0.0, [hk], eng='dve')
                kb.store(HFs[f, chs, tsl], hft[:, f % 2, :], hk, ('HFs', f, cb, ti), final=False)
                hfkeys.append(('HFs', f, cb, ti))
    S.barrier()
    kb.aoff = aoff_p3
    Tre_b = T_re.unsqueeze(1).to_broadcast([128, CG, 128]); Tim_b = T_im.unsqueeze(1).to_broadcast([128, CG, 128])

    def mk_lane(tag):
        return dict(tag=tag, As=kb.sb('As' + tag, [128, CG, 2, 128]), Bt=kb.sb('Bt' + tag, [128, CG, 3, 128]),
                    tw=[kb.sb('tw%d%s' % (i, tag), [128, CG, 128]) for i in range(4)])
    LF, LD = mk_lane('F'), mk_lane('D')
    Xg = kb.sb('Xg', [64, 3, CG, 128]); Ys = kb.sb('Ys', [128, CG, 2, 128])
    Hx = [kb.sb('Hx%d' % i, [64, CG, 128]) for i in range(2)]
    Htmp = kb.sb('Htmp', [128, CG, 2, 128])
    KSp = [kb.sb('KSp%d' % i, [128, 2, CG, 2, 128]) for i in range(2)]

    def fwd_fft(ln, Xv, xkey, dst, dkey):
        t = ln['tag']; As = ln['As']; Bt = ln['Bt']; tw = ln['tw']
        ak, bk = 'As' + t, 'Bt' + t
        tk = ['tw%d%s' % (i, t) for i in range(4)]
        for c0 in range(0, CG, 2):
            ps, pk = kb.psum()
            for c in (c0, c0 + 1):
                kb.mm(ps[:, (c - c0) * 256:(c - c0 + 1) * 256], Xv[:, c, :], F64c, True, True, [xkey, 'fc'], [pk])
            kb.cp(As[:, c0:c0 + 2, :, :].rearrange("p a b c -> p (a b c)"), ps[:, :], [pk, ak], [ak], eng='act')
            yield
        Are, Aim = As[:, :, 0, :], As[:, :, 1, :]
        kb.tt(tw[0][:], Are, Tre_b, ALU.mult, [ak, 'fc', tk[0]], [tk[0]])
        kb.tt(tw[1][:], Aim, Tim_b, ALU.mult, [ak, 'fc', tk[1]], [tk[1]], eng='pool')
        kb.tt(tw[2][:], Are, Tim_b, ALU.mult, [ak, 'fc', tk[2]], [tk[2]], eng='pool')
        kb.tt(tw[3][:], Aim, Tre_b, ALU.mult, [ak, 'fc', tk[3]], [tk[3]])
        yield
        kb.tt(Bt[:, :, 1, :], tw[0][:], tw[1][:], ALU.subtract, [tk[0], tk[1], bk], [bk])
        kb.tt(Bt[:, :, 2, :], tw[2][:], tw[3][:], ALU.add, [tk[2], tk[3], bk], [bk], eng='pool')
        yield
        kb.ts(Bt[:, :, 0, :], Bt[:, :, 2, :], -1.0, None, ALU.mult, None, [bk], [bk])
        yield
        for c0 in range(0, CG, 2):
            ps, pk = kb.psum()
            o = ps[:, :].rearrange("p (c x) -> p c x", c=2)
            kb.mm(o, F_re, Bt[:, c0:c0 + 2, 1:3, :].rearrange("p c a b -> p c (a b)"), True, False, [bk, 'fc'], [pk])
            kb.mm(o, F_im, Bt[:, c0:c0 + 2, 0:2, :].rearrange("p c a b -> p c (a b)"), False, True, [bk, 'fc'], [pk])
            kb.cp(dst[:, c0:c0 + 2, :, :].rearrange("p a b c -> p (a b c)"), ps[:, :], [pk, dkey], [dkey], eng='act')
            yield

    def conv(ln, zidx, gidx, Kt, kkey, order):
        t = ln['tag']; As = ln['As']; Bt = ln['Bt']; tw = ln['tw']
        ak, bk = 'As' + t, 'Bt' + t
        tk = ['tw%d%s' % (i, t) for i in range(4)]
        yield from fwd_fft(ln, Xg[:, zidx, :, :], ('Xg', zidx), As, ak)
        Xre, Xim = As[:, :, 0, :], As[:, :, 1, :]
        Kre, Kim = Kt[:, order, :, 0, :], Kt[:, order, :, 1, :]
        kb.tt(tw[0][:], Xre, Kre, ALU.mult, [ak, kkey, tk[0]], [tk[0]])
        kb.tt(tw[1][:], Xim, Kim, ALU.mult, [ak, kkey, tk[1]], [tk[1]], eng='pool')
        kb.tt(tw[2][:], Xre, Kim, ALU.mult, [ak, kkey, tk[2]], [tk[2]], eng='pool')
        kb.tt(tw[3][:], Xim, Kre, ALU.mult, [ak, kkey, tk[3]], [tk[3]])
        yield
        kb.tt(Ys[:, :, 0, :], tw[0][:], tw[1][:], ALU.subtract, [tk[0], tk[1], 'Ys'], ['Ys'])
        kb.tt(Ys[:, :, 1, :], tw[2][:], tw[3][:], ALU.add, [tk[2], tk[3], 'Ys'], ['Ys'], eng='pool')
        yield
        Cs = As
        for c0 in range(0, CG, 2):
            ps, pk = kb.psum()
            for c in (c0, c0 + 1):
                o = ps[:, (c - c0) * 256:(c - c0 + 1) * 256]
                kb.mm(o, Ys[:, c, 0, :], G1, True, False, ['Ys', 'fc'], [pk])
                kb.mm(o, Ys[:, c, 1, :], G2, False, True, ['Ys', 'fc'], [pk])
            kb.cp(Cs[:, c0:c0 + 2, :, :].rearrange("p a b c -> p (a b c)"), ps[:, :], [pk, ak], [ak], eng='act')
            yield
        Cre, Cim = Cs[:, :, 0, :], Cs[:, :, 1, :]
        Cp = Bt
        kb.tt(tw[0][:], Cre, Tre_b, ALU.mult, [ak, 'fc', tk[0]], [tk[0]])
        kb.tt(tw[1][:], Cim, Tim_b, ALU.mult, [ak, 'fc', tk[1]], [tk[1]], eng='pool')
        kb.tt(tw[2][:], Cim, Tre_b, ALU.mult, [ak, 'fc', tk[2]], [tk[2]], eng='pool')
        kb.tt(tw[3][:], Cre, Tim_b, ALU.mult, [ak, 'fc', tk[3]], [tk[3]])
        yield
        kb.tt(Cp[:, :, 0, :], tw[0][:], tw[1][:], ALU.add, [tk[0], tk[1], bk], [bk])
        kb.tt(Cp[:, :, 1, :], tw[2][:], tw[3][:], ALU.subtract, [tk[2], tk[3], bk], [bk], eng='pool')
        yield
        for c0 in range(0, CG, 4):
            ps, pk = kb.psum()
            o = ps[0:64, :].rearrange("p (c x) -> p c x", c=4)
            kb.mm(o, F_re[:, 0:64], Cp[:, c0:c0 + 4, 0, :], True, False, [bk, 'fc'], [pk])
            kb.mm(o, F_im[:, 0:64], Cp[:, c0:c0 + 4, 1, :], False, True, [bk, 'fc'], [pk])
            kb.tt(Xg[:, zidx, c0:c0 + 4, :].rearrange("p a b -> p (a b)"), ps[0:64, :], Xg[:, gidx, c0:c0 + 4, :].rearrange("p a b -> p (a b)"), ALU.mult, [pk, ('Xg', gidx), ('Xg', zidx)], [('Xg', zidx)])
            yield

    def filter_lane(gi):
        ch = slice(gi * CG, (gi + 1) * CG)
        Kt = KSp[gi % 2]; kkey = ('KSp', gi % 2)
        for order in range(2):
            for di in range(2):
                f = di * 2 + order
                hb = Hx[f % 2]; hk = ('Hx', f % 2)
                S.dma('sp', hb[:], HFs[f, ch, :].rearrange("c (a b) -> a c b", b=128), reads=hfkeys, writes=[hk])
                if di == 0:
                    yield from fwd_fft(LF, hb, hk, Kt[:, order, :, :, :], kkey)
                else:
                    yield from fwd_fft(LF, hb, hk, Htmp, 'Htmp')
                    kb.tt(Kt[:, order, :, 0, :], Kt[:, order, :, 0, :], Htmp[:, :, 0, :], ALU.add, [kkey, 'Htmp'], [kkey])
                    kb.tt(Kt[:, order, :, 1, :], Kt[:, order, :, 1, :], Htmp[:, :, 1, :], ALU.subtract, [kkey, 'Htmp'], [kkey], eng='pool')
                    yield
        kb.ts(Kt[:].rearrange("p a b c d -> p (a b c d)"), Kt[:].rearrange("p a b c d -> p (a b c d)"), 1.0 / NFFT, None, ALU.mult, None, [kkey], [kkey])
        yield

    def data_lane(gi):
        ch = slice(gi * CG, (gi + 1) * CG)
        Kt = KSp[gi % 2]; kkey = ('KSp', gi % 2)
        for b in range(nb):
            for seg in range(3):
                S.dma('sp', Xg[:, seg, :, :], UC[seg, b, ch, :].rearrange("c (a b) -> a c b", b=128), reads=uckeys, writes=[('Xg', seg)])
            yield from conv(LD, 0, 1, Kt, kkey, 0)
            yield from conv(LD, 0, 2, Kt, kkey, 1)
            kb.store(Z2T[ch, b, :].rearrange("c (a b) -> a c b", b=128), Xg[:, 0, :, :], ('Xg', 0), ('Z2T', gi, b))
            yield

    for step in range(ngroups + 1):
        lanes = []
        if step < ngroups:
            lanes.append(filter_lane(step))
        if step >= 1:
            lanes.append(data_lane(step - 1))
        while lanes:
            for g in list(lanes):
                try:
                    next(g)
                except StopIteration:
                    lanes.remove(g)
    return kb.finish()


def hyc_inputs(inp, c0, nblk):
    o = 0
    nch = nblk * 128
    fc, zp, win = hy_consts()
    cw = inp['hy_conv_w'][o]
    taps = np.zeros((128, nblk, 3, 3), np.float32)
    for cb in range(nblk):
        for seg in range(3):
            ch0 = seg * D + c0 + cb * 128
            taps[:, cb, seg, :] = cw[:, ch0:ch0 + 128].T
    fo_full = inp['hy_filt_out'][o].reshape(64, 2, 2, D)
    fo = np.ascontiguousarray(fo_full[:, :, :, c0:c0 + nch].reshape(64, 4, nch))
    sk = inp['hy_skip'][o][:, c0:c0 + nch]
    skip = np.ascontiguousarray(sk.reshape(2, nblk, 128).transpose(2, 1, 0))
    return dict(
        taps=taps, zp=zp, win=np.ascontiguousarray(win[c0:c0 + nch]),
        w1=np.ascontiguousarray(inp['hy_pos_w1'][o]), w23=np.ascontiguousarray(np.stack([inp['hy_pos_w2'][o], inp['hy_pos_w3'][o]], axis=1)),
        bfr=np.ascontiguousarray(np.stack([inp['hy_pos_b1'][o], inp['hy_pos_b2'][o], inp['hy_pos_b3'][o], inp['hy_freq'][o]], axis=1)),
        fo=fo, skip=skip, fc=fc)


ARENA = 53200


def build_fused():
    kb = KB(fused=True, arena_floats=ARENA)
    nc = kb.nc
    L = LSEQ
    XT = nc.dram_tensor('XT', [D, L], F32, kind="ExternalInput").ap()
    CXT = nc.dram_tensor('CXT', [D, 256], F32, kind="ExternalInput").ap()
    Yint = nc.dram_tensor('Yint', [D, L], F32, kind="Internal").ap()
    H0 = nc.dram_tensor('H0int', [D, L], F32, kind="Internal").ap()
    U = nc.dram_tensor('Uint', [3 * D, L], F32, kind="Internal").ap()
    Z2 = nc.dram_tensor('Z2int', [D, 1, L], F32, kind="Internal").ap()
    kb.init_psum(8)
    for hh in range(2):
        kb.next_stage('a%d_' % hh, {'XT': XT, 'CXT': CXT, 'YT': Yint})
        kb.yoffs = (hh * 256, 512 + hh * 256)
        build_mixa(kb=kb)
    kb.next_stage('b_', {'HT': XT, 'YT': Yint, 'HO': H0, 'UT': U})
    build_post(L, last=False, kb=kb)
    kb.next_stage('c_', {'UT3': U.rearrange("(s c) (b t) -> s c b t", s=3, b=1), 'Z2T': Z2})
    build_hyc(nb=1, nblk=8, kb=kb)
    kb.next_stage('d_', {'HT': H0, 'YT': Z2.rearrange("c b t -> c (b t)")})
    kb.dyn_tok = True
    kb.dyn_half = L // 2
    build_post(L // 2, last=True, kb=kb)
    kb.stage_amax = kb.amax
    return kb.finish(force=True)


def fused_inputs(inp, core):
    b, r = core // 2, core % 2
    m = {}
    for hh in range(2):
        a = mixa_inputs(inp, b, hh)
        m['XT'] = a.pop('XT'); m['CXT'] = a.pop('CXT')
        for k, v in a.items():
            m['a%d_%s' % (hh, k)] = v
    mw, mb = inp['mod_w'], inp['mod_b']
    modw = np.concatenate([mw[0][:, 2 * D:6 * D], mw[1][:, 0:2 * D]], axis=1)
    modb = np.concatenate([mb[0][2 * D:6 * D], mb[1][0:2 * D]])
    bd = dict(wo=blk_w(inp['ab_w_out'][0], 8), modw=blk_w(modw, 8), modb=fm_vec(modb), cT=fm_vec(inp['c'][b]),
              nw=np.ascontiguousarray(np.stack([fm_vec(inp['norm2_w'][0]), fm_vec(inp['norm1_w'][1])], axis=-1)),
              w1=blk_w(inp['ffn_w1'][0], 8), w3=blk_w(inp['ffn_w3'][0], 8), w2=blk_w(inp['ffn_w2'][0], 22),
              wn=blk_w(inp['hy_w_in'][0], 8))
    for k, v in bd.items():
        m['b_' + k] = v
    for k, v in hyc_inputs(inp, 0, 8).items():
        m['c_' + k] = v
    dd = dict(wo=blk_w(inp['hy_w_out'][0], 8), modw=blk_w(np.ascontiguousarray(mw[1][:, 2 * D:6 * D]), 8), modb=fm_vec(mb[1][2 * D:6 * D]), cT=fm_vec(inp['c'][b]),
              nw=np.ascontiguousarray(np.stack([fm_vec(inp['norm2_w'][1]), fm_vec(inp['final_norm_w'])], axis=-1)),
              w1=blk_w(inp['ffn_w1'][1], 8), w3=blk_w(inp['ffn_w3'][1], 8), w2=blk_w(inp['ffn_w2'][1], 22))
    for k, v in dd.items():
        m['d_' + k] = v
    return m


def kernel(**inputs):
    inp = {k: np.asarray(v, dtype=np.float32) for k, v in inputs.items()}
    B, L = 4, LSEQ
    cores = list(range(NCORES))
    nc = build_fused()
    shared = {}
    maps = []
    for c in cores:
        m = fused_inputs(inp, c)
        for k in list(m.keys()):
            if k in shared and shared[k].shape == m[k].shape and k not in ('XT', 'CXT') and not k.endswith('cT') and not k.startswith('a'):
                m[k] = shared[k]
            else:
                shared.setdefault(k, m[k])
        maps.append(m)
    res = run_bass_kernel_spmd(nc, maps, core_ids=cores)
    out = np.zeros((B, L, D), np.float32)
    NT = L // 2
    for c in cores:
        b, r = c // 2, c % 2
        out[b, r * NT:(r + 1) * NT, :] = res.results[c]['d_HO'].T
    return out
```

```python
import math
import os
import numpy as np
DBG_CUT = 99
from contextlib import ExitStack
import concourse.bass as bass
import concourse.mybir as mybir
from concourse.bass_utils import run_bass_kernel_spmd

F32 = mybir.dt.float32
AF = mybir.ActivationFunctionType
ALU = mybir.AluOpType
AX = mybir.AxisListType

D = 1024
KC = 8
DFF = 2816
FC = 22
EPS = 1e-6
NCORES = 8


class Sched:
    ENG = ['pe', 'act', 'dve', 'pool', 'sp']

    def __init__(self, nc, ctx):
        self.nc = nc
        self.ctx = ctx
        self.sem = {e: ctx.enter_context(nc.semaphore('s_' + e)) for e in self.ENG if e != 'sp'}
        self.cnt = {e: 0 for e in self.ENG}
        self.waited = {e: {} for e in self.ENG}
        self.prog = {e: [] for e in self.ENG}
        self.lastw = {}
        self.readers = {}
        self.dsem = {}

    def semh(self, k):
        return self.sem[k] if isinstance(k, str) else self.dsem[k][0]

    def _deps(self, eng, reads, writes):
        need = {}

        def add(tok):
            if tok is not None and need.get(tok[0], 0) < tok[1]:
                need[tok[0]] = tok[1]
        for r in reads:
            add(self.lastw.get(r))
        for w in writes:
            add(self.lastw.get(w))
            for t in self.readers.get(w, ()):
                add(t)
        out = []
        for k, v in need.items():
            if eng == 'pe' and k == 'pe':
                continue
            if self.waited[eng].get(k, 0) >= v:
                continue
            self.waited[eng][k] = v
            out.append((k, v))
        return out

    def op(self, eng, fn, reads=(), writes=()):
        writes = list(writes) + [r for r in reads if isinstance(r, tuple) and r[0] == 'ps' and r not in writes]
        waits = self._deps(eng, reads, writes)
        self.cnt[eng] += 1
        tok = (eng, self.cnt[eng])
        self.prog[eng].append((waits, fn, self.sem[eng], 1))
        for w in writes:
            self.lastw[w] = tok
            self.readers[w] = []
        for r in reads:
            if r not in writes:
                self.readers.setdefault(r, []).append(tok)
        return tok

    def dma(self, q, out, in_, reads=(), writes=(), semkey=None, in_fn=None):
        if semkey is None:
            semkey = writes[0]
        semkey = ('dma', semkey)
        if semkey not in self.dsem:
            self.dsem[semkey] = [self.ctx.enter_context(self.nc.semaphore('d%d' % len(self.dsem))), 0]
        waits = self._deps(q, reads, writes)
        self.dsem[semkey][1] += 16
        tok = (semkey, self.dsem[semkey][1])
        self.prog[q].append((waits, (lambda e, out=out, in_=in_, in_fn=in_fn: e.dma_start(out=out, in_=(in_fn(e) if in_fn is not None else in_))), self.dsem[semkey][0], 16))
        for w in writes:
            self.lastw[w] = tok
            self.readers[w] = []
        for r in reads:
            self.readers.setdefault(r, []).append(tok)
        return tok

    def barrier(self):
        toks = [(e, self.cnt[e]) for e in self.sem if self.cnt[e] > 0] + [(k, v[1]) for k, v in self.dsem.items() if v[1] > 0]
        for eng in self.ENG:
            waits = []
            for k, v in toks:
                if self.waited[eng].get(k, 0) < v:
                    self.waited[eng][k] = v
                    waits.append((k, v))
            self.prog[eng].append((waits, None, None, 0))

    def wait_keys(self, eng, keys):
        waits = self._deps(eng, list(keys), ())
        self.prog[eng].append((waits, None, None, 0))

    def emit(self):
        engs = {'pe': 'tensor', 'act': 'scalar', 'dve': 'vector', 'pool': 'gpsimd', 'sp': 'sync'}
        with self.nc.allow_non_contiguous_dma(reason="small strided pads / per-chunk layouts"), self.nc.Block() as block:
            for e, name in engs.items():
                prog = self.prog[e]
                if not prog:
                    continue

                def body(eng, prog=prog):
                    for waits, fn, sem, inc in prog:
                        for k, v in waits:
                            eng.wait_ge(self.semh(k), v)
                        if fn is not None:
                            fn(eng).then_inc(sem, inc)
                getattr(block, name)(body)


class KB:
    def __init__(self, fused=False, arena_floats=0):
        self.nc = bass.Bass("TRN2", target_bir_lowering=False)
        self.ctx = ExitStack()
        self.S = Sched(self.nc, self.ctx)
        self.ps = []
        self.psi = 0
        self.wbufs = []
        self.wi = 0
        self.outkeys = []
        self.fused = fused
        self.prefix = ''
        self.bind = {}
        self.yoffs = (0, 256)
        self.dyn_tok = False
        self.arena = None
        self.aoff = 0
        self.amax = 0
        if arena_floats:
            self.arena = self.ctx.enter_context(self.nc.sbuf_tensor('arena_all', [128, arena_floats], F32))
            self.asize = arena_floats

    def next_stage(self, prefix, bind=None):
        self.S.barrier()
        self.aoff = 0
        self.wbufs = []
        self.wi = 0
        self.prefix = prefix
        self.bind = dict(bind or {})

    def inp(self, name, shape):
        if name in self.bind:
            return self.bind[name]
        return self.nc.dram_tensor(self.prefix + name, list(shape), F32, kind="ExternalInput").ap()

    def outp(self, name, shape):
        if name in self.bind:
            return self.bind[name]
        return self.nc.dram_tensor(self.prefix + name, list(shape), F32, kind="ExternalOutput").ap()

    def scratch(self, name, shape):
        if name in self.bind:
            return self.bind[name]
        return self.nc.dram_tensor(self.prefix + name, list(shape), F32, kind="Internal").ap()

    def sb(self, name, shape):
        if self.arena is None:
            return self.ctx.enter_context(self.nc.sbuf_tensor(name, list(shape), F32))
        n = 1
        for d in shape[1:]:
            n *= d
        assert self.aoff + n <= self.asize, ('SBUF arena overflow', name, self.aoff, n)
        ap = self.arena[0:shape[0], self.aoff:self.aoff + n]
        self.aoff += n
        self.amax = max(self.amax, self.aoff)
        if len(shape) > 2:
            names = ['d%d' % i for i in range(len(shape) - 1)]
            ap = ap.rearrange("p (%s) -> p %s" % (' '.join(names), ' '.join(names)), **{nm: shape[i + 1] for i, nm in enumerate(names[:-1])})
        return ap

    def init_psum(self, n=8):
        if self.ps:
            return
        for i in range(n):
            self.ps.append(self.ctx.enter_context(self.nc.psum_tensor('ps%d' % i, [128, 512], F32)))

    def psum(self):
        i = self.psi
        self.psi = (self.psi + 1) % len(self.ps)
        return self.ps[i], ('ps', i)

    def init_wpool(self, n, width):
        for i in range(n):
            self.wbufs.append(self.sb('wbuf%d' % i, [128, width]))

    def wload(self, src, width, q='sp'):
        i = self.wi
        self.wi = (self.wi + 1) % len(self.wbufs)
        buf = self.wbufs[i]
        self.S.dma(q, buf[:, 0:width], src, writes=[('w', i)])
        return buf, ('w', i)

    def mm(self, out, lhsT, rhs, start, stop, reads, writes):
        return self.S.op('pe', lambda e: e.matmul(out, lhsT=lhsT, rhs=rhs, start=start, stop=stop), reads, writes)

    def act(self, out, in_, func, reads, writes, bias=None, scale=None, eng='act'):
        kw = {}
        if bias is not None:
            kw['bias'] = bias
        if scale is not None:
            kw['scale'] = scale
        return self.S.op(eng, lambda e: e.activation(out=out, in_=in_, func=func, **kw), reads, writes)

    def tt(self, out, in0, in1, op, reads, writes, eng='dve'):
        return self.S.op(eng, lambda e: e.tensor_tensor(out=out, in0=in0, in1=in1, op=op), reads, writes)

    def ts(self, out, in0, s1, s2, op0, op1, reads, writes, eng='dve'):
        if op1 is None:
            return self.S.op(eng, lambda e: e.tensor_scalar(out=out, in0=in0, scalar1=s1, scalar2=None, op0=op0), reads, writes)
        return self.S.op(eng, lambda e: e.tensor_scalar(out=out, in0=in0, scalar1=s1, scalar2=s2, op0=op0, op1=op1), reads, writes)

    def stt(self, out, in0, scalar, in1, op0, op1, reads, writes, eng='dve'):
        return self.S.op(eng, lambda e: e.scalar_tensor_tensor(out=out, in0=in0, scalar=scalar, in1=in1, op0=op0, op1=op1), reads, writes)

    def cp(self, out, in_, reads, writes, eng='dve'):
        if eng == 'act':
            return self.S.op('act', lambda e: e.copy(out=out, in_=in_), reads, writes)
        return self.S.op(eng, lambda e: e.tensor_copy(out=out, in_=in_), reads, writes)

    def memset(self, ap, val, writes, eng='pool'):
        return self.S.op(eng, lambda e: e.memset(ap, val), (), writes)

    def recip(self, out, in_, reads, writes):
        return self.S.op('dve', lambda e: e.reciprocal(out=out, in_=in_), reads, writes)

    def load(self, dst, src, key, q='sp'):
        return self.S.dma(q, dst, src, writes=[key])

    def load_tok(self, dst, view, t0, T, key):
        if not self.dyn_tok:
            return self.load(dst, view[:, :, t0:t0 + T], key)
        half = self.dyn_half

        def in_fn(e):
            if getattr(self, '_rbase', None) is None:
                self._rbase = e.snap((e.partition_id() % 2) * half, min_val=0, max_val=half)
            return view[:, :, bass.ds(self._rbase + t0, T)]
        return self.S.dma('sp', dst, None, writes=[key], in_fn=in_fn)

    def store(self, dst, src, srckey, dstkey, q='pool', final=True):
        self.S.dma(q, dst, src, reads=[srckey], writes=[dstkey], semkey=srckey)
        if final:
            self.outkeys.append(dstkey)

    def finish(self, force=False):
        if self.fused and not force:
            return None
        self.S.wait_keys('pool', self.outkeys)
        self.S.emit()
        self.ctx.close()
        return self.nc


def emit_modvec(kb, cT, ckey, nv, wl, bl, nblk, out, okey):
    for b in range(nblk):
        wb, wk = kb.wload(wl[b], KC * 128)
        ps, pk = kb.psum()
        for k in range(KC):
            kb.mm(ps[:, 0:nv], wb[:, k * 128:(k + 1) * 128], cT[:, k, :], k == 0, k == KC - 1, [wk, ckey], [pk])
        kb.ts(out[:, b, :], ps[:, 0:nv], bl[:, b:b + 1], None, ALU.add, None, [pk, 'modb'], [okey])


def emit_norm(kb, X, xkey, A, akey, T, ones, gain, shift, gkey, sq, rstd):
    for k in range(KC):
        kb.act(sq[:, k, :T], X[:, k, :T], AF.Square, [xkey], [('sq', k)])
    ps, pk = kb.psum()
    for k in range(KC):
        kb.mm(ps[:, :T], ones[:, :], sq[:, k, :T], k == 0, k == KC - 1, ['ones', ('sq', k)], [pk])
    kb.act(rstd[:, :T], ps[:, :T], AF.Sqrt, [pk, 'eps'], ['rstd'], bias=kb.eps_ap, scale=1.0 / D)
    kb.recip(rstd[:, :T], rstd[:, :T], ['rstd'], ['rstd'])
    for k in range(KC):
        kb.tt(sq[:, k, :T], X[:, k, :T], rstd[:, :T], ALU.mult, [xkey, 'rstd', ('sq', k)], [('sq', k)], eng='dve' if k % 2 == 0 else 'pool')
        kb.act(A[:, k, :T], sq[:, k, :T], AF.Identity, [('sq', k), gkey], [akey], bias=shift[:, k, 0:1], scale=gain[:, k, 0:1])


def emit_linear(kb, wl, nblk, kc, rhs_fn, rkeys, T, evac):
    for b in range(nblk):
        wb, wk = kb.wload(wl[b], kc * 128)
        ps, pk = kb.psum()
        for k in range(kc):
            kb.mm(ps[:, :T], wb[:, k * 128:(k + 1) * 128], rhs_fn(k), k == 0, k == kc - 1, [wk] + rkeys, [pk])
        evac(b, ps, pk)


def emit_ffn(kb, A, akey, H, hkey, G, T, w1l, w3l, w2l, g2, gkey, tmp):
    for j in range(FC):
        w1, k1 = kb.wload(w1l[j], KC * 128)
        w3, k3 = kb.wload(w3l[j], KC * 128)
        p1, pk1 = kb.psum()
        p3, pk3 = kb.psum()
        for k in range(KC):
            kb.mm(p1[:, :T], w1[:, k * 128:(k + 1) * 128], A[:, k, :T], k == 0, k == KC - 1, [k1, akey], [pk1])
        for k in range(KC):
            kb.mm(p3[:, :T], w3[:, k * 128:(k + 1) * 128], A[:, k, :T], k == 0, k == KC - 1, [k3, akey], [pk3])
        tk = ('tmp', j % 2)
        kb.act(tmp[:, j % 2, :T], p1[:, :T], AF.Silu, [pk1], [tk])
        kb.tt(G[:, j, :T], tmp[:, j % 2, :T], p3[:, :T], ALU.mult, [tk, pk3], [('G', j)])
    for i in range(KC):
        w2, k2 = kb.wload(w2l[i], FC * 128)
        ps, pk = kb.psum()
        for j in range(FC):
            kb.mm(ps[:, :T], w2[:, j * 128:(j + 1) * 128], G[:, j, :T], j == 0, j == FC - 1, [k2, ('G', j)], [pk])
        kb.stt(H[:, i, :T], ps[:, :T], g2[:, i, 0:1], H[:, i, :T], ALU.mult, ALU.add, [pk, gkey, hkey], [hkey])


TT = 512


def build_post(ntok, last, kb=None):
    kb = kb or KB()
    nmb = 32 if last else 48
    HT = kb.inp('HT', [D, ntok]); YT = kb.inp('YT', [D, ntok])
    wo = kb.inp('wo', [KC, 128, D]); cTd = kb.inp('cT', [128, KC])
    modw = kb.inp('modw', [nmb, 128, D]); modb = kb.inp('modb', [128, nmb]); nw = kb.inp('nw', [128, KC, 2])
    w1l = kb.inp('w1', [FC, 128, D]); w3l = kb.inp('w3', [FC, 128, D]); w2l = kb.inp('w2', [KC, 128, DFF])
    HO = kb.outp('HO', [D, ntok])
    if not last:
        wn = kb.inp('wn', [24, 128, D]); UT = kb.outp('UT', [3 * D, ntok])
    kb.init_psum(8)
    kb.init_wpool(4, DFF)
    ones = kb.sb('ones', [128, 128]); epst = kb.sb('epst', [128, 1])
    kb.memset(ones[:], 1.0, ['ones']); kb.memset(epst[:], EPS, ['eps'])
    kb.eps_ap = epst[:, 0:1]
    cT = kb.sb('cTs', [128, KC, 1]); modbs = kb.sb('modbs', [128, nmb]); nws = kb.sb('nws', [128, KC, 2])
    mod = kb.sb('mod', [128, nmb, 1])
    kb.load(cT[:, :, 0], cTd[:, :], 'cT'); kb.load(modbs[:], modb[:, :], 'modb'); kb.load(nws[:], nw[:, :, :], 'nw')
    kb.act(cT[:, :, 0], cT[:, :, 0], AF.Silu, ['cT'], ['cT'])
    emit_modvec(kb, cT, 'cT', 1, modw, modbs, nmb, mod, 'mod')
    gains = kb.sb('gains', [128, KC, 2])
    kb.stt(gains[:, :, 0:1], mod[:, 16:24, :], 1.0, nws[:, :, 0:1], ALU.add, ALU.mult, ['mod', 'nw'], ['gains'])
    if not last:
        kb.stt(gains[:, :, 1:2], mod[:, 40:48, :], 1.0, nws[:, :, 1:2], ALU.add, ALU.mult, ['mod', 'nw', 'gains'], ['gains'])
    else:
        kb.cp(gains[:, :, 1:2], nws[:, :, 1:2], ['nw', 'gains'], ['gains'])
    zshift = kb.sb('zshift', [128, KC, 1])
    kb.memset(zshift[:], 0.0, ['zshift'])
    H = kb.sb('H', [128, KC, TT]); Y = kb.sb('Y', [128, KC, TT]); A = kb.sb('A', [128, KC, TT]); sq = kb.sb('sq', [128, KC, TT])
    rstd = kb.sb('rstd', [128, TT]); G = kb.sb('G', [128, FC, TT]); tmp = kb.sb('tmp', [128, 2, TT])
    HTv = HT.rearrange("(k p) t -> p k t", p=128); YTv = YT.rearrange("(k p) t -> p k t", p=128)
    HOv = HO.rearrange("(k p) t -> p k t", p=128)
    if not last:
        UTv = UT.rearrange("(k p) t -> p k t", p=128)
    for ti in range(ntok // TT):
        tsl = slice(ti * TT, (ti + 1) * TT)
        kb.load_tok(H[:], HTv, ti * TT, TT, 'H'); kb.load_tok(Y[:], YTv, ti * TT, TT, 'Y')

        def ev_o(b, ps, pk):
            kb.stt(H[:, b, :], ps[:, :TT], mod[:, b, 0:1], H[:, b, :], ALU.mult, ALU.add, [pk, 'mod', 'H'], ['H'])
        emit_linear(kb, wo, KC, KC, lambda k: Y[:, k, :], ['Y'], TT, ev_o)
        emit_norm(kb, H, 'H', A, 'A', TT, ones, gains[:, :, 0:1], mod[:, 8:16, :], 'gains', sq, rstd)
        emit_ffn(kb, A, 'A', H, 'H', G, TT, w1l, w3l, w2l, mod[:, 24:32, :], 'mod', tmp)
        if not last:
            kb.store(HOv[:, :, tsl], H[:], 'H', ('HO', ti))
            emit_norm(kb, H, 'H', A, 'A', TT, ones, gains[:, :, 1:2], mod[:, 32:40, :], 'gains', sq, rstd)

            def ev_u(b, ps, pk):
                kb.cp(tmp[:, b % 2, :], ps[:, :TT], [pk], [('tmp', b % 2)], eng='act' if b % 2 else 'dve')
                kb.store(UTv[:, b, tsl], tmp[:, b % 2, :], ('tmp', b % 2), ('UT', ti, b))
            emit_linear(kb, wn, 24, KC, lambda k: A[:, k, :], ['A'], TT, ev_u)
        else:
            emit_norm(kb, H, 'H', A, 'A', TT, ones, gains[:, :, 1:2], zshift, 'gains', sq, rstd)
            kb.store(HOv[:, :, tsl], A[:], 'A', ('HO', ti))
    return kb.finish()


def blk_w(W, kc):
    K, N = W.shape
    nb = N // 128
    return np.ascontiguousarray(W.reshape(kc, 128, nb, 128).transpose(2, 1, 0, 3).reshape(nb, 128, kc * 128))


def fm_vec(v):
    return np.ascontiguousarray(v.reshape(-1, 128).T)


CH = 64
NCTX = 4
NLAT = 128
NG = NCTX + NLAT
NEG = -30000.0
AB_SIZES = (512, 512, 512, 512, 8, 8, 256, 256, 512, 512, 32)


def mixa_consts():
    i = np.arange(64)
    P, Fq = i[:, None], i[None, :]
    tri = [(P <= Fq).astype(np.float32), (P >= Fq).astype(np.float32)]
    ident = np.eye(64, dtype=np.float32)
    v1 = [(P >= Fq), (P <= Fq)]
    v2 = [(Fq >= P), (Fq <= P)]
    s1 = [(P > Fq), (P < Fq)]
    s2 = [(Fq > P), (Fq < P)]
    c = {}
    dd = [0, 0, 1, 1]
    c['TRIS'] = np.stack([tri[d] for d in dd], 1)
    c['ID4'] = np.stack([ident for d in dd], 1)
    c['NEG1'] = np.stack([np.where(v1[d], 0.0, NEG) for d in dd], 1).astype(np.float32)
    c['NEG2'] = np.stack([np.where(v2[d], 0.0, NEG) for d in dd], 1).astype(np.float32)
    c['ST1'] = np.stack([s1[d].astype(np.float32) for d in dd], 1)
    c['ST2'] = np.stack([s2[d].astype(np.float32) for d in dd], 1)
    c['INC2'] = np.stack([v2[d].astype(np.float32) for d in dd], 1)
    names = ['TRIS', 'ID4', 'NEG1', 'NEG2', 'ST1', 'ST2', 'INC2']
    return np.ascontiguousarray(np.concatenate([c[n] for n in names], 1)), names


def build_mixa(stage=99, ntiles=99, kb=None):
    kb = kb or KB()
    S = kb.S
    L = 8192
    LC = 256
    XT = kb.inp('XT', [D, L]); CXT = kb.inp('CXT', [D, LC])
    cTd = kb.inp('cT', [128, KC, 2]); modw = kb.inp('modw', [16, 128, D]); modb = kb.inp('modb', [128, 16]); nw = kb.inp('nw', [128, KC])
    wl = kb.inp('wl', [18, 128, D]); wg = kb.inp('wg', [128, KC, 8]); taps = kb.inp('taps', [128, 6, 3])
    gconst = kb.inp('gconst', [64, 2, 32])
    gwb = kb.inp('gwb', [17, 2, 128]); hnw = kb.inp('hnw', [128, 2]); cm = kb.inp('cm', [64, 28, 64]); identd = kb.inp('ident', [128, 128])
    YT = kb.outp('YT', [512, L])
    shared_at = getattr(kb, 'shared_at', None)
    if shared_at is None:
        ATl = kb.scratch('ATl', [D, L + 2]); ATc = kb.scratch('ATc', [D, LC + 2])
    else:
        ATl, ATc = shared_at
    G_U0 = kb.scratch('G_U0', [NG, 64, 4, 128]); G_KC = kb.scratch('G_KC', [NG, 128, 4, 64]); G_KD = kb.scratch('G_KD', [NG, 64, 4, 128])
    G_AT = kb.scratch('G_AT', [NG, 64, 4, 64]); G_QT = kb.scratch('G_QT', [NG, 128, 2, 64])
    L_KD = kb.scratch('L_KD', [NG, 64, 4, 64]); L_V = kb.scratch('L_V', [NG, 64, 2, 128]); L_AT = kb.scratch('L_AT', [NG, 64, 4, 64]); L_QD = kb.scratch('L_QD', [NG, 64, 4, 64])
    OG = kb.scratch('OG', [NLAT, 64, 4, 128]); OL = kb.scratch('OL', [NLAT, 64, 4, 128])
    ZT = kb.scratch('ZT', [4, 128, L])
    kb.init_psum(8)
    kb.init_wpool(3, D)
    ones = kb.sb('ones', [128, 128]); epst = kb.sb('epst', [128, 1]); ident = kb.sb('ident_s', [128, 128])
    kb.memset(ones[:], 1.0, ['ones']); kb.memset(epst[:], EPS, ['eps']); kb.eps_ap = epst[:, 0:1]
    kb.load(ident[:], identd[:, :], 'ident')
    cms = kb.sb('cms', [64, 28, 64]); kb.load(cms[:], cm[:, :, :], 'cm')
    TRIS, ID4, NEG1, NEG2, ST1, ST2, INC2 = [cms[:, 4 * i:4 * i + 4, :] for i in range(7)]
    cT = kb.sb('cTs', [128, KC, 2]); modbs = kb.sb('modbs', [128, 16]); nws = kb.sb('nws', [128, KC, 1]); mod = kb.sb('mod', [128, 16, 2])
    wgs = kb.sb('wgs', [128, KC, 8]); tapss = kb.sb('tapss', [128, 6, 3]); gcs = kb.sb('gcs', [64, 2, 32]); gwbs = kb.sb('gwbs', [17, 2, 128]); hnws = kb.sb('hnws', [128, 2])
    kb.load(cT[:], cTd[:, :, :], 'cT'); kb.load(modbs[:], modb[:, :], 'modb'); kb.load(nws[:, :, 0], nw[:, :], 'nw')
    kb.load(wgs[:], wg[:, :, :], 'wg'); kb.load(tapss[:], taps[:, :, :], 'taps'); kb.load(gcs[:], gconst[:, :, :], 'gcs'); kb.load(gwbs[:], gwb[:, :, :], 'gwb'); kb.load(hnws[:], hnw[:, :], 'hnw')
    kb.act(cT[:], cT[:], AF.Silu, ['cT'], ['cT'])
    emit_modvec(kb, cT, 'cT', 2, modw, modbs, 16, mod, 'mod')
    gains = kb.sb('gains', [128, KC, 2])
    for v in range(2):
        kb.stt(gains[:, :, v:v + 1], mod[:, 8:16, v:v + 1], 1.0, nws[:, :, 0:1], ALU.add, ALU.mult, ['mod', 'nw', 'gains'], ['gains'])
    negA = kb.sb('negA', [64, 32])
    kb.act(negA[:], gcs[:, 0, :], AF.Exp, ['gcs'], ['negA'])
    kb.ts(negA[:], negA[:], -1.0, None, ALU.mult, None, ['negA'], ['negA'])
    GGL = kb.sb('GGL', [128, NG, 4]); GEGC = kb.sb('GEGC', [64, NG, 4]); LGL = kb.sb('LGL', [64, NG, 4])
    kb.mixa_aoff_scan = kb.aoff
    arena = kb.sb('arena', [128, 12800])
    X = arena[:, 0:4096].rearrange("p (k t) -> p k t", k=KC); A = arena[:, 4096:8192].rearrange("p (k t) -> p k t", k=KC)
    sq = arena[:, 8192:12288].rearrange("p (k t) -> p k t", k=KC); rstd = arena[:, 12288:12800]
    zt = kb.sb('zt', [128, KC, 1]); kb.memset(zt[:], 0.0, ['zt'])
    ATlv = ATl.rearrange("(k p) t -> p k t", p=128); ATcv = ATc.rearrange("(k p) t -> p k t", p=128)
    XTv = XT.rearrange("(k p) t -> p k t", p=128); CXTv = CXT.rearrange("(k p) t -> p k t", p=128)
    tiles = [('c', 0, LC, 0)] + [('l', i * TT, TT, NCTX + i * 8) for i in range(L // TT)]
    if shared_at is None:
        for (dst, n) in ((ATlv, L), (ATcv, LC)):
            kb.store(dst[:, :, 0:1], zt[:], 'zt', ('ATpad', n, 0), final=False)
            kb.store(dst[:, :, n + 1:n + 2], zt[:], 'zt', ('ATpad', n, 1), final=False)
        for (sq_, t0, T, g0) in tiles:
            src = CXTv if sq_ == 'c' else XTv
            dst = ATcv if sq_ == 'c' else ATlv
            v = 1 if sq_ == 'c' else 0
            kb.load(X[:, :, :T], src[:, :, t0:t0 + T], 'X')
            emit_norm(kb, X, 'X', A, 'A', T, ones, gains[:, :, v:v + 1], mod[:, 0:8, v:v + 1], 'gains', sq, rstd)
            kb.store(dst[:, :, 1 + t0:1 + t0 + T], A[:, :, :T], 'A', ('AT', sq_, t0), final=False)
    atkeys = [('AT', s_, t0) for (s_, t0, T, g0) in tiles] + [('ATpad', n, i) for n in (L, LC) for i in range(2)]
    if kb.fused:
        kb.shared_at = (ATl, ATc)
    if stage < 2:
        return kb.finish()
    tiles = tiles[:ntiles]
    S.barrier()
    AH = arena[:, 0:KC * (TT + 2)].rearrange("p (k t) -> p k t", k=KC)
    Ue = kb.sb('Ue', [128, TT + 2]); cv = kb.sb('cv', [128, TT]); sqb = kb.sb('sqb', [128, TT]); rs = kb.sb('rs', [128, TT])
    qT = kb.sb('qT', [128, 2, TT]); kT = kb.sb('kT', [128, 2, TT]); vT = kb.sb('vT', [128, TT])
    k_tm = arena[0:64, 4112:6160].rearrange("p (c h f) -> p c h f", c=8, h=2); v_tm = arena[0:64, 6160:8208].rearrange("p (c h f) -> p c h f", c=8, h=2)
    qlT = kb.sb('qlT', [64, 2, TT]); klT = kb.sb('klT', [64, 2, TT]); kl_tm = kb.sb('kl_tm', [64, 8, 2, 64]); vl_tm = arena[0:64, 8208:10256].rearrange("p (c h f) -> p c h f", c=8, h=2)
    llrT = kb.sb('llrT', [17, 2, TT]); kb.memset(llrT[:], 1.0, ['llrT'])
    zs = kb.sb('zs', [128, 2, TT])
    g_tm = kb.sb('g_tm', [64, 8, 4]); b_tm = kb.sb('b_tm', [64, 8, 4]); gx = kb.sb('gx', [64, 8, 4])
    la_tm = arena[0:64, 10256:12304].rearrange("p (c h f) -> p c h f", c=8, h=4)
    def mk_lane(tag):
        sb = lambda n, shp: kb.sb(n + tag, shp)
        d = dict(tag=tag)
        d['RGB'] = sb('RGB', [64, 8, 64]); d['gcc'] = sb('gcc', [64, 4]); d['t1'] = sb('t1', [64, 4, 64]); d['t2'] = sb('t2', [64, 4, 64])
        d['E1'] = sb('E1', [64, 4, 64]); d['E2'] = sb('E2', [64, 4, 64]); d['Wm'] = sb('Wm', [64, 4, 64]); d['WmT'] = sb('WmT', [64, 4, 64])
        d['Pb'] = [sb('Pb%d' % i, [64, 4, 64]) for i in range(2)]; d['PTb'] = [sb('PTb%d' % i, [64, 4, 64]) for i in range(2)]; d['XTb'] = [sb('XTb%d' % i, [64, 4, 64]) for i in range(2)]
        d['attT'] = sb('attT', [64, 4, 64]); d['egc'] = sb('egc', [64, 4]); d['bege'] = sb('bege', [64, 4]); d['ekd'] = sb('ekd', [64, 4])
        d['VB'] = sb('VB', [64, 4, 128]); d['KBG'] = sb('KBG', [64, 4, 128]); d['KDg'] = sb('KDg', [64, 4, 128]); d['U0s'] = sb('U0s', [64, 4, 128]); d['KCs'] = sb('KCs', [128, 4, 64])
        d['refc'] = sb('refc', [64, 4]); d['dlt'] = sb('dlt', [64, 4, 64]); d['Ea'] = sb('Ea', [64, 4, 64]); d['Eb'] = sb('Eb', [64, 4, 64]); d['Ec'] = sb('Ec', [64, 4, 64])
        d['QQ'] = sb('QQ', [64, 4, 64]); d['KKl'] = sb('KKl', [64, 4, 64]); d['QDl'] = sb('QDl', [64, 4, 64]); d['ATl_'] = sb('ATTl', [64, 4, 64]); d['bct'] = sb('bct', [64, 4, 64]); d['KDl'] = sb('KDl', [64, 4, 64])
        return d
    prep_lanes = [mk_lane('L0'), mk_lane('L1')]

    def bc_last(ap, n):
        return ap.unsqueeze(2).to_broadcast([ap.shape[0], ap.shape[1], n])

    def l2norm(dst, T, qscale):
        kb.act(sqb[:, :T], dst, AF.Square, ['qkv'], ['sqb'])
        ps, pk = kb.psum()
        kb.mm(ps[:, :T], ones[:, :], sqb[:, :T], True, True, ['ones', 'sqb'], [pk])
        kb.act(rs[:, :T], ps[:, :T], AF.Sqrt, [pk, 'eps'], ['rs'], bias=kb.eps_ap, scale=1.0)
        kb.recip(rs[:, :T], rs[:, :T], ['rs'], ['rs'])
        kb.stt(dst, dst, qscale, rs[:, :T], ALU.mult, ALU.mult, ['qkv', 'rs'], ['qkv'])

    def transp(src_fn, nch, K_, M_, dst_fn, keys_r, key_w):
        per = 512 // M_
        for c0 in range(0, nch, per):
            ps, pk = kb.psum()
            n = min(per, nch - c0)
            for c in range(c0, c0 + n):
                kb.mm(ps[0:64, (c - c0) * M_:(c - c0 + 1) * M_], src_fn(c), ident[0:K_, 0:M_], True, True, keys_r + ['ident'], [pk])
            for c in range(c0, c0 + n):
                kb.cp(dst_fn(c), ps[0:64, (c - c0) * M_:(c - c0 + 1) * M_], [pk], [key_w], eng='act' if c % 2 else 'dve')

    for (sq_, t0, T, g0) in tiles:
        nch = T // CH
        src = ATcv if sq_ == 'c' else ATlv
        S.dma('sp', AH[:, :, 0:T + 2], src[:, :, t0:t0 + T + 2], reads=atkeys, writes=['AH'])

        def proj(b, M_, T=T):
            wb, wk = kb.wload(wl[b], D)
            ps, pk = kb.psum()
            for k in range(KC):
                kb.mm(ps[0:M_, :T], wb[:, k * 128:k * 128 + M_], AH[:, k, 1:T + 1], k == 0, k == KC - 1, [wk, 'AH'], [pk])
            return wb, wk, ps, pk
        for b in range(6):
            wb, wk, ps, pk = proj(b, 128)
            ph, phk = kb.psum()
            for k in range(KC):
                kb.mm(ph[:, 0:2], wb[:, k * 128:(k + 1) * 128], AH[:, k, 0:T + 2:T + 1], k == 0, k == KC - 1, [wk, 'AH'], [phk])
            kb.cp(Ue[:, 1:T + 1], ps[:, :T], [pk], ['Ue'], eng='act')
            kb.cp(Ue[:, 0:T + 2:T + 1], ph[:, 0:2], [phk, 'Ue'], ['Ue'])
            kb.ts(cv[:, :T], Ue[:, 1:T + 1], tapss[:, b, 1:2], None, ALU.mult, None, ['Ue', 'taps'], ['cv'])
            kb.stt(cv[:, :T], Ue[:, 0:T], tapss[:, b, 0:1], cv[:, :T], ALU.mult, ALU.add, ['Ue', 'taps', 'cv'], ['cv'])
            kb.stt(cv[:, :T], Ue[:, 2:T + 2], tapss[:, b, 2:3], cv[:, :T], ALU.mult, ALU.add, ['Ue', 'taps', 'cv'], ['cv'])
            seg, hl = b // 2, b % 2
            dst = (qT[:, hl, :T], kT[:, hl, :T], vT[:, :T])[seg]
            kb.act(dst, cv[:, :T], AF.Silu, ['cv'], ['qkv'])
            if seg < 2:
                l2norm(dst, T, 128 ** -0.5 if seg == 0 else 1.0)
            if seg == 0:
                kb.store(G_QT[g0:g0 + nch, :, hl, :].rearrange("g d i -> d g i"), qT[:, hl, :T].rearrange("d (g i) -> d g i", i=CH), 'qkv', ('G_QT', g0, hl), final=False)
            if seg == 1:
                transp(lambda c, hl=hl: kT[:, hl, c * CH:(c + 1) * CH], nch, 128, 128, lambda c, hl=hl: k_tm[:, c, hl, :], ['qkv'], 'k_tm')
            if seg == 2:
                transp(lambda c: vT[:, c * CH:(c + 1) * CH], nch, 128, 128, lambda c, hl=hl: v_tm[:, c, hl, :], ['qkv'], 'v_tm')
        for bi, b in enumerate((6, 7, 14, 15)):
            wb, wk, ps, pk = proj(b, 128)
            kb.act(zs[:, bi % 2, :T], ps[:, :T], AF.Silu, [pk], [('zs', bi % 2)])
            if sq_ == 'l':
                kb.store(ZT[bi, :, t0:t0 + T], zs[:, bi % 2, :T], ('zs', bi % 2), ('ZT', bi, t0), final=False)
        for b in (8, 9, 10, 11):
            wb, wk, ps, pk = proj(b, 64)
            hl = b % 2
            if b < 10:
                kb.ts(qlT[:, hl, :T], ps[0:64, :T], 64 ** -0.5, None, ALU.mult, None, [pk], ['qlT'])
            else:
                kb.cp(klT[:, hl, :T], ps[0:64, :T], [pk], ['klT'], eng='act')
                transp(lambda c, hl=hl: klT[:, hl, c * CH:(c + 1) * CH], nch, 64, 64, lambda c, hl=hl: kl_tm[:, c, hl, :], ['klT'], 'kl_tm')
        for b in (12, 13):
            wb, wk, ps, pk = proj(b, 128)
            hl = b % 2
            kb.cp(vT[:, :T], ps[:, :T], [pk, 'qkv'], ['qkv'], eng='act')
            transp(lambda c: vT[:, c * CH:(c + 1) * CH], nch, 128, 128, lambda c, hl=hl: vl_tm[:, c, hl, :], ['qkv'], 'vl_tm')
        for d in range(2):
            wb, wk, ps, pk = proj(16 + d, 16)
            kb.cp(llrT[0:16, d, :T], ps[0:16, :T], [pk, 'llrT'], ['llrT'])
        psg, pgk = kb.psum()
        for c in range(nch):
            for k in range(KC):
                kb.mm(psg[0:64, c * 8:(c + 1) * 8], AH[:, k, 1 + c * CH:1 + (c + 1) * CH], wgs[:, k, :], k == 0, k == KC - 1, ['AH', 'wg'], [pgk])
        psgv = psg[0:64, 0:nch * 8].rearrange("p (c e) -> p c e", e=8)
        gcv = gcs[:, 1, :].rearrange("p (c e) -> p c e", e=4)
        kb.tt(gx[:, :nch, :], psgv[:, :, 0:4], gcv[:, :nch, :], ALU.add, [pgk, 'gcs'], ['gx'])
        kb.act(gx[:, :nch, :], gx[:, :nch, :], AF.Exp, ['gx'], ['gx'])
        kb.act(gx[:, :nch, :], gx[:, :nch, :], AF.Ln, ['gx'], ['gx'], bias=1.0)
        kb.tt(g_tm[:, :nch, :], gx[:, :nch, :], negA[:].rearrange("p (c e) -> p c e", e=4)[:, :nch, :], ALU.mult, ['gx', 'negA'], ['g_tm'])
        kb.act(b_tm[:, :nch, :], psgv[:, :, 4:8], AF.Sigmoid, [pgk], ['b_tm'])
        for d in range(2):
            for c0 in range(0, nch, 4):
                ps, pk = kb.psum()
                n = min(4, nch - c0)
                for c in range(c0, c0 + n):
                    kb.mm(ps[0:64, (c - c0) * 128:(c - c0 + 1) * 128], llrT[0:17, d, c * CH:(c + 1) * CH], gwbs[0:17, d, :], True, True, ['llrT', 'gwb'], [pk])
                dstv = la_tm[:, c0:c0 + n, 2 * d:2 * d + 2, :]
                kb.act(dstv, ps[0:64, 0:n * 128].rearrange("p (c h f) -> p c h f", h=2, f=64), AF.Exp, [pk, 'la_tm'], ['la_tm'], scale=-1.0)
                kb.act(dstv, dstv, AF.Ln, ['la_tm'], ['la_tm'], bias=1.0)
                kb.ts(dstv, dstv, -1.0 / 16.0, None, ALU.mult, None, ['la_tm'], ['la_tm'])
        def prep(c, LN, g0=g0):
            g = g0 + c
            sfx = LN['tag']
            RGB, gcc, t1, t2, E1, E2, Wm, WmT = LN['RGB'], LN['gcc'], LN['t1'], LN['t2'], LN['E1'], LN['E2'], LN['Wm'], LN['WmT']
            Pb, PTb, XTb, attT, egc, bege, ekd = LN['Pb'], LN['PTb'], LN['XTb'], LN['attT'], LN['egc'], LN['bege'], LN['ekd']
            VB, KBG, KDg, U0s, KCs = LN['VB'], LN['KBG'], LN['KDg'], LN['U0s'], LN['KCs']
            refc, dlt, Ea, Eb, Ec, QQ, KKl, QDl, ATl_, bct, KDl = LN['refc'], LN['dlt'], LN['Ea'], LN['Eb'], LN['Ec'], LN['QQ'], LN['KKl'], LN['QDl'], LN['ATl_'], LN['bct'], LN['KDl']
            csl = slice(c * CH, (c + 1) * CH)
            kb.tt(RGB[:, 0:4, :], TRIS, bc_last(g_tm[:, c, :], 64), ALU.mult, ['cm', 'g_tm'], [sfx + 'RGB'])
            kb.tt(RGB[:, 4:8, :], ID4, bc_last(b_tm[:, c, :], 64), ALU.mult, ['cm', 'b_tm', sfx + 'RGB'], [sfx + 'RGB'], eng='pool')
            psR, pRk = kb.psum()
            kb.mm(psR[0:64, :], ones[0:64, 0:64], RGB[:].rearrange("p a b -> p (a b)"), True, True, ['ones', sfx + 'RGB'], [pRk])
            R = psR[0:64, 0:256].rearrange("p (a b) -> p a b", b=64); Bb = psR[0:64, 256:512].rearrange("p (a b) -> p a b", b=64)
            psC, pCk = kb.psum()
            for d in range(2):
                kb.mm(psC[0:64, 2 * d:2 * d + 2], cms[:, 2 * d, :], g_tm[:, c, 2 * d:2 * d + 2], True, True, ['cm', 'g_tm'], [pCk])
            kb.mm(psC[0:128, 4:8], ones[0:64, 0:128], g_tm[:, c, :], True, True, ['ones', 'g_tm'], [pCk])
            kb.cp(gcc[:], psC[0:64, 0:4], [pCk], [sfx + 'gcc'], eng='act')
            kb.stt(t1[:], R, -1.0, NEG1, ALU.mult, ALU.add, [pRk, 'cm'], [sfx + 't1'])
            kb.tt(t1[:], t1[:], bc_last(gcc[:], 64), ALU.add, [sfx + 't1', sfx + 'gcc'], [sfx + 't1'])
            kb.act(E1[:], t1[:], AF.Exp, [sfx + 't1'], [sfx + 'E1'])
            kb.tt(t2[:], R, NEG2, ALU.add, [pRk, 'cm'], [sfx + 't2'])
            kb.tt(t2[:], t2[:], bc_last(gcc[:], 64), ALU.subtract, [sfx + 't2', sfx + 'gcc'], [sfx + 't2'])
            kb.act(E2[:], t2[:], AF.Exp, [sfx + 't2'], [sfx + 'E2'])
            kb.tt(Wm[:], E1[:], bc_last(b_tm[:, c, :], 64), ALU.mult, [sfx + 'E1', 'b_tm'], [sfx + 'Wm'])
            kb.tt(Wm[:], Wm[:], ST1, ALU.mult, [sfx + 'Wm', 'cm'], [sfx + 'Wm'], eng='pool')
            kb.tt(WmT[:], Bb, E2[:], ALU.mult, [pRk, sfx + 'E2'], [sfx + 'WmT'])
            kb.tt(WmT[:], WmT[:], ST2, ALU.mult, [sfx + 'WmT', 'cm'], [sfx + 'WmT'], eng='pool')
            kb.act(egc[:], gcc[:], AF.Exp, [sfx + 'gcc'], [sfx + 'egc'])
            kb.tt(bege[:], egc[:], b_tm[:, c, :], ALU.mult, [sfx + 'egc', 'b_tm'], [sfx + 'bege'])
            kb.tt(ekd[:], psC[0:64, 4:8], gcc[:], ALU.subtract, [pCk, sfx + 'gcc'], [sfx + 'ekd'])
            kb.act(ekd[:], ekd[:], AF.Exp, [sfx + 'ekd'], [sfx + 'ekd'])
            kb.act(GGL[:, g, :], psC[0:128, 4:8], AF.Exp, [pCk], ['GGL'])
            kb.cp(GEGC[:, g, :], egc[:], [sfx + 'egc'], ['GEGC'], eng='pool')
            yield
            psK, pKk = kb.psum()
            for hl in range(2):
                kb.mm(psK[0:64, hl * 64:(hl + 1) * 64], kT[:, hl, csl], kT[:, hl, csl], True, True, ['qkv'], [pKk])
                kb.mm(psK[0:64, 128 + hl * 64:128 + (hl + 1) * 64], kT[:, hl, csl], qT[:, hl, csl], True, True, ['qkv'], [pKk])
            KK = psK[0:64, 0:128].rearrange("p (a b) -> p a b", b=64); QK = psK[0:64, 128:256].rearrange("p (a b) -> p a b", b=64)
            P, PT, XTm = Pb[0], PTb[0], XTb[0]
            for d in range(2):
                kb.tt(P[:, 2 * d:2 * d + 2, :], KK, Wm[:, 2 * d:2 * d + 2, :], ALU.mult, [pKk, sfx + 'Wm', sfx + 'P0'], [sfx + 'P0'])
                kb.tt(PT[:, 2 * d:2 * d + 2, :], KK, WmT[:, 2 * d:2 * d + 2, :], ALU.mult, [pKk, sfx + 'WmT', sfx + 'PT0'], [sfx + 'PT0'])
                kb.tt(attT[:, 2 * d:2 * d + 2, :], QK, E2[:, 2 * d:2 * d + 2, :], ALU.mult, [pKk, sfx + 'E2', sfx + 'attT'], [sfx + 'attT'])
            kb.tt(XTm[:], ID4, PT[:], ALU.subtract, ['cm', sfx + 'PT0', sfx + 'X0'], [sfx + 'X0'])
            yield
            cur = 0
            for lvl in range(5):
                nxt = 1 - cur
                psP, pPk = kb.psum()
                for i in range(4):
                    kb.mm(psP[0:64, i * 64:(i + 1) * 64], PTb[cur][:, i, :], Pb[cur][:, i, :], True, True, [sfx + 'P%d' % cur, sfx + 'PT%d' % cur], [pPk])
                    if lvl < 4:
                        kb.mm(psP[0:64, 256 + i * 64:256 + (i + 1) * 64], Pb[cur][:, i, :], PTb[cur][:, i, :], True, True, [sfx + 'P%d' % cur, sfx + 'PT%d' % cur], [pPk])
                kb.cp(Pb[nxt][:].rearrange("p a b -> p (a b)"), psP[0:64, 0:256], [pPk, sfx + 'P%d' % nxt], [sfx + 'P%d' % nxt], eng='act')
                if lvl < 4:
                    kb.cp(PTb[nxt][:].rearrange("p a b -> p (a b)"), psP[0:64, 256:512], [pPk, sfx + 'PT%d' % nxt], [sfx + 'PT%d' % nxt])
                psX, pXk = kb.psum()
                for i in range(4):
                    kb.mm(psX[0:64, i * 64:(i + 1) * 64], Pb[nxt][:, i, :], XTb[cur][:, i, :], True, True, [sfx + 'P%d' % nxt, sfx + 'X%d' % cur], [pXk])
                kb.tt(XTb[nxt][:].rearrange("p a b -> p (a b)"), psX[0:64, 0:256], XTb[cur][:].rearrange("p a b -> p (a b)"), ALU.add, [pXk, sfx + 'X%d' % cur, sfx + 'X%d' % nxt], [sfx + 'X%d' % nxt])
                cur = nxt
                yield
            XTf, xk = XTb[cur], sfx + 'X%d' % cur
            for d in range(2):
                dsl = slice(2 * d, 2 * d + 2)
                kb.tt(VB[:, dsl, :], v_tm[:, c, :, :], bc_last(b_tm[:, c, dsl], 128), ALU.mult, ['v_tm', 'b_tm', sfx + 'VB'], [sfx + 'VB'], eng='pool')
                kb.tt(KBG[:, dsl, :], k_tm[:, c, :, :], bc_last(bege[:, dsl], 128), ALU.mult, ['k_tm', sfx + 'bege', sfx + 'KBG'], [sfx + 'KBG'])
                kb.tt(KDg[:, dsl, :], k_tm[:, c, :, :], bc_last(ekd[:, dsl], 128), ALU.mult, ['k_tm', sfx + 'ekd', sfx + 'KDg'], [sfx + 'KDg'], eng='pool')
            psU, pUk = kb.psum()
            psKc, pKck = kb.psum()
            for i in range(4):
                kb.mm(psU[0:64, i * 128:(i + 1) * 128], XTf[:, i, :], VB[:, i, :], True, True, [xk, sfx + 'VB'], [pUk])
                kb.mm(psKc[0:128, i * 64:(i + 1) * 64], KBG[:, i, :], XTf[:, i, :], True, True, [xk, sfx + 'KBG'], [pKck])
            kb.cp(U0s[:].rearrange("p a b -> p (a b)"), psU[0:64, :], [pUk, sfx + 'U0s'], [sfx + 'U0s'], eng='act')
            kb.cp(KCs[:].rearrange("p a b -> p (a b)"), psKc[0:128, 0:256], [pKck, sfx + 'KCs'], [sfx + 'KCs'])
            kb.store(G_U0[g], U0s[:], sfx + 'U0s', ('G_U0', g), final=False)
            kb.store(G_KC[g], KCs[:], sfx + 'KCs', ('G_KC', g), final=False)
            kb.store(G_KD[g], KDg[:], sfx + 'KDg', ('G_KD', g), final=False)
            kb.store(G_AT[g], attT[:], sfx + 'attT', ('G_AT', g), final=False)
            yield
            LA = la_tm[:, c, :, :]
            psB, pBk = kb.psum()
            for i in range(4):
                kb.mm(psB[0:64, i * 64:(i + 1) * 64], LA[:, i, :], cms[:, 2 * (i // 2), :], True, True, ['la_tm', 'cm'], [pBk])
            for d in range(2):
                kb.mm(psB[0:64, 256 + d * 128:256 + (d + 1) * 128], cms[:, 2 * d, :], LA[:, 2 * d:2 * d + 2, :].rearrange("p a b -> p (a b)"), True, True, ['la_tm', 'cm'], [pBk])
            psT, pTk = kb.psum()
            kb.mm(psT[0:64, 0:256], ones[0:64, 0:64], LA.rearrange("p a b -> p (a b)"), True, True, ['la_tm', 'ones'], [pTk])
            for i in range(4):
                kb.mm(psT[0:64, 256 + i:257 + i], LA[:, i, :], ones[0:64, 0:1], True, True, ['la_tm', 'ones'], [pTk])
            bcT = psB[0:64, 0:256].rearrange("p (a b) -> p a b", b=64)
            for d in range(2):
                ridx = 32 if d == 0 else 31
                kb.cp(refc[:, 2 * d:2 * d + 2], bcT[:, 2 * d:2 * d + 2, ridx], [pBk, sfx + 'refc'], [sfx + 'refc'])
            kb.tt(dlt[:], bcT, bc_last(refc[:], 64), ALU.subtract, [pBk, sfx + 'refc'], [sfx + 'dlt'])
            kb.act(Ea[:], dlt[:], AF.Exp, [sfx + 'dlt'], [sfx + 'Ea'])
            kb.act(Eb[:], dlt[:], AF.Exp, [sfx + 'dlt'], [sfx + 'Eb'], scale=-1.0)
            kb.act(Ec[:], bcT, AF.Exp, [pBk], [sfx + 'Ec'])
            kb.cp(bct[:].rearrange("p a b -> p (a b)"), psB[0:64, 256:512], [pBk], [sfx + 'bct'], eng='act')
            kb.tt(bct[:].rearrange("p a b -> p (a b)"), psT[0:64, 0:256], bct[:].rearrange("p a b -> p (a b)"), ALU.subtract, [pTk, sfx + 'bct'], [sfx + 'bct'])
            kb.act(bct[:], bct[:], AF.Exp, [sfx + 'bct'], [sfx + 'bct'])
            kb.act(LGL[:, g, :], psT[0:64, 256:260], AF.Exp, [pTk], ['LGL'])
            yield
            for d in range(2):
                dsl = slice(2 * d, 2 * d + 2)
                kb.tt(QQ[:, dsl, :], qlT[:, :, csl], Ea[:, dsl, :], ALU.mult, ['qlT', sfx + 'Ea', sfx + 'QQ'], [sfx + 'QQ'])
                kb.tt(KKl[:, dsl, :], klT[:, :, csl], Eb[:, dsl, :], ALU.mult, ['klT', sfx + 'Eb', sfx + 'KKl'], [sfx + 'KKl'], eng='pool')
                kb.tt(QDl[:, dsl, :], qlT[:, :, csl], Ec[:, dsl, :], ALU.mult, ['qlT', sfx + 'Ec', sfx + 'QDl'], [sfx + 'QDl'])
            psA, pAk = kb.psum()
            for i in range(4):
                kb.mm(psA[0:64, i * 64:(i + 1) * 64], KKl[:, i, :], QQ[:, i, :], True, True, [sfx + 'KKl', sfx + 'QQ'], [pAk])
            kb.tt(ATl_[:], psA[0:64, 0:256].rearrange("p (a b) -> p a b", b=64), INC2, ALU.mult, [pAk, 'cm'], [sfx + 'ATTl'])
            yield
            for d in range(2):
                dsl = slice(2 * d, 2 * d + 2)
                kb.tt(KDl[:, dsl, :], kl_tm[:, c, :, :], bct[:, dsl, :], ALU.mult, ['kl_tm', sfx + 'bct', sfx + 'KDl'], [sfx + 'KDl'])
            kb.store(L_KD[g], KDl[:], sfx + 'KDl', ('L_KD', g), final=False)
            kb.store(L_AT[g], ATl_[:], sfx + 'ATTl', ('L_AT', g), final=False)
            kb.store(L_QD[g], QDl[:], sfx + 'QDl', ('L_QD', g), final=False)
            kb.store(L_V[g], vl_tm[:, c, :, :], 'vl_tm', ('L_V', g), final=False)
        if stage >= 3:
            for c0 in range(0, nch, 2):
                gens = [prep(c0, prep_lanes[0]), prep(c0 + 1, prep_lanes[1])]
                while gens:
                    for gnr in list(gens):
                        try:
                            next(gnr)
                        except StopIteration:
                            gens.remove(gnr)
    kb.mixa_state = dict(G_U0=G_U0, G_KC=G_KC, G_KD=G_KD, G_AT=G_AT, G_QT=G_QT, L_KD=L_KD, L_V=L_V, L_AT=L_AT, L_QD=L_QD, OG=OG, OL=OL, ZT=ZT,
                         GGL=GGL, GEGC=GEGC, LGL=LGL, YT=YT, ident=ident, hnws=hnws, tiles=tiles,
                         g0of=(lambda g: 0 if g < NCTX else NCTX + ((g - NCTX) // 8) * 8))
    if stage >= 4:
        emit_mixa_scan(kb)
    return kb.finish()


def emit_mixa_scan(kb):
    st = kb.mixa_state
    S = kb.S
    G_U0, G_KC, G_KD, G_AT, G_QT = st['G_U0'], st['G_KC'], st['G_KD'], st['G_AT'], st['G_QT']
    L_KD, L_V, L_AT, L_QD, OG, OL, ZT = st['L_KD'], st['L_V'], st['L_AT'], st['L_QD'], st['OG'], st['OL'], st['ZT']
    GGL, GEGC, LGL, YT, ident, hnws = st['GGL'], st['GEGC'], st['LGL'], st['YT'], st['ident'], st['hnws']
    if kb.arena is not None:
        S.barrier()
        kb.aoff = kb.mixa_aoff_scan
    Sg = kb.sb('Sg', [128, 4, 128]); Sl = kb.sb('Sl', [64, 4, 128])
    kb.memset(Sg[:], 0.0, [('Sg', i) for i in range(4)]); kb.memset(Sl[:], 0.0, [('Sl', i) for i in range(4)])
    NB = 2
    U0b = [[kb.sb('U0b%d%d' % (d, s), [64, 2, 128]) for s in range(NB)] for d in range(2)]
    KCb = [[kb.sb('KCb%d%d' % (d, s), [128, 2, 64]) for s in range(NB)] for d in range(2)]
    KDb = [[kb.sb('KDb%d%d' % (d, s), [64, 2, 128]) for s in range(NB)] for d in range(2)]
    ATb = [[kb.sb('ATb%d%d' % (d, s), [64, 2, 64]) for s in range(NB)] for d in range(2)]
    QTb = [[kb.sb('QTb%d%d' % (d, s), [128, 2, 64]) for s in range(NB)] for d in range(2)]
    LKDb = [[kb.sb('LKDb%d%d' % (d, s), [64, 2, 64]) for s in range(NB)] for d in range(2)]
    LVb = [[kb.sb('LVb%d%d' % (d, s), [64, 2, 128]) for s in range(NB)] for d in range(2)]
    LATb = [[kb.sb('LATb%d%d' % (d, s), [64, 2, 64]) for s in range(NB)] for d in range(2)]
    LQDb = [[kb.sb('LQDb%d%d' % (d, s), [64, 2, 64]) for s in range(NB)] for d in range(2)]
    ub = [kb.sb('ub%d' % d, [64, 2, 128]) for d in range(2)]
    qs = [kb.sb('qs%d' % d, [64, 2, 128]) for d in range(2)]
    ob = [kb.sb('ob%d' % d, [64, 2, 128]) for d in range(2)]
    obl = [kb.sb('obl%d' % d, [64, 2, 128]) for d in range(2)]
    order = [list(range(NG)), list(range(NCTX - 1, -1, -1)) + list(range(NG - 1, NCTX - 1, -1))]
    for s in range(NG):
        slot = s % NB
        for d in range(2):
            g = order[d][s]
            emit = g >= NCTX
            dsl = slice(2 * d, 2 * d + 2)
            gk = ('gop', d, slot); lk = ('lop', d, slot)
            S.dma('sp', U0b[d][slot][:], G_U0[g][:, dsl, :], reads=[('G_U0', g)], writes=[gk])
            S.dma('sp', KCb[d][slot][:], G_KC[g][:, dsl, :], reads=[('G_KC', g)], writes=[gk])
            S.dma('sp', KDb[d][slot][:], G_KD[g][:, dsl, :], reads=[('G_KD', g)], writes=[gk])
            if emit:
                S.dma('sp', ATb[d][slot][:], G_AT[g][:, dsl, :], reads=[('G_AT', g)], writes=[gk])
                S.dma('sp', QTb[d][slot][:], G_QT[g], reads=[('G_QT', st['g0of'](g), 0), ('G_QT', st['g0of'](g), 1)], writes=[gk])
            S.dma('sp', LKDb[d][slot][:], L_KD[g][:, dsl, :], reads=[('L_KD', g)], writes=[lk])
            S.dma('sp', LVb[d][slot][:], L_V[g], reads=[('L_V', g)], writes=[lk])
            if emit:
                S.dma('sp', LATb[d][slot][:], L_AT[g][:, dsl, :], reads=[('L_AT', g)], writes=[lk])
                S.dma('sp', LQDb[d][slot][:], L_QD[g][:, dsl, :], reads=[('L_QD', g)], writes=[lk])
            ps1, p1k = kb.psum()
            for hl in range(2):
                i = 2 * d + hl
                kb.mm(ps1[0:64, hl * 128:(hl + 1) * 128], KCb[d][slot][:, hl, :], Sg[:, i, :], True, True, [gk, ('Sg', i)], [p1k])
            if emit:
                pso, pok = kb.psum()
                for hl in range(2):
                    i = 2 * d + hl
                    kb.mm(pso[0:64, hl * 128:(hl + 1) * 128], QTb[d][slot][:, hl, :], Sg[:, i, :], True, True, [gk, ('Sg', i)], [pok])
            kb.tt(ub[d][:].rearrange("p a b -> p (a b)"), U0b[d][slot][:].rearrange("p a b -> p (a b)"), ps1[0:64, 0:256], ALU.subtract, [gk, p1k, ('ub', d)], [('ub', d)])
            if emit:
                for hl in range(2):
                    i = 2 * d + hl
                    kb.ts(qs[d][:, hl, :], pso[0:64, hl * 128:(hl + 1) * 128], GEGC[:, g, i:i + 1], None, ALU.mult, None, [pok, 'GEGC', ('qs', d)], [('qs', d)], eng='pool' if False else 'dve')
                pso2, po2k = kb.psum()
                for hl in range(2):
                    kb.mm(pso2[0:64, hl * 128:(hl + 1) * 128], ATb[d][slot][:, hl, :], ub[d][:, hl, :], True, True, [gk, ('ub', d)], [po2k])
                kb.tt(ob[d][:].rearrange("p a b -> p (a b)"), qs[d][:].rearrange("p a b -> p (a b)"), pso2[0:64, 0:256], ALU.add, [('qs', d), po2k, ('ob', d)], [('ob', d)])
                kb.store(OG[g - NCTX][:, dsl, :], ob[d][:], ('ob', d), ('OG', g, d), final=False)
            pss, psk = kb.psum()
            for hl in range(2):
                kb.mm(pss[0:128, hl * 128:(hl + 1) * 128], KDb[d][slot][:, hl, :], ub[d][:, hl, :], True, True, [gk, ('ub', d)], [psk])
            for hl in range(2):
                i = 2 * d + hl
                kb.stt(Sg[:, i, :], Sg[:, i, :], GGL[:, g, i:i + 1], pss[0:128, hl * 128:(hl + 1) * 128], ALU.mult, ALU.add, [psk, 'GGL', ('Sg', i)], [('Sg', i)])
            if emit:
                pso, pok = kb.psum()
                for hl in range(2):
                    i = 2 * d + hl
                    kb.mm(pso[0:64, hl * 128:(hl + 1) * 128], LQDb[d][slot][:, hl, :], Sl[:, i, :], True, False, [lk, ('Sl', i)], [pok])
                    kb.mm(pso[0:64, hl * 128:(hl + 1) * 128], LATb[d][slot][:, hl, :], LVb[d][slot][:, hl, :], False, True, [lk], [pok])
                kb.cp(obl[d][:].rearrange("p a b -> p (a b)"), pso[0:64, 0:256], [pok, ('obl', d)], [('obl', d)], eng='act')
                kb.store(OL[g - NCTX][:, dsl, :], obl[d][:], ('obl', d), ('OL', g, d), final=False)
            pss, psk = kb.psum()
            for hl in range(2):
                kb.mm(pss[0:64, hl * 128:(hl + 1) * 128], LKDb[d][slot][:, hl, :], LVb[d][slot][:, hl, :], True, True, [lk], [psk])
            for hl in range(2):
                i = 2 * d + hl
                kb.stt(Sl[:, i, :], Sl[:, i, :], LGL[:, g, i:i + 1], pss[0:64, hl * 128:(hl + 1) * 128], ALU.mult, ALU.add, [psk, 'LGL', ('Sl', i)], [('Sl', i)])
    FB = []
    for i in range(2):
        FB.append(dict(Ob=kb.sb('Ob%d' % i, [128, 4, 128]), osum=kb.sb('osum%d' % i, [128, 2, 128]), osq=kb.sb('osq%d' % i, [128, 2, 128]), ss=kb.sb('ss%d' % i, [128, 2]),
                       Zb=kb.sb('Zb%d' % i, [128, 2, 128]), Yb=kb.sb('Yb%d' % i, [128, 2, 128])))
    cnt = 0
    for which, (OD, zoff, yoff, ncol) in enumerate(((OG, 0, kb.yoffs[0], 0), (OL, 2, kb.yoffs[1], 1))):
        nm = 'OG' if which == 0 else 'OL'
        for tg in range(NLAT // 2):
            fb = FB[cnt % 2]; sx = 'f%d' % (cnt % 2); cnt += 1
            Ob, osum, osq, ss, Zb, Yb = fb['Ob'], fb['osum'], fb['osq'], fb['ss'], fb['Zb'], fb['Yb']
            toks = slice(tg * 128, (tg + 1) * 128)
            rk = [(nm, NCTX + 2 * tg + cc, d) for cc in range(2) for d in range(2)]
            S.dma('sp', Ob[:], OD[2 * tg:2 * tg + 2].rearrange("c p i e -> (c p) i e"), reads=rk, writes=['Ob' + sx])
            S.dma('sp', Zb[:], ZT[zoff:zoff + 2, :, toks].rearrange("h e t -> e h t"), reads=[('ZT', zoff + h, (tg * 128 // TT) * TT) for h in range(2)], writes=['Zb' + sx])
            kb.tt(osum[:], Ob[:, 0:2, :], Ob[:, 2:4, :], ALU.add, ['Ob' + sx, 'osum' + sx], ['osum' + sx])
            kb.tt(osq[:], osum[:], osum[:], ALU.mult, ['osum' + sx, 'osq' + sx], ['osq' + sx], eng='pool')
            S.op('dve', (lambda e, ss=ss, osq=osq: e.tensor_reduce(out=ss[:], in_=osq[:], axis=AX.X, op=ALU.add)), ['osq' + sx, 'ss' + sx], ['ss' + sx])
            kb.act(ss[:], ss[:], AF.Sqrt, ['ss' + sx, 'eps'], ['ss' + sx], bias=kb.eps_ap, scale=1.0 / 128)
            kb.recip(ss[:], ss[:], ['ss' + sx], ['ss' + sx])
            kb.tt(osum[:], osum[:], ss[:].unsqueeze(2).to_broadcast([128, 2, 128]), ALU.mult, ['osum' + sx, 'ss' + sx], ['osum' + sx])
            psY, pYk = kb.psum()
            for hl in range(2):
                kb.mm(psY[0:128, hl * 128:(hl + 1) * 128], osum[:, hl, :], ident[:, :], True, True, ['osum' + sx, 'ident'], [pYk])
            kb.stt(Yb[:].rearrange("p a b -> p (a b)"), psY[0:128, 0:256], hnws[:, ncol:ncol + 1], Zb[:].rearrange("p a b -> p (a b)"), ALU.mult, ALU.mult, [pYk, 'hnw', 'Zb' + sx, 'Yb' + sx], ['Yb' + sx])
            kb.store(YT[yoff:yoff + 256, toks].rearrange("(h e) t -> e h t", e=128), Yb[:], 'Yb' + sx, ('YT', which, tg))


def mixa_inputs(inp, b, hh):
    e = 0
    W = inp['ab_w_in'][e]
    offs = np.concatenate([[0], np.cumsum(AB_SIZES)])

    def cols(seg, start, n):
        return list(range(offs[seg] + start, offs[seg] + start + n))
    blocks = []
    for seg in (0, 1, 2):
        for hl in range(2):
            blocks.append(cols(seg, (2 * hh + hl) * 128, 128))
    for hl in range(2):
        blocks.append(cols(3, (2 * hh + hl) * 128, 128))
    for seg in (6, 7):
        for hl in range(2):
            blocks.append(cols(seg, (2 * hh + hl) * 64, 64))
    for seg in (8, 9):
        for hl in range(2):
            blocks.append(cols(seg, (2 * hh + hl) * 128, 128))
    for d in range(2):
        blocks.append(cols(10, d * 16, 16))
    Wp = np.zeros((D, 18 * 128), np.float32)
    for bi, cl in enumerate(blocks):
        Wp[:, bi * 128:bi * 128 + len(cl)] = W[:, cl]
    gcols = [offs[4] + d * 4 + 2 * hh + hl for d in range(2) for hl in range(2)] + [offs[5] + d * 4 + 2 * hh + hl for d in range(2) for hl in range(2)]
    wg = np.ascontiguousarray(W[:, gcols].reshape(KC, 128, 8).transpose(1, 0, 2))
    cw = inp['ab_conv_w'][e]
    taps = np.zeros((128, 6, 3), np.float32)
    for seg in range(3):
        for hl in range(2):
            ch0 = seg * 512 + (2 * hh + hl) * 128
            taps[:, seg * 2 + hl, :] = cw[:, ch0:ch0 + 128].T
    al = np.array([inp['gdn_a_log'][e][d, 2 * hh + hl] for d in range(2) for hl in range(2)], np.float32)
    dtb = np.array([inp['gdn_dt_bias'][e][d, 2 * hh + hl] for d in range(2) for hl in range(2)], np.float32)
    gconst = np.zeros((64, 2, 32), np.float32)
    gconst[:, 0, :] = np.tile(al, 8)[None, :]
    gconst[:, 1, :] = np.tile(dtb, 8)[None, :]
    gwb = np.zeros((17, 2, 128), np.float32)
    hs = slice(2 * hh * 64, (2 * hh + 2) * 64)
    for d in range(2):
        gwb[0:16, d, :] = inp['gla_gate_w'][e][d][:, hs]
        gwb[16, d, :] = inp['gla_gate_b'][e][d][hs]
    cm, _ = mixa_consts()
    return dict(
        XT=np.ascontiguousarray(inp['x'][b].T), CXT=np.ascontiguousarray(inp['ctx'][b].T),
        cT=np.ascontiguousarray(np.stack([fm_vec(inp['c'][b]), fm_vec(inp['c_ctx'])], axis=-1)),
        modw=blk_w(np.ascontiguousarray(inp['mod_w'][0][:, 0:2048]), 8), modb=fm_vec(inp['mod_b'][0][0:2048]), nw=fm_vec(inp['norm1_w'][0]),
        wl=blk_w(Wp, 8), wg=wg, taps=taps, gconst=gconst, gwb=gwb,
        hnw=np.ascontiguousarray(np.stack([inp['gdn_norm_w'][e], inp['gla_norm_w'][e]], axis=-1)), cm=cm, ident=np.eye(128, dtype=np.float32))


LSEQ = 8192
NFFT = 16384
CG = 8
MAGIC = 12582912.0


def hy_consts():
    n = np.arange(128, dtype=np.float64)
    th = 2 * np.pi * np.outer(n, n) / 128.0
    Fre, Fim = np.cos(th), -np.sin(th)
    tw = 2 * np.pi * np.outer(n, n) / NFFT
    Tre, Tim = np.cos(tw), -np.sin(tw)
    F64c = np.zeros((128, 256)); F64c[:64, :128] = Fre[:64]; F64c[:64, 128:] = Fim[:64]
    G1 = np.concatenate([Fre, -Fim], 1); G2 = np.concatenate([Fim, Fre], 1)
    fc = np.concatenate([F64c, Fre, Fim, G1, G2, Tre, Tim], 1).astype(np.float32)
    l = LSEQ
    t = np.linspace(0.0, 1.0, l, dtype=np.float32)[:, None]
    w = (np.float32(2.0 * math.pi / l) * np.arange(l, dtype=np.float32))[:, None]
    f = np.linspace(1e-4, 15, 16, dtype=np.float32)[None, :]
    zp = np.concatenate([t, np.cos(f * w), -np.sin(f * w)], axis=-1).astype(np.float32).T
    deltas = np.abs(np.linspace(math.log(1e-2) / 1.5, math.log(1e-2) / 0.3, D, dtype=np.float32))
    win = (np.exp(-t * deltas[None, :]) + np.float32(0.05)).astype(np.float32).T
    return np.ascontiguousarray(fc), np.ascontiguousarray(zp), np.ascontiguousarray(win)


def build_hyc(stage=99, ngroups=None, nb=4, nblk=1, kb=None):
    kb = kb or KB()
    S = kb.S
    L = LSEQ
    NCH = nblk * 128
    if ngroups is None:
        ngroups = NCH // CG
    UT3 = kb.inp('UT3', [3, NCH, nb, L]); tapsd = kb.inp('taps', [128, nblk, 3, 3]); zpd = kb.inp('zp', [33, L]); wind = kb.inp('win', [NCH, L])
    w1d = kb.inp('w1', [33, 64]); w23d = kb.inp('w23', [64, 2, 64]); bfrd = kb.inp('bfr', [64, 4]); fod = kb.inp('fo', [64, 4, NCH]); skipd = kb.inp('skip', [128, nblk, 2])
    fcd = kb.inp('fc', [128, 1280])
    Z2T = kb.outp('Z2T', [NCH, nb, L])
    UC = kb.scratch('UC', [3, nb, NCH, L]); HFs = kb.scratch('HFs', [4, NCH, L])
    kb.init_psum(8)
    fc = kb.sb('fc_s', [128, 1280]); kb.load(fc[:], fcd[:, :], 'fc')
    F64c = fc[0:64, 0:256]; F_re = fc[:, 256:384]; F_im = fc[:, 384:512]; G1 = fc[:, 512:768]; G2 = fc[:, 768:1024]; T_re = fc[:, 1024:1152]; T_im = fc[:, 1152:1280]
    taps = kb.sb('taps_s', [128, nblk, 3, 3]); w1 = kb.sb('w1s', [33, 64]); w23 = kb.sb('w23s', [64, 2, 64]); bfr = kb.sb('bfrs', [64, 4]); fo = kb.sb('fos', [64, 4, NCH]); skip = kb.sb('skips', [128, nblk, 2])
    kb.load(taps[:], tapsd[:, :, :, :], 'taps'); kb.load(w1[:], w1d[:, :], 'w1'); kb.load(w23[:], w23d[:, :, :], 'w23'); kb.load(bfr[:], bfrd[:, :], 'bfr'); kb.load(fo[:], fod[:, :, :], 'fo'); kb.load(skip[:], skipd[:, :, :], 'skip')
    frb = kb.sb('frb', [64, 3])
    kb.tt(frb[:], bfr[:, 0:3], bfr[:, 3:4].to_broadcast([64, 3]), ALU.mult, ['bfr'], ['frb'])
    aoff_p3 = kb.aoff
    CW = 2048
    Uin = kb.sb('Uin', [128, CW + 2]); Uout = kb.sb('Uout', [128, CW])
    uckeys = []
    for seg in range(3):
        for b in range(nb):
            for cb in range(nblk):
                chs = slice(cb * 128, (cb + 1) * 128)
                for t0 in range(0, L, CW):
                    lo = max(t0 - 1, 0); hi = min(t0 + CW + 1, L)
                    if t0 == 0:
                        kb.memset(Uin[:, 0:1], 0.0, ['Uin'], eng='dve')
                    if t0 + CW == L:
                        kb.memset(Uin[:, CW + 1:CW + 2], 0.0, ['Uin'], eng='dve')
                    S.dma('sp', Uin[:, lo - (t0 - 1):hi - (t0 - 1)], UT3[seg, chs, b, lo:hi], writes=['Uin'])
                    kb.ts(Uout[:], Uin[:, 1:CW + 1], taps[:, cb, seg, 1:2], None, ALU.mult, None, ['Uin', 'taps', 'Uout'], ['Uout'])
                    kb.stt(Uout[:], Uin[:, 0:CW], taps[:, cb, seg, 0:1], Uout[:], ALU.mult, ALU.add, ['Uin', 'taps', 'Uout'], ['Uout'])
                    kb.stt(Uout[:], Uin[:, 2:CW + 2], taps[:, cb, seg, 2:3], Uout[:], ALU.mult, ALU.add, ['Uin', 'taps', 'Uout'], ['Uout'])
                    kb.store(UC[seg, b, chs, t0:t0 + CW], Uout[:], 'Uout', ('UC', seg, b, cb, t0), final=False)
                    uckeys.append(('UC', seg, b, cb, t0))
    zp = kb.sb('zp_s', [33, TT]); aa = kb.sb('aa', [64, TT]); tq = kb.sb('tq', [64, TT]); hd = [kb.sb('hd%d' % i, [64, TT]) for i in range(2)]
    hft = kb.sb('hft', [128, 2, TT]); wt = kb.sb('wt', [128, TT])

    def sin_layer(ps, pk, li, out, okey):
        kb.act(aa[:], ps[0:64, :TT], AF.Identity, [pk, 'bfr', 'frb'], ['aa'], bias=frb[:, li:li + 1], scale=bfr[:, 3:4])
        kb.ts(tq[:], aa[:], 1.0 / (2 * math.pi), MAGIC, ALU.mult, ALU.add, ['aa'], ['tq'])
        kb.ts(tq[:], tq[:], MAGIC, -2 * math.pi, ALU.subtract, ALU.mult, ['tq'], ['tq'])
        kb.tt(aa[:], aa[:], tq[:], ALU.add, ['aa', 'tq'], ['aa'])
        kb.ts(aa[:], aa[:], 3.141592, -3.141592, ALU.min, ALU.max, ['aa'], ['aa'])
        kb.act(out, aa[:], AF.Sin, ['aa'], [okey])
    hfkeys = []
    for ti in range(L // TT):
        tsl = slice(ti * TT, (ti + 1) * TT)
        kb.load(zp[:], zpd[:, tsl], 'zp')
        ps, pk = kb.psum()
        kb.mm(ps[0:64, :TT], w1[:, :], zp[:, :], True, True, ['w1', 'zp'], [pk])
        sin_layer(ps, pk, 0, hd[0][:], 'hd0')
        ps, pk = kb.psum()
        kb.mm(ps[0:64, :TT], w23[:, 0, :], hd[0][:], True, True, ['w23', 'hd0'], [pk])
        sin_layer(ps, pk, 1, hd[1][:], 'hd1')
        ps, pk = kb.psum()
        kb.mm(ps[0:64, :TT], w23[:, 1, :], hd[1][:], True, True, ['w23', 'hd1'], [pk])
        sin_layer(ps, pk, 2, hd[0][:], 'hd0')
        for cb in range(nblk):
            chs = slice(cb * 128, (cb + 1) * 128)
            kb.load(wt[:], wind[chs, tsl], 'wt')
            for f in range(4):
                ps, pk = kb.psum()
                kb.mm(ps[:, :TT], fo[:, f, chs], hd[0][:], True, True, ['fo', 'hd0'], [pk])
                hk = ('hft', f % 2)
                kb.tt(hft[:, f % 2, :], ps[:, :TT], wt[:], ALU.mult, [pk, 'wt', hk], [hk])
                if ti == 0:
                    if f < 2:
                        kb.tt(hft[:, f % 2, 0:1], hft[:, f % 2, 0:1], skip[:, cb, f:f + 1], ALU.add, [hk, 'skip'], [hk])
                    else:
                        kb.memset(hft[:, f % 2, 0:1], 0.0, [hk], eng='dve')
                kb.store(HFs[f, chs, tsl], hft[:, f % 2, :], hk, ('HFs', f, cb, ti), final=False)
                hfkeys.append(('HFs', f, cb, ti))
    S.barrier()
    kb.aoff = aoff_p3
    Tre_b = T_re.unsqueeze(1).to_broadcast([128, CG, 128]); Tim_b = T_im.unsqueeze(1).to_broadcast([128, CG, 128])

    def mk_lane(tag):
        return dict(tag=tag, As=kb.sb('As' + tag, [128, CG, 2, 128]), Bt=kb.sb('Bt' + tag, [128, CG, 3, 128]),
                    tw=[kb.sb('tw%d%s' % (i, tag), [128, CG, 128]) for i in range(4)])
    LF, LD = mk_lane('F'), mk_lane('D')
    Xgs = [kb.sb('Xg%d' % i, [64, 3, CG, 128]) for i in range(2)]; Ys = kb.sb('Ys', [128, CG, 2, 128])
    Hx = [kb.sb('Hx%d' % i, [64, CG, 128]) for i in range(2)]
    Htmp = kb.sb('Htmp', [128, CG, 2, 128])
    KSp = [kb.sb('KSp%d' % i, [128, 2, CG, 2, 128]) for i in range(2)]

    def fwd_fft(ln, Xv, xkey, dst, dkey):
        t = ln['tag']; As = ln['As']; Bt = ln['Bt']; tw = ln['tw']
        ak, bk = 'As' + t, 'Bt' + t
        tk = ['tw%d%s' % (i, t) for i in range(4)]
        for c0 in range(0, CG, 2):
            ps, pk = kb.psum()
            for c in (c0, c0 + 1):
                kb.mm(ps[:, (c - c0) * 256:(c - c0 + 1) * 256], Xv[:, c, :], F64c, True, True, [xkey, 'fc'], [pk])
            kb.cp(As[:, c0:c0 + 2, :, :].rearrange("p a b c -> p (a b c)"), ps[:, :], [pk, ak], [ak], eng='act')
            yield
        Are, Aim = As[:, :, 0, :], As[:, :, 1, :]
        kb.tt(tw[0][:], Are, Tre_b, ALU.mult, [ak, 'fc', tk[0]], [tk[0]])
        kb.tt(tw[1][:], Aim, Tim_b, ALU.mult, [ak, 'fc', tk[1]], [tk[1]], eng='pool')
        kb.tt(tw[2][:], Are, Tim_b, ALU.mult, [ak, 'fc', tk[2]], [tk[2]], eng='pool')
        kb.tt(tw[3][:], Aim, Tre_b, ALU.mult, [ak, 'fc', tk[3]], [tk[3]])
        yield
        kb.tt(Bt[:, :, 1, :], tw[0][:], tw[1][:], ALU.subtract, [tk[0], tk[1], bk], [bk])
        kb.tt(Bt[:, :, 2, :], tw[2][:], tw[3][:], ALU.add, [tk[2], tk[3], bk], [bk], eng='pool')
        yield
        kb.ts(Bt[:, :, 0, :], Bt[:, :, 2, :], -1.0, None, ALU.mult, None, [bk], [bk])
        yield
        for c0 in range(0, CG, 2):
            ps, pk = kb.psum()
            o = ps[:, :].rearrange("p (c x) -> p c x", c=2)
            kb.mm(o, F_re, Bt[:, c0:c0 + 2, 1:3, :].rearrange("p c a b -> p c (a b)"), True, False, [bk, 'fc'], [pk])
            kb.mm(o, F_im, Bt[:, c0:c0 + 2, 0:2, :].rearrange("p c a b -> p c (a b)"), False, True, [bk, 'fc'], [pk])
            kb.cp(dst[:, c0:c0 + 2, :, :].rearrange("p a b c -> p (a b c)"), ps[:, :], [pk, dkey], [dkey], eng='act')
            yield

    def conv(ln, zidx, gidx, Kt, kkey, order, Xg, xs):
        t = ln['tag']; As = ln['As']; Bt = ln['Bt']; tw = ln['tw']
        ak, bk = 'As' + t, 'Bt' + t
        tk = ['tw%d%s' % (i, t) for i in range(4)]
        yield from fwd_fft(ln, Xg[:, zidx, :, :], ('Xg', xs, zidx), As, ak)
        Xre, Xim = As[:, :, 0, :], As[:, :, 1, :]
        Kre, Kim = Kt[:, order, :, 0, :], Kt[:, order, :, 1, :]
        kb.tt(tw[0][:], Xre, Kre, ALU.mult, [ak, kkey, tk[0]], [tk[0]])
        kb.tt(tw[1][:], Xim, Kim, ALU.mult, [ak, kkey, tk[1]], [tk[1]], eng='pool')
        kb.tt(tw[2][:], Xre, Kim, ALU.mult, [ak, kkey, tk[2]], [tk[2]], eng='pool')
        kb.tt(tw[3][:], Xim, Kre, ALU.mult, [ak, kkey, tk[3]], [tk[3]])
        yield
        kb.tt(Ys[:, :, 0, :], tw[0][:], tw[1][:], ALU.subtract, [tk[0], tk[1], 'Ys'], ['Ys'])
        kb.tt(Ys[:, :, 1, :], tw[2][:], tw[3][:], ALU.add, [tk[2], tk[3], 'Ys'], ['Ys'], eng='pool')
        yield
        Cs = As
        for c0 in range(0, CG, 2):
            ps, pk = kb.psum()
            for c in (c0, c0 + 1):
                o = ps[:, (c - c0) * 256:(c - c0 + 1) * 256]
                kb.mm(o, Ys[:, c, 0, :], G1, True, False, ['Ys', 'fc'], [pk])
                kb.mm(o, Ys[:, c, 1, :], G2, False, True, ['Ys', 'fc'], [pk])
            kb.cp(Cs[:, c0:c0 + 2, :, :].rearrange("p a b c -> p (a b c)"), ps[:, :], [pk, ak], [ak], eng='act')
            yield
        Cre, Cim = Cs[:, :, 0, :], Cs[:, :, 1, :]
        Cp = Bt
        kb.tt(tw[0][:], Cre, Tre_b, ALU.mult, [ak, 'fc', tk[0]], [tk[0]])
        kb.tt(tw[1][:], Cim, Tim_b, ALU.mult, [ak, 'fc', tk[1]], [tk[1]], eng='pool')
        kb.tt(tw[2][:], Cim, Tre_b, ALU.mult, [ak, 'fc', tk[2]], [tk[2]], eng='pool')
        kb.tt(tw[3][:], Cre, Tim_b, ALU.mult, [ak, 'fc', tk[3]], [tk[3]])
        yield
        kb.tt(Cp[:, :, 0, :], tw[0][:], tw[1][:], ALU.add, [tk[0], tk[1], bk], [bk])
        kb.tt(Cp[:, :, 1, :], tw[2][:], tw[3][:], ALU.subtract, [tk[2], tk[3], bk], [bk], eng='pool')
        yield
        for c0 in range(0, CG, 4):
            ps, pk = kb.psum()
            o = ps[0:64, :].rearrange("p (c x) -> p c x", c=4)
            kb.mm(o, F_re[:, 0:64], Cp[:, c0:c0 + 4, 0, :], True, False, [bk, 'fc'], [pk])
            kb.mm(o, F_im[:, 0:64], Cp[:, c0:c0 + 4, 1, :], False, True, [bk, 'fc'], [pk])
            kb.tt(Xg[:, zidx, c0:c0 + 4, :].rearrange("p a b -> p (a b)"), ps[0:64, :], Xg[:, gidx, c0:c0 + 4, :].rearrange("p a b -> p (a b)"), ALU.mult, [pk, ('Xg', xs, gidx), ('Xg', xs, zidx)], [('Xg', xs, zidx)])
            yield

    def filter_lane(gi):
        ch = slice(gi * CG, (gi + 1) * CG)
        Kt = KSp[gi % 2]; kkey = ('KSp', gi % 2)
        for order in range(2):
            for di in range(2):
                f = di * 2 + order
                hb = Hx[f % 2]; hk = ('Hx', f % 2)
                S.dma('sp', hb[:], HFs[f, ch, :].rearrange("c (a b) -> a c b", b=128), reads=hfkeys, writes=[hk])
                if di == 0:
                    yield from fwd_fft(LF, hb, hk, Kt[:, order, :, :, :], kkey)
                else:
                    yield from fwd_fft(LF, hb, hk, Htmp, 'Htmp')
                    kb.tt(Kt[:, order, :, 0, :], Kt[:, order, :, 0, :], Htmp[:, :, 0, :], ALU.add, [kkey, 'Htmp'], [kkey])
                    kb.tt(Kt[:, order, :, 1, :], Kt[:, order, :, 1, :], Htmp[:, :, 1, :], ALU.subtract, [kkey, 'Htmp'], [kkey], eng='pool')
                    yield
        kb.ts(Kt[:].rearrange("p a b c d -> p (a b c d)"), Kt[:].rearrange("p a b c d -> p (a b c d)"), 1.0 / NFFT, None, ALU.mult, None, [kkey], [kkey])
        yield

    def data_lane(gi):
        ch = slice(gi * CG, (gi + 1) * CG)
        Kt = KSp[gi % 2]; kkey = ('KSp', gi % 2)
        for b in range(nb):
            xs = (gi * nb + b) % 2
            Xg = Xgs[xs]
            for seg in range(3):
                S.dma('sp', Xg[:, seg, :, :], UC[seg, b, ch, :].rearrange("c (a b) -> a c b", b=128), reads=uckeys, writes=[('Xg', xs, seg)])
            yield from conv(LD, 0, 1, Kt, kkey, 0, Xg, xs)
            yield from conv(LD, 0, 2, Kt, kkey, 1, Xg, xs)
            kb.store(Z2T[ch, b, :].rearrange("c (a b) -> a c b", b=128), Xg[:, 0, :, :], ('Xg', xs, 0), ('Z2T', gi, b))
            yield

    for step in range(ngroups + 1):
        lanes = []
        if step < ngroups:
            lanes.append(filter_lane(step))
        if step >= 1:
            lanes.append(data_lane(step - 1))
        while lanes:
            for g in list(lanes):
                try:
                    next(g)
                except StopIteration:
                    lanes.remove(g)
    return kb.finish()


def hyc_inputs(inp, c0, nblk):
    o = 0
    nch = nblk * 128
    fc, zp, win = hy_consts()
    cw = inp['hy_conv_w'][o]
    taps = np.zeros((128, nblk, 3, 3), np.float32)
    for cb in range(nblk):
        for seg in range(3):
            ch0 = seg * D + c0 + cb * 128
            taps[:, cb, seg, :] = cw[:, ch0:ch0 + 128].T
    fo_full = inp['hy_filt_out'][o].reshape(64, 2, 2, D)
    fo = np.ascontiguousarray(fo_full[:, :, :, c0:c0 + nch].reshape(64, 4, nch))
    sk = inp['hy_skip'][o][:, c0:c0 + nch]
    skip = np.ascontiguousarray(sk.reshape(2, nblk, 128).transpose(2, 1, 0))
    return dict(
        taps=taps, zp=zp, win=np.ascontiguousarray(win[c0:c0 + nch]),
        w1=np.ascontiguousarray(inp['hy_pos_w1'][o]), w23=np.ascontiguousarray(np.stack([inp['hy_pos_w2'][o], inp['hy_pos_w3'][o]], axis=1)),
        bfr=np.ascontiguousarray(np.stack([inp['hy_pos_b1'][o], inp['hy_pos_b2'][o], inp['hy_pos_b3'][o], inp['hy_freq'][o]], axis=1)),
        fo=fo, skip=skip, fc=fc)


ARENA = 53200


def build_fused():
    kb = KB(fused=True, arena_floats=ARENA)
    nc = kb.nc
    L = LSEQ
    XT = nc.dram_tensor('XT', [D, L], F32, kind="ExternalInput").ap()
    CXT = nc.dram_tensor('CXT', [D, 256], F32, kind="ExternalInput").ap()
    Yint = nc.dram_tensor('Yint', [D, L], F32, kind="Internal").ap()
    H0 = nc.dram_tensor('H0int', [D, L], F32, kind="Internal").ap()
    U = nc.dram_tensor('Uint', [3 * D, L], F32, kind="Internal").ap()
    Z2 = nc.dram_tensor('Z2int', [D, 1, L], F32, kind="Internal").ap()
    kb.init_psum(8)
    for hh in range(2):
        kb.next_stage('a%d_' % hh, {'XT': XT, 'CXT': CXT, 'YT': Yint})
        kb.yoffs = (hh * 256, 512 + hh * 256)
        build_mixa(kb=kb)
    kb.next_stage('b_', {'HT': XT, 'YT': Yint, 'HO': H0, 'UT': U})
    build_post(L, last=False, kb=kb)
    kb.next_stage('c_', {'UT3': U.rearrange("(s c) (b t) -> s c b t", s=3, b=1), 'Z2T': Z2})
    build_hyc(nb=1, nblk=8, kb=kb)
    kb.next_stage('d_', {'HT': H0, 'YT': Z2.rearrange("c b t -> c (b t)")})
    kb.dyn_tok = True
    kb.dyn_half = L // 2
    build_post(L // 2, last=True, kb=kb)
    kb.stage_amax = kb.amax
    return kb.finish(force=True)


def fused_inputs(inp, core):
    b, r = core // 2, core % 2
    m = {}
    for hh in range(2):
        a = mixa_inputs(inp, b, hh)
        m['XT'] = a.pop('XT'); m['CXT'] = a.pop('CXT')
        for k, v in a.items():
            m['a%d_%s' % (hh, k)] = v
    mw, mb = inp['mod_w'], inp['mod_b']
    modw = np.concatenate([mw[0][:, 2 * D:6 * D], mw[1][:, 0:2 * D]], axis=1)
    modb = np.concatenate([mb[0][2 * D:6 * D], mb[1][0:2 * D]])
    bd = dict(wo=blk_w(inp['ab_w_out'][0], 8), modw=blk_w(modw, 8), modb=fm_vec(modb), cT=fm_vec(inp['c'][b]),
              nw=np.ascontiguousarray(np.stack([fm_vec(inp['norm2_w'][0]), fm_vec(inp['norm1_w'][1])], axis=-1)),
              w1=blk_w(inp['ffn_w1'][0], 8), w3=blk_w(inp['ffn_w3'][0], 8), w2=blk_w(inp['ffn_w2'][0], 22),
              wn=blk_w(inp['hy_w_in'][0], 8))
    for k, v in bd.items():
        m['b_' + k] = v
    for k, v in hyc_inputs(inp, 0, 8).items():
        m['c_' + k] = v
    dd = dict(wo=blk_w(inp['hy_w_out'][0], 8), modw=blk_w(np.ascontiguousarray(mw[1][:, 2 * D:6 * D]), 8), modb=fm_vec(mb[1][2 * D:6 * D]), cT=fm_vec(inp['c'][b]),
              nw=np.ascontiguousarray(np.stack([fm_vec(inp['norm2_w'][1]), fm_vec(inp['final_norm_w'])], axis=-1)),
              w1=blk_w(inp['ffn_w1'][1], 8), w3=blk_w(inp['ffn_w3'][1], 8), w2=blk_w(inp['ffn_w2'][1], 22))
    for k, v in dd.items():
        m['d_' + k] = v
    return m


def kernel(**inputs):
    inp = {k: np.asarray(v, dtype=np.float32) for k, v in inputs.items()}
    B, L = 4, LSEQ
    cores = list(range(NCORES))
    nc = build_fused()
    shared = {}
    maps = []
    for c in cores:
        m = fused_inputs(inp, c)
        for k in list(m.keys()):
            if k in shared and shared[k].shape == m[k].shape and k not in ('XT', 'CXT') and not k.endswith('cT') and not k.startswith('a'):
                m[k] = shared[k]
            else:
                shared.setdefault(k, m[k])
        maps.append(m)
    res = run_bass_kernel_spmd(nc, maps, core_ids=cores)
    out = np.zeros((B, L, D), np.float32)
    NT = L // 2
    for c in cores:
        b, r = c // 2, c % 2
        out[b, r * NT:(r + 1) * NT, :] = res.results[c]['d_HO'].T
    return out
```

```python
import math
import os
import numpy as np
DBG_CUT = 99
from contextlib import ExitStack
import concourse.bass as bass
import concourse.mybir as mybir
from concourse.bass_utils import run_bass_kernel_spmd

F32 = mybir.dt.float32
AF = mybir.ActivationFunctionType
ALU = mybir.AluOpType
AX = mybir.AxisListType

D = 1024
KC = 8
DFF = 2816
FC = 22
EPS = 1e-6
NCORES = 8


class Sched:
    ENG = ['pe', 'act', 'dve', 'pool', 'sp']

    def __init__(self, nc, ctx):
        self.nc = nc
        self.ctx = ctx
        self.sem = {e: ctx.enter_context(nc.semaphore('s_' + e)) for e in self.ENG if e != 'sp'}
        self.cnt = {e: 0 for e in self.ENG}
        self.waited = {e: {} for e in self.ENG}
        self.prog = {e: [] for e in self.ENG}
        self.lastw = {}
        self.readers = {}
        self.dsem = {}

    def semh(self, k):
        return self.sem[k] if isinstance(k, str) else self.dsem[k][0]

    def _deps(self, eng, reads, writes):
        need = {}

        def add(tok):
            if tok is not None and need.get(tok[0], 0) < tok[1]:
                need[tok[0]] = tok[1]
        for r in reads:
            add(self.lastw.get(r))
        for w in writes:
            add(self.lastw.get(w))
            for t in self.readers.get(w, ()):
                add(t)
        out = []
        for k, v in need.items():
            if eng == 'pe' and k == 'pe':
                continue
            if self.waited[eng].get(k, 0) >= v:
                continue
            self.waited[eng][k] = v
            out.append((k, v))
        return out

    def op(self, eng, fn, reads=(), writes=()):
        writes = list(writes) + [r for r in reads if isinstance(r, tuple) and r[0] == 'ps' and r not in writes]
        waits = self._deps(eng, reads, writes)
        self.cnt[eng] += 1
        tok = (eng, self.cnt[eng])
        self.prog[eng].append((waits, fn, self.sem[eng], 1))
        for w in writes:
            self.lastw[w] = tok
            self.readers[w] = []
        for r in reads:
            if r not in writes:
                self.readers.setdefault(r, []).append(tok)
        return tok

    def dma(self, q, out, in_, reads=(), writes=(), semkey=None, in_fn=None):
        if semkey is None:
            semkey = writes[0]
        semkey = ('dma', semkey)
        if semkey not in self.dsem:
            self.dsem[semkey] = [self.ctx.enter_context(self.nc.semaphore('d%d' % len(self.dsem))), 0]
        waits = self._deps(q, reads, writes)
        self.dsem[semkey][1] += 16
        tok = (semkey, self.dsem[semkey][1])
        self.prog[q].append((waits, (lambda e, out=out, in_=in_, in_fn=in_fn: e.dma_start(out=out, in_=(in_fn(e) if in_fn is not None else in_))), self.dsem[semkey][0], 16))
        for w in writes:
            self.lastw[w] = tok
            self.readers[w] = []
        for r in reads:
            self.readers.setdefault(r, []).append(tok)
        return tok

    def barrier(self):
        toks = [(e, self.cnt[e]) for e in self.sem if self.cnt[e] > 0] + [(k, v[1]) for k, v in self.dsem.items() if v[1] > 0]
        for eng in self.ENG:
            waits = []
            for k, v in toks:
                if self.waited[eng].get(k, 0) < v:
                    self.waited[eng][k] = v
                    waits.append((k, v))
            self.prog[eng].append((waits, None, None, 0))

    def wait_keys(self, eng, keys):
        waits = self._deps(eng, list(keys), ())
        self.prog[eng].append((waits, None, None, 0))

    def emit(self):
        engs = {'pe': 'tensor', 'act': 'scalar', 'dve': 'vector', 'pool': 'gpsimd', 'sp': 'sync'}
        with self.nc.allow_non_contiguous_dma(reason="small strided pads / per-chunk layouts"), self.nc.Block() as block:
            for e, name in engs.items():
                prog = self.prog[e]
                if not prog:
                    continue

                def body(eng, prog=prog):
                    for waits, fn, sem, inc in prog:
                        for k, v in waits:
                            eng.wait_ge(self.semh(k), v)
                        if fn is not None:
                            fn(eng).then_inc(sem, inc)
                getattr(block, name)(body)


class KB:
    def __init__(self, fused=False, arena_floats=0):
        self.nc = bass.Bass("TRN2", target_bir_lowering=False)
        self.ctx = ExitStack()
        self.S = Sched(self.nc, self.ctx)
        self.ps = []
        self.psi = 0
        self.wbufs = []
        self.wi = 0
        self.outkeys = []
        self.fused = fused
        self.prefix = ''
        self.bind = {}
        self.yoffs = (0, 256)
        self.dyn_tok = False
        self.arena = None
        self.aoff = 0
        self.amax = 0
        if arena_floats:
            self.arena = self.ctx.enter_context(self.nc.sbuf_tensor('arena_all', [128, arena_floats], F32))
            self.asize = arena_floats

    def next_stage(self, prefix, bind=None):
        self.S.barrier()
        self.aoff = 0
        self.wbufs = []
        self.wi = 0
        self.prefix = prefix
        self.bind = dict(bind or {})

    def inp(self, name, shape):
        if name in self.bind:
            return self.bind[name]
        return self.nc.dram_tensor(self.prefix + name, list(shape), F32, kind="ExternalInput").ap()

    def outp(self, name, shape):
        if name in self.bind:
            return self.bind[name]
        return self.nc.dram_tensor(self.prefix + name, list(shape), F32, kind="ExternalOutput").ap()

    def scratch(self, name, shape):
        if name in self.bind:
            return self.bind[name]
        return self.nc.dram_tensor(self.prefix + name, list(shape), F32, kind="Internal").ap()

    def sb(self, name, shape):
        if self.arena is None:
            return self.ctx.enter_context(self.nc.sbuf_tensor(name, list(shape), F32))
        n = 1
        for d in shape[1:]:
            n *= d
        assert self.aoff + n <= self.asize, ('SBUF arena overflow', name, self.aoff, n)
        ap = self.arena[0:shape[0], self.aoff:self.aoff + n]
        self.aoff += n
        self.amax = max(self.amax, self.aoff)
        if len(shape) > 2:
            names = ['d%d' % i for i in range(len(shape) - 1)]
            ap = ap.rearrange("p (%s) -> p %s" % (' '.join(names), ' '.join(names)), **{nm: shape[i + 1] for i, nm in enumerate(names[:-1])})
        return ap

    def init_psum(self, n=8):
        if self.ps:
            return
        for i in range(n):
            self.ps.append(self.ctx.enter_context(self.nc.psum_tensor('ps%d' % i, [128, 512], F32)))

    def psum(self):
        i = self.psi
        self.psi = (self.psi + 1) % len(self.ps)
        return self.ps[i], ('ps', i)

    def init_wpool(self, n, width):
        for i in range(n):
            self.wbufs.append(self.sb('wbuf%d' % i, [128, width]))

    def wload(self, src, width, q='sp'):
        i = self.wi
        self.wi = (self.wi + 1) % len(self.wbufs)
        buf = self.wbufs[i]
        self.S.dma(q, buf[:, 0:width], src, writes=[('w', i)])
        return buf, ('w', i)

    def mm(self, out, lhsT, rhs, start, stop, reads, writes):
        return self.S.op('pe', lambda e: e.matmul(out, lhsT=lhsT, rhs=rhs, start=start, stop=stop), reads, writes)

    def act(self, out, in_, func, reads, writes, bias=None, scale=None, eng='act'):
        kw = {}
        if bias is not None:
            kw['bias'] = bias
        if scale is not None:
            kw['scale'] = scale
        return self.S.op(eng, lambda e: e.activation(out=out, in_=in_, func=func, **kw), reads, writes)

    def tt(self, out, in0, in1, op, reads, writes, eng='dve'):
        return self.S.op(eng, lambda e: e.tensor_tensor(out=out, in0=in0, in1=in1, op=op), reads, writes)

    def ts(self, out, in0, s1, s2, op0, op1, reads, writes, eng='dve'):
        if op1 is None:
            return self.S.op(eng, lambda e: e.tensor_scalar(out=out, in0=in0, scalar1=s1, scalar2=None, op0=op0), reads, writes)
        return self.S.op(eng, lambda e: e.tensor_scalar(out=out, in0=in0, scalar1=s1, scalar2=s2, op0=op0, op1=op1), reads, writes)

    def stt(self, out, in0, scalar, in1, op0, op1, reads, writes, eng='dve'):
        return self.S.op(eng, lambda e: e.scalar_tensor_tensor(out=out, in0=in0, scalar=scalar, in1=in1, op0=op0, op1=op1), reads, writes)

    def cp(self, out, in_, reads, writes, eng='dve'):
        if eng == 'act':
            return self.S.op('act', lambda e: e.copy(out=out, in_=in_), reads, writes)
        return self.S.op(eng, lambda e: e.tensor_copy(out=out, in_=in_), reads, writes)

    def memset(self, ap, val, writes, eng='pool'):
        return self.S.op(eng, lambda e: e.memset(ap, val), (), writes)

    def recip(self, out, in_, reads, writes):
        return self.S.op('dve', lambda e: e.reciprocal(out=out, in_=in_), reads, writes)

    def load(self, dst, src, key, q='sp'):
        return self.S.dma(q, dst, src, writes=[key])

    def load_tok(self, dst, view, t0, T, key):
        if not self.dyn_tok:
            return self.load(dst, view[:, :, t0:t0 + T], key)
        half = self.dyn_half

        def in_fn(e):
            if getattr(self, '_rbase', None) is None:
                self._rbase = e.snap((e.partition_id() % 2) * half, min_val=0, max_val=half)
            return view[:, :, bass.ds(self._rbase + t0, T)]
        return self.S.dma('sp', dst, None, writes=[key], in_fn=in_fn)

    def store(self, dst, src, srckey, dstkey, q='pool', final=True):
        self.S.dma(q, dst, src, reads=[srckey], writes=[dstkey], semkey=srckey)
        if final:
            self.outkeys.append(dstkey)

    def finish(self, force=False):
        if self.fused and not force:
            return None
        self.S.wait_keys('pool', self.outkeys)
        self.S.emit()
        self.ctx.close()
        return self.nc


def emit_modvec(kb, cT, ckey, nv, wl, bl, nblk, out, okey):
    for b in range(nblk):
        wb, wk = kb.wload(wl[b], KC * 128)
        ps, pk = kb.psum()
        for k in range(KC):
            kb.mm(ps[:, 0:nv], wb[:, k * 128:(k + 1) * 128], cT[:, k, :], k == 0, k == KC - 1, [wk, ckey], [pk])
        kb.ts(out[:, b, :], ps[:, 0:nv], bl[:, b:b + 1], None, ALU.add, None, [pk, 'modb'], [okey])


def emit_norm(kb, X, xkey, A, akey, T, ones, gain, shift, gkey, sq, rstd):
    for k in range(KC):
        kb.act(sq[:, k, :T], X[:, k, :T], AF.Square, [xkey], [('sq', k)])
    ps, pk = kb.psum()
    for k in range(KC):
        kb.mm(ps[:, :T], ones[:, :], sq[:, k, :T], k == 0, k == KC - 1, ['ones', ('sq', k)], [pk])
    kb.act(rstd[:, :T], ps[:, :T], AF.Sqrt, [pk, 'eps'], ['rstd'], bias=kb.eps_ap, scale=1.0 / D)
    kb.recip(rstd[:, :T], rstd[:, :T], ['rstd'], ['rstd'])
    for k in range(KC):
        kb.tt(sq[:, k, :T], X[:, k, :T], rstd[:, :T], ALU.mult, [xkey, 'rstd', ('sq', k)], [('sq', k)], eng='dve' if k % 2 == 0 else 'pool')
        kb.act(A[:, k, :T], sq[:, k, :T], AF.Identity, [('sq', k), gkey], [akey], bias=shift[:, k, 0:1], scale=gain[:, k, 0:1])


def emit_linear(kb, wl, nblk, kc, rhs_fn, rkeys, T, evac):
    for b in range(nblk):
        wb, wk = kb.wload(wl[b], kc * 128)
        ps, pk = kb.psum()
        for k in range(kc):
            kb.mm(ps[:, :T], wb[:, k * 128:(k + 1) * 128], rhs_fn(k), k == 0, k == kc - 1, [wk] + rkeys, [pk])
        evac(b, ps, pk)


def emit_ffn(kb, A, akey, H, hkey, G, T, w1l, w3l, w2l, g2, gkey, tmp):
    for j in range(FC):
        w1, k1 = kb.wload(w1l[j], KC * 128)
        w3, k3 = kb.wload(w3l[j], KC * 128)
        p1, pk1 = kb.psum()
        p3, pk3 = kb.psum()
        for k in range(KC):
            kb.mm(p1[:, :T], w1[:, k * 128:(k + 1) * 128], A[:, k, :T], k == 0, k == KC - 1, [k1, akey], [pk1])
        for k in range(KC):
            kb.mm(p3[:, :T], w3[:, k * 128:(k + 1) * 128], A[:, k, :T], k == 0, k == KC - 1, [k3, akey], [pk3])
        tk = ('tmp', j % 2)
        kb.act(tmp[:, j % 2, :T], p1[:, :T], AF.Silu, [pk1], [tk])
        kb.tt(G[:, j, :T], tmp[:, j % 2, :T], p3[:, :T], ALU.mult, [tk, pk3], [('G', j)])
    for i in range(KC):
        w2, k2 = kb.wload(w2l[i], FC * 128)
        ps, pk = kb.psum()
        for j in range(FC):
            kb.mm(ps[:, :T], w2[:, j * 128:(j + 1) * 128], G[:, j, :T], j == 0, j == FC - 1, [k2, ('G', j)], [pk])
        kb.stt(H[:, i, :T], ps[:, :T], g2[:, i, 0:1], H[:, i, :T], ALU.mult, ALU.add, [pk, gkey, hkey], [hkey])


TT = 512


def build_post(ntok, last, kb=None):
    kb = kb or KB()
    nmb = 32 if last else 48
    HT = kb.inp('HT', [D, ntok]); YT = kb.inp('YT', [D, ntok])
    wo = kb.inp('wo', [KC, 128, D]); cTd = kb.inp('cT', [128, KC])
    modw = kb.inp('modw', [nmb, 128, D]); modb = kb.inp('modb', [128, nmb]); nw = kb.inp('nw', [128, KC, 2])
    w1l = kb.inp('w1', [FC, 128, D]); w3l = kb.inp('w3', [FC, 128, D]); w2l = kb.inp('w2', [KC, 128, DFF])
    HO = kb.outp('HO', [D, ntok])
    if not last:
        wn = kb.inp('wn', [24, 128, D]); UT = kb.outp('UT', [3 * D, ntok])
    kb.init_psum(8)
    kb.init_wpool(4, DFF)
    ones = kb.sb('ones', [128, 128]); epst = kb.sb('epst', [128, 1])
    kb.memset(ones[:], 1.0, ['ones']); kb.memset(epst[:], EPS, ['eps'])
    kb.eps_ap = epst[:, 0:1]
    cT = kb.sb('cTs', [128, KC, 1]); modbs = kb.sb('modbs', [128, nmb]); nws = kb.sb('nws', [128, KC, 2])
    mod = kb.sb('mod', [128, nmb, 1])
    kb.load(cT[:, :, 0], cTd[:, :], 'cT'); kb.load(modbs[:], modb[:, :], 'modb'); kb.load(nws[:], nw[:, :, :], 'nw')
    kb.act(cT[:, :, 0], cT[:, :, 0], AF.Silu, ['cT'], ['cT'])
    emit_modvec(kb, cT, 'cT', 1, modw, modbs, nmb, mod, 'mod')
    gains = kb.sb('gains', [128, KC, 2])
    kb.stt(gains[:, :, 0:1], mod[:, 16:24, :], 1.0, nws[:, :, 0:1], ALU.add, ALU.mult, ['mod', 'nw'], ['gains'])
    if not last:
        kb.stt(gains[:, :, 1:2], mod[:, 40:48, :], 1.0, nws[:, :, 1:2], ALU.add, ALU.mult, ['mod', 'nw', 'gains'], ['gains'])
    else:
        kb.cp(gains[:, :, 1:2], nws[:, :, 1:2], ['nw', 'gains'], ['gains'])
    zshift = kb.sb('zshift', [128, KC, 1])
    kb.memset(zshift[:], 0.0, ['zshift'])
    H = kb.sb('H', [128, KC, TT]); Y = kb.sb('Y', [128, KC, TT]); A = kb.sb('A', [128, KC, TT]); sq = kb.sb('sq', [128, KC, TT])
    rstd = kb.sb('rstd', [128, TT]); G = kb.sb('G', [128, FC, TT]); tmp = kb.sb('tmp', [128, 2, TT])
    HTv = HT.rearrange("(k p) t -> p k t", p=128); YTv = YT.rearrange("(k p) t -> p k t", p=128)
    HOv = HO.rearrange("(k p) t -> p k t", p=128)
    if not last:
        UTv = UT.rearrange("(k p) t -> p k t", p=128)
    for ti in range(ntok // TT):
        tsl = slice(ti * TT, (ti + 1) * TT)
        kb.load_tok(H[:], HTv, ti * TT, TT, 'H'); kb.load_tok(Y[:], YTv, ti * TT, TT, 'Y')

        def ev_o(b, ps, pk):
            kb.stt(H[:, b, :], ps[:, :TT], mod[:, b, 0:1], H[:, b, :], ALU.mult, ALU.add, [pk, 'mod', 'H'], ['H'])
        emit_linear(kb, wo, KC, KC, lambda k: Y[:, k, :], ['Y'], TT, ev_o)
        emit_norm(kb, H, 'H', A, 'A', TT, ones, gains[:, :, 0:1], mod[:, 8:16, :], 'gains', sq, rstd)
        emit_ffn(kb, A, 'A', H, 'H', G, TT, w1l, w3l, w2l, mod[:, 24:32, :], 'mod', tmp)
        if not last:
            kb.store(HOv[:, :, tsl], H[:], 'H', ('HO', ti))
            emit_norm(kb, H, 'H', A, 'A', TT, ones, gains[:, :, 1:2], mod[:, 32:40, :], 'gains', sq, rstd)

            def ev_u(b, ps, pk):
                kb.cp(tmp[:, b % 2, :], ps[:, :TT], [pk], [('tmp', b % 2)], eng='act' if b % 2 else 'dve')
                kb.store(UTv[:, b, tsl], tmp[:, b % 2, :], ('tmp', b % 2), ('UT', ti, b))
            emit_linear(kb, wn, 24, KC, lambda k: A[:, k, :], ['A'], TT, ev_u)
        else:
            emit_norm(kb, H, 'H', A, 'A', TT, ones, gains[:, :, 1:2], zshift, 'gains', sq, rstd)
            kb.store(HOv[:, :, tsl], A[:], 'A', ('HO', ti))
    return kb.finish()


def blk_w(W, kc):
    K, N = W.shape
    nb = N // 128
    return np.ascontiguousarray(W.reshape(kc, 128, nb, 128).transpose(2, 1, 0, 3).reshape(nb, 128, kc * 128))


def fm_vec(v):
    return np.ascontiguousarray(v.reshape(-1, 128).T)


CH = 64
NCTX = 4
NLAT = 128
NG = NCTX + NLAT
NEG = -30000.0
AB_SIZES = (512, 512, 512, 512, 8, 8, 256, 256, 512, 512, 32)


def mixa_consts():
    i = np.arange(64)
    P, Fq = i[:, None], i[None, :]
    tri = [(P <= Fq).astype(np.float32), (P >= Fq).astype(np.float32)]
    ident = np.eye(64, dtype=np.float32)
    v1 = [(P >= Fq), (P <= Fq)]
    v2 = [(Fq >= P), (Fq <= P)]
    s1 = [(P > Fq), (P < Fq)]
    s2 = [(Fq > P), (Fq < P)]
    c = {}
    dd = [0, 0, 1, 1]
    c['TRIS'] = np.stack([tri[d] for d in dd], 1)
    c['ID4'] = np.stack([ident for d in dd], 1)
    c['NEG1'] = np.stack([np.where(v1[d], 0.0, NEG) for d in dd], 1).astype(np.float32)
    c['NEG2'] = np.stack([np.where(v2[d], 0.0, NEG) for d in dd], 1).astype(np.float32)
    c['ST1'] = np.stack([s1[d].astype(np.float32) for d in dd], 1)
    c['ST2'] = np.stack([s2[d].astype(np.float32) for d in dd], 1)
    c['INC2'] = np.stack([v2[d].astype(np.float32) for d in dd], 1)
    names = ['TRIS', 'ID4', 'NEG1', 'NEG2', 'ST1', 'ST2', 'INC2']
    return np.ascontiguousarray(np.concatenate([c[n] for n in names], 1)), names


def build_mixa(stage=99, ntiles=99, kb=None):
    kb = kb or KB()
    S = kb.S
    L = 8192
    LC = 256
    XT = kb.inp('XT', [D, L]); CXT = kb.inp('CXT', [D, LC])
    cTd = kb.inp('cT', [128, KC, 2]); modw = kb.inp('modw', [16, 128, D]); modb = kb.inp('modb', [128, 16]); nw = kb.inp('nw', [128, KC])
    wl = kb.inp('wl', [18, 128, D]); wg = kb.inp('wg', [128, KC, 8]); taps = kb.inp('taps', [128, 6, 3])
    gconst = kb.inp('gconst', [64, 2, 32])
    gwb = kb.inp('gwb', [17, 2, 128]); hnw = kb.inp('hnw', [128, 2]); cm = kb.inp('cm', [64, 28, 64]); identd = kb.inp('ident', [128, 128])
    YT = kb.outp('YT', [512, L])
    shared_at = getattr(kb, 'shared_at', None)
    if shared_at is None:
        ATl = kb.scratch('ATl', [D, L + 2]); ATc = kb.scratch('ATc', [D, LC + 2])
    else:
        ATl, ATc = shared_at
    G_U0 = kb.scratch('G_U0', [NG, 64, 4, 128]); G_KC = kb.scratch('G_KC', [NG, 128, 4, 64]); G_KD = kb.scratch('G_KD', [NG, 64, 4, 128])
    G_AT = kb.scratch('G_AT', [NG, 64, 4, 64]); G_QT = kb.scratch('G_QT', [NG, 128, 2, 64])
    L_KD = kb.scratch('L_KD', [NG, 64, 4, 64]); L_V = kb.scratch('L_V', [NG, 64, 2, 128]); L_AT = kb.scratch('L_AT', [NG, 64, 4, 64]); L_QD = kb.scratch('L_QD', [NG, 64, 4, 64])
    OG = kb.scratch('OG', [NLAT, 64, 4, 128]); OL = kb.scratch('OL', [NLAT, 64, 4, 128])
    ZT = kb.scratch('ZT', [4, 128, L])
    kb.init_psum(8)
    kb.init_wpool(3, D)
    ones = kb.sb('ones', [128, 128]); epst = kb.sb('epst', [128, 1]); ident = kb.sb('ident_s', [128, 128])
    kb.memset(ones[:], 1.0, ['ones']); kb.memset(epst[:], EPS, ['eps']); kb.eps_ap = epst[:, 0:1]
    kb.load(ident[:], identd[:, :], 'ident')
    cms = kb.sb('cms', [64, 28, 64]); kb.load(cms[:], cm[:, :, :], 'cm')
    TRIS, ID4, NEG1, NEG2, ST1, ST2, INC2 = [cms[:, 4 * i:4 * i + 4, :] for i in range(7)]
    cT = kb.sb('cTs', [128, KC, 2]); modbs = kb.sb('modbs', [128, 16]); nws = kb.sb('nws', [128, KC, 1]); mod = kb.sb('mod', [128, 16, 2])
    wgs = kb.sb('wgs', [128, KC, 8]); tapss = kb.sb('tapss', [128, 6, 3]); gcs = kb.sb('gcs', [64, 2, 32]); gwbs = kb.sb('gwbs', [17, 2, 128]); hnws = kb.sb('hnws', [128, 2])
    kb.load(cT[:], cTd[:, :, :], 'cT'); kb.load(modbs[:], modb[:, :], 'modb'); kb.load(nws[:, :, 0], nw[:, :], 'nw')
    kb.load(wgs[:], wg[:, :, :], 'wg'); kb.load(tapss[:], taps[:, :, :], 'taps'); kb.load(gcs[:], gconst[:, :, :], 'gcs'); kb.load(gwbs[:], gwb[:, :, :], 'gwb'); kb.load(hnws[:], hnw[:, :], 'hnw')
    kb.act(cT[:], cT[:], AF.Silu, ['cT'], ['cT'])
    emit_modvec(kb, cT, 'cT', 2, modw, modbs, 16, mod, 'mod')
    gains = kb.sb('gains', [128, KC, 2])
    for v in range(2):
        kb.stt(gains[:, :, v:v + 1], mod[:, 8:16, v:v + 1], 1.0, nws[:, :, 0:1], ALU.add, ALU.mult, ['mod', 'nw', 'gains'], ['gains'])
    negA = kb.sb('negA', [64, 32])
    kb.act(negA[:], gcs[:, 0, :], AF.Exp, ['gcs'], ['negA'])
    kb.ts(negA[:], negA[:], -1.0, None, ALU.mult, None, ['negA'], ['negA'])
    GGL = kb.sb('GGL', [128, NG, 4]); GEGC = kb.sb('GEGC', [64, NG, 4]); LGL = kb.sb('LGL', [64, NG, 4])
    kb.mixa_aoff_scan = kb.aoff
    arena = kb.sb('arena', [128, 12800])
    X = arena[:, 0:4096].rearrange("p (k t) -> p k t", k=KC); A = arena[:, 4096:8192].rearrange("p (k t) -> p k t", k=KC)
    sq = arena[:, 8192:12288].rearrange("p (k t) -> p k t", k=KC); rstd = arena[:, 12288:12800]
    zt = kb.sb('zt', [128, KC, 1]); kb.memset(zt[:], 0.0, ['zt'])
    ATlv = ATl.rearrange("(k p) t -> p k t", p=128); ATcv = ATc.rearrange("(k p) t -> p k t", p=128)
    XTv = XT.rearrange("(k p) t -> p k t", p=128); CXTv = CXT.rearrange("(k p) t -> p k t", p=128)
    tiles = [('c', 0, LC, 0)] + [('l', i * TT, TT, NCTX + i * 8) for i in range(L // TT)]
    if shared_at is None:
        for (dst, n) in ((ATlv, L), (ATcv, LC)):
            kb.store(dst[:, :, 0:1], zt[:], 'zt', ('ATpad', n, 0), final=False)
            kb.store(dst[:, :, n + 1:n + 2], zt[:], 'zt', ('ATpad', n, 1), final=False)
        for (sq_, t0, T, g0) in tiles:
            src = CXTv if sq_ == 'c' else XTv
            dst = ATcv if sq_ == 'c' else ATlv
            v = 1 if sq_ == 'c' else 0
            kb.load(X[:, :, :T], src[:, :, t0:t0 + T], 'X')
            emit_norm(kb, X, 'X', A, 'A', T, ones, gains[:, :, v:v + 1], mod[:, 0:8, v:v + 1], 'gains', sq, rstd)
            kb.store(dst[:, :, 1 + t0:1 + t0 + T], A[:, :, :T], 'A', ('AT', sq_, t0), final=False)
    atkeys = [('AT', s_, t0) for (s_, t0, T, g0) in tiles] + [('ATpad', n, i) for n in (L, LC) for i in range(2)]
    if kb.fused:
        kb.shared_at = (ATl, ATc)
    if stage < 2:
        return kb.finish()
    tiles = tiles[:ntiles]
    S.barrier()
    AH = arena[:, 0:KC * (TT + 2)].rearrange("p (k t) -> p k t", k=KC)
    Ue = kb.sb('Ue', [128, TT + 2]); cv = kb.sb('cv', [128, TT]); sqb = kb.sb('sqb', [128, TT]); rs = kb.sb('rs', [128, TT])
    qT = kb.sb('qT', [128, 2, TT]); kT = kb.sb('kT', [128, 2, TT]); vT = kb.sb('vT', [128, TT])
    k_tm = arena[0:64, 4112:6160].rearrange("p (c h f) -> p c h f", c=8, h=2); v_tm = arena[0:64, 6160:8208].rearrange("p (c h f) -> p c h f", c=8, h=2)
    qlT = kb.sb('qlT', [64, 2, TT]); klT = kb.sb('klT', [64, 2, TT]); kl_tm = kb.sb('kl_tm', [64, 8, 2, 64]); vl_tm = arena[0:64, 8208:10256].rearrange("p (c h f) -> p c h f", c=8, h=2)
    llrT = kb.sb('llrT', [17, 2, TT]); kb.memset(llrT[:], 1.0, ['llrT'])
    zs = kb.sb('zs', [128, 2, TT])
    g_tm = kb.sb('g_tm', [64, 8, 4]); b_tm = kb.sb('b_tm', [64, 8, 4]); gx = kb.sb('gx', [64, 8, 4])
    la_tm = arena[0:64, 10256:12304].rearrange("p (c h f) -> p c h f", c=8, h=4)
    def mk_lane(tag):
        sb = lambda n, shp: kb.sb(n + tag, shp)
        d = dict(tag=tag)
        d['RGB'] = sb('RGB', [64, 8, 64]); d['gcc'] = sb('gcc', [64, 4]); d['t1'] = sb('t1', [64, 4, 64]); d['t2'] = sb('t2', [64, 4, 64])
        d['E1'] = sb('E1', [64, 4, 64]); d['E2'] = sb('E2', [64, 4, 64]); d['Wm'] = sb('Wm', [64, 4, 64]); d['WmT'] = sb('WmT', [64, 4, 64])
        d['Pb'] = [sb('Pb%d' % i, [64, 4, 64]) for i in range(2)]; d['PTb'] = [sb('PTb%d' % i, [64, 4, 64]) for i in range(2)]; d['XTb'] = [sb('XTb%d' % i, [64, 4, 64]) for i in range(2)]
        d['attT'] = sb('attT', [64, 4, 64]); d['egc'] = sb('egc', [64, 4]); d['bege'] = sb('bege', [64, 4]); d['ekd'] = sb('ekd', [64, 4])
        d['VB'] = sb('VB', [64, 4, 128]); d['KBG'] = sb('KBG', [64, 4, 128]); d['KDg'] = sb('KDg', [64, 4, 128]); d['U0s'] = sb('U0s', [64, 4, 128]); d['KCs'] = sb('KCs', [128, 4, 64])
        d['refc'] = sb('refc', [64, 4]); d['dlt'] = sb('dlt', [64, 4, 64]); d['Ea'] = sb('Ea', [64, 4, 64]); d['Eb'] = sb('Eb', [64, 4, 64]); d['Ec'] = sb('Ec', [64, 4, 64])
        d['QQ'] = sb('QQ', [64, 4, 64]); d['KKl'] = sb('KKl', [64, 4, 64]); d['QDl'] = sb('QDl', [64, 4, 64]); d['ATl_'] = sb('ATTl', [64, 4, 64]); d['bct'] = sb('bct', [64, 4, 64]); d['KDl'] = sb('KDl', [64, 4, 64])
        return d
    prep_lanes = [mk_lane('L0'), mk_lane('L1')]

    def bc_last(ap, n):
        return ap.unsqueeze(2).to_broadcast([ap.shape[0], ap.shape[1], n])

    def l2norm(dst, T, qscale):
        kb.act(sqb[:, :T], dst, AF.Square, ['qkv'], ['sqb'])
        ps, pk = kb.psum()
        kb.mm(ps[:, :T], ones[:, :], sqb[:, :T], True, True, ['ones', 'sqb'], [pk])
        kb.act(rs[:, :T], ps[:, :T], AF.Sqrt, [pk, 'eps'], ['rs'], bias=kb.eps_ap, scale=1.0)
        kb.recip(rs[:, :T], rs[:, :T], ['rs'], ['rs'])
        kb.stt(dst, dst, qscale, rs[:, :T], ALU.mult, ALU.mult, ['qkv', 'rs'], ['qkv'])

    def transp(src_fn, nch, K_, M_, dst_fn, keys_r, key_w):
        per = 512 // M_
        for c0 in range(0, nch, per):
            ps, pk = kb.psum()
            n = min(per, nch - c0)
            for c in range(c0, c0 + n):
                kb.mm(ps[0:64, (c - c0) * M_:(c - c0 + 1) * M_], src_fn(c), ident[0:K_, 0:M_], True, True, keys_r + ['ident'], [pk])
            for c in range(c0, c0 + n):
                kb.cp(dst_fn(c), ps[0:64, (c - c0) * M_:(c - c0 + 1) * M_], [pk], [key_w], eng='act' if c % 2 else 'dve')

    for (sq_, t0, T, g0) in tiles:
        nch = T // CH
        src = ATcv if sq_ == 'c' else ATlv
        S.dma('sp', AH[:, :, 0:T + 2], src[:, :, t0:t0 + T + 2], reads=atkeys, writes=['AH'])

        def proj(b, M_, T=T):
            wb, wk = kb.wload(wl[b], D)
            ps, pk = kb.psum()
            for k in range(KC):
                kb.mm(ps[0:M_, :T], wb[:, k * 128:k * 128 + M_], AH[:, k, 1:T + 1], k == 0, k == KC - 1, [wk, 'AH'], [pk])
            return wb, wk, ps, pk
        for b in range(6):
            wb, wk, ps, pk = proj(b, 128)
            ph, phk = kb.psum()
            for k in range(KC):
                kb.mm(ph[:, 0:2], wb[:, k * 128:(k + 1) * 128], AH[:, k, 0:T + 2:T + 1], k == 0, k == KC - 1, [wk, 'AH'], [phk])
            kb.cp(Ue[:, 1:T + 1], ps[:, :T], [pk], ['Ue'], eng='act')
            kb.cp(Ue[:, 0:T + 2:T + 1], ph[:, 0:2], [phk, 'Ue'], ['Ue'])
            kb.ts(cv[:, :T], Ue[:, 1:T + 1], tapss[:, b, 1:2], None, ALU.mult, None, ['Ue', 'taps'], ['cv'])
            kb.stt(cv[:, :T], Ue[:, 0:T], tapss[:, b, 0:1], cv[:, :T], ALU.mult, ALU.add, ['Ue', 'taps', 'cv'], ['cv'])
            kb.stt(cv[:, :T], Ue[:, 2:T + 2], tapss[:, b, 2:3], cv[:, :T], ALU.mult, ALU.add, ['Ue', 'taps', 'cv'], ['cv'])
            seg, hl = b // 2, b % 2
            dst = (qT[:, hl, :T], kT[:, hl, :T], vT[:, :T])[seg]
            kb.act(dst, cv[:, :T], AF.Silu, ['cv'], ['qkv'])
            if seg < 2:
                l2norm(dst, T, 128 ** -0.5 if seg == 0 else 1.0)
            if seg == 0:
                kb.store(G_QT[g0:g0 + nch, :, hl, :].rearrange("g d i -> d g i"), qT[:, hl, :T].rearrange("d (g i) -> d g i", i=CH), 'qkv', ('G_QT', g0, hl), final=False)
            if seg == 1:
                transp(lambda c, hl=hl: kT[:, hl, c * CH:(c + 1) * CH], nch, 128, 128, lambda c, hl=hl: k_tm[:, c, hl, :], ['qkv'], 'k_tm')
            if seg == 2:
                transp(lambda c: vT[:, c * CH:(c + 1) * CH], nch, 128, 128, lambda c, hl=hl: v_tm[:, c, hl, :], ['qkv'], 'v_tm')
        for bi, b in enumerate((6, 7, 14, 15)):
            wb, wk, ps, pk = proj(b, 128)
            kb.act(zs[:, bi % 2, :T], ps[:, :T], AF.Silu, [pk], [('zs', bi % 2)])
            if sq_ == 'l':
                kb.store(ZT[bi, :, t0:t0 + T], zs[:, bi % 2, :T], ('zs', bi % 2), ('ZT', bi, t0), final=False)
        for b in (8, 9, 10, 11):
            wb, wk, ps, pk = proj(b, 64)
            hl = b % 2
            if b < 10:
                kb.ts(qlT[:, hl, :T], ps[0:64, :T], 64 ** -0.5, None, ALU.mult, None, [pk], ['qlT'])
            else:
                kb.cp(klT[:, hl, :T], ps[0:64, :T], [pk], ['klT'], eng='act')
                transp(lambda c, hl=hl: klT[:, hl, c * CH:(c + 1) * CH], nch, 64, 64, lambda c, hl=hl: kl_tm[:, c, hl, :], ['klT'], 'kl_tm')
        for b in (12, 13):
            wb, wk, ps, pk = proj(b, 128)
            hl = b % 2
            kb.cp(vT[:, :T], ps[:, :T], [pk, 'qkv'], ['qkv'], eng='act')
            transp(lambda c: vT[:, c * CH:(c + 1) * CH], nch, 128, 128, lambda c, hl=hl: vl_tm[:, c, hl, :], ['qkv'], 'vl_tm')
        for d in range(2):
            wb, wk, ps, pk = proj(16 + d, 16)
            kb.cp(llrT[0:16, d, :T], ps[0:16, :T], [pk, 'llrT'], ['llrT'])
        psg, pgk = kb.psum()
        for c in range(nch):
            for k in range(KC):
                kb.mm(psg[0:64, c * 8:(c + 1) * 8], AH[:, k, 1 + c * CH:1 + (c + 1) * CH], wgs[:, k, :], k == 0, k == KC - 1, ['AH', 'wg'], [pgk])
        psgv = psg[0:64, 0:nch * 8].rearrange("p (c e) -> p c e", e=8)
        gcv = gcs[:, 1, :].rearrange("p (c e) -> p c e", e=4)
        kb.tt(gx[:, :nch, :], psgv[:, :, 0:4], gcv[:, :nch, :], ALU.add, [pgk, 'gcs'], ['gx'])
        kb.act(gx[:, :nch, :], gx[:, :nch, :], AF.Exp, ['gx'], ['gx'])
        kb.act(gx[:, :nch, :], gx[:, :nch, :], AF.Ln, ['gx'], ['gx'], bias=1.0)
        kb.tt(g_tm[:, :nch, :], gx[:, :nch, :], negA[:].rearrange("p (c e) -> p c e", e=4)[:, :nch, :], ALU.mult, ['gx', 'negA'], ['g_tm'])
        kb.act(b_tm[:, :nch, :], psgv[:, :, 4:8], AF.Sigmoid, [pgk], ['b_tm'])
        for d in range(2):
            for c0 in range(0, nch, 4):
                ps, pk = kb.psum()
                n = min(4, nch - c0)
                for c in range(c0, c0 + n):
                    kb.mm(ps[0:64, (c - c0) * 128:(c - c0 + 1) * 128], llrT[0:17, d, c * CH:(c + 1) * CH], gwbs[0:17, d, :], True, True, ['llrT', 'gwb'], [pk])
                dstv = la_tm[:, c0:c0 + n, 2 * d:2 * d + 2, :]
                kb.act(dstv, ps[0:64, 0:n * 128].rearrange("p (c h f) -> p c h f", h=2, f=64), AF.Exp, [pk, 'la_tm'], ['la_tm'], scale=-1.0)
                kb.act(dstv, dstv, AF.Ln, ['la_tm'], ['la_tm'], bias=1.0)
                kb.ts(dstv, dstv, -1.0 / 16.0, None, ALU.mult, None, ['la_tm'], ['la_tm'])
        def prep(c, LN, g0=g0):
            g = g0 + c
            sfx = LN['tag']
            RGB, gcc, t1, t2, E1, E2, Wm, WmT = LN['RGB'], LN['gcc'], LN['t1'], LN['t2'], LN['E1'], LN['E2'], LN['Wm'], LN['WmT']
            Pb, PTb, XTb, attT, egc, bege, ekd = LN['Pb'], LN['PTb'], LN['XTb'], LN['attT'], LN['egc'], LN['bege'], LN['ekd']
            VB, KBG, KDg, U0s, KCs = LN['VB'], LN['KBG'], LN['KDg'], LN['U0s'], LN['KCs']
            refc, dlt, Ea, Eb, Ec, QQ, KKl, QDl, ATl_, bct, KDl = LN['refc'], LN['dlt'], LN['Ea'], LN['Eb'], LN['Ec'], LN['QQ'], LN['KKl'], LN['QDl'], LN['ATl_'], LN['bct'], LN['KDl']
            csl = slice(c * CH, (c + 1) * CH)
            kb.tt(RGB[:, 0:4, :], TRIS, bc_last(g_tm[:, c, :], 64), ALU.mult, ['cm', 'g_tm'], [sfx + 'RGB'])
            kb.tt(RGB[:, 4:8, :], ID4, bc_last(b_tm[:, c, :], 64), ALU.mult, ['cm', 'b_tm', sfx + 'RGB'], [sfx + 'RGB'], eng='pool')
            psR, pRk = kb.psum()
            kb.mm(psR[0:64, :], ones[0:64, 0:64], RGB[:].rearrange("p a b -> p (a b)"), True, True, ['ones', sfx + 'RGB'], [pRk])
            R = psR[0:64, 0:256].rearrange("p (a b) -> p a b", b=64); Bb = psR[0:64, 256:512].rearrange("p (a b) -> p a b", b=64)
            psC, pCk = kb.psum()
            for d in range(2):
                kb.mm(psC[0:64, 2 * d:2 * d + 2], cms[:, 2 * d, :], g_tm[:, c, 2 * d:2 * d + 2], True, True, ['cm', 'g_tm'], [pCk])
            kb.mm(psC[0:128, 4:8], ones[0:64, 0:128], g_tm[:, c, :], True, True, ['ones', 'g_tm'], [pCk])
            kb.cp(gcc[:], psC[0:64, 0:4], [pCk], [sfx + 'gcc'], eng='act')
            kb.stt(t1[:], R, -1.0, NEG1, ALU.mult, ALU.add, [pRk, 'cm'], [sfx + 't1'])
            kb.tt(t1[:], t1[:], bc_last(gcc[:], 64), ALU.add, [sfx + 't1', sfx + 'gcc'], [sfx + 't1'])
            kb.act(E1[:], t1[:], AF.Exp, [sfx + 't1'], [sfx + 'E1'])
            kb.tt(t2[:], R, NEG2, ALU.add, [pRk, 'cm'], [sfx + 't2'])
            kb.tt(t2[:], t2[:], bc_last(gcc[:], 64), ALU.subtract, [sfx + 't2', sfx + 'gcc'], [sfx + 't2'])
            kb.act(E2[:], t2[:], AF.Exp, [sfx + 't2'], [sfx + 'E2'])
            kb.tt(Wm[:], E1[:], bc_last(b_tm[:, c, :], 64), ALU.mult, [sfx + 'E1', 'b_tm'], [sfx + 'Wm'])
            kb.tt(Wm[:], Wm[:], ST1, ALU.mult, [sfx + 'Wm', 'cm'], [sfx + 'Wm'], eng='pool')
            kb.tt(WmT[:], Bb, E2[:], ALU.mult, [pRk, sfx + 'E2'], [sfx + 'WmT'])
            kb.tt(WmT[:], WmT[:], ST2, ALU.mult, [sfx + 'WmT', 'cm'], [sfx + 'WmT'], eng='pool')
            kb.act(egc[:], gcc[:], AF.Exp, [sfx + 'gcc'], [sfx + 'egc'])
            kb.tt(bege[:], egc[:], b_tm[:, c, :], ALU.mult, [sfx + 'egc', 'b_tm'], [sfx + 'bege'])
            kb.tt(ekd[:], psC[0:64, 4:8], gcc[:], ALU.subtract, [pCk, sfx + 'gcc'], [sfx + 'ekd'])
            kb.act(ekd[:], ekd[:], AF.Exp, [sfx + 'ekd'], [sfx + 'ekd'])
            kb.act(GGL[:, g, :], psC[0:128, 4:8], AF.Exp, [pCk], ['GGL'])
            kb.cp(GEGC[:, g, :], egc[:], [sfx + 'egc'], ['GEGC'], eng='pool')
            yield
            psK, pKk = kb.psum()
            for hl in range(2):
                kb.mm(psK[0:64, hl * 64:(hl + 1) * 64], kT[:, hl, csl], kT[:, hl, csl], True, True, ['qkv'], [pKk])
                kb.mm(psK[0:64, 128 + hl * 64:128 + (hl + 1) * 64], kT[:, hl, csl], qT[:, hl, csl], True, True, ['qkv'], [pKk])
            KK = psK[0:64, 0:128].rearrange("p (a b) -> p a b", b=64); QK = psK[0:64, 128:256].rearrange("p (a b) -> p a b", b=64)
            P, PT, XTm = Pb[0], PTb[0], XTb[0]
            for d in range(2):
                kb.tt(P[:, 2 * d:2 * d + 2, :], KK, Wm[:, 2 * d:2 * d + 2, :], ALU.mult, [pKk, sfx + 'Wm', sfx + 'P0'], [sfx + 'P0'])
                kb.tt(PT[:, 2 * d:2 * d + 2, :], KK, WmT[:, 2 * d:2 * d + 2, :], ALU.mult, [pKk, sfx + 'WmT', sfx + 'PT0'], [sfx + 'PT0'])
                kb.tt(attT[:, 2 * d:2 * d + 2, :], QK, E2[:, 2 * d:2 * d + 2, :], ALU.mult, [pKk, sfx + 'E2', sfx + 'attT'], [sfx + 'attT'])
            kb.tt(XTm[:], ID4, PT[:], ALU.subtract, ['cm', sfx + 'PT0', sfx + 'X0'], [sfx + 'X0'])
            yield
            cur = 0
            for lvl in range(5):
                nxt = 1 - cur
                psP, pPk = kb.psum()
                for i in range(4):
                    kb.mm(psP[0:64, i * 64:(i + 1) * 64], PTb[cur][:, i, :], Pb[cur][:, i, :], True, True, [sfx + 'P%d' % cur, sfx + 'PT%d' % cur], [pPk])
                    if lvl < 4:
                        kb.mm(psP[0:64, 256 + i * 64:256 + (i + 1) * 64], Pb[cur][:, i, :], PTb[cur][:, i, :], True, True, [sfx + 'P%d' % cur, sfx + 'PT%d' % cur], [pPk])
                kb.cp(Pb[nxt][:].rearrange("p a b -> p (a b)"), psP[0:64, 0:256], [pPk, sfx + 'P%d' % nxt], [sfx + 'P%d' % nxt], eng='act')
                if lvl < 4:
                    kb.cp(PTb[nxt][:].rearrange("p a b -> p (a b)"), psP[0:64, 256:512], [pPk, sfx + 'PT%d' % nxt], [sfx + 'PT%d' % nxt])
                psX, pXk = kb.psum()
                for i in range(4):
                    kb.mm(psX[0:64, i * 64:(i + 1) * 64], Pb[nxt][:, i, :], XTb[cur][:, i, :], True, True, [sfx + 'P%d' % nxt, sfx + 'X%d' % cur], [pXk])
                kb.tt(XTb[nxt][:].rearrange("p a b -> p (a b)"), psX[0:64, 0:256], XTb[cur][:].rearrange("p a b -> p (a b)"), ALU.add, [pXk, sfx + 'X%d' % cur, sfx + 'X%d' % nxt], [sfx + 'X%d' % nxt])
                cur = nxt
                yield
            XTf, xk = XTb[cur], sfx + 'X%d' % cur
            for d in range(2):
                dsl = slice(2 * d, 2 * d + 2)
                kb.tt(VB[:, dsl, :], v_tm[:, c, :, :], bc_last(b_tm[:, c, dsl], 128), ALU.mult, ['v_tm', 'b_tm', sfx + 'VB'], [sfx + 'VB'], eng='pool')
                kb.tt(KBG[:, dsl, :], k_tm[:, c, :, :], bc_last(bege[:, dsl], 128), ALU.mult, ['k_tm', sfx + 'bege', sfx + 'KBG'], [sfx + 'KBG'])
                kb.tt(KDg[:, dsl, :], k_tm[:, c, :, :], bc_last(ekd[:, dsl], 128), ALU.mult, ['k_tm', sfx + 'ekd', sfx + 'KDg'], [sfx + 'KDg'], eng='pool')
            psU, pUk = kb.psum()
            psKc, pKck = kb.psum()
            for i in range(4):
                kb.mm(psU[0:64, i * 128:(i + 1) * 128], XTf[:, i, :], VB[:, i, :], True, True, [xk, sfx + 'VB'], [pUk])
                kb.mm(psKc[0:128, i * 64:(i + 1) * 64], KBG[:, i, :], XTf[:, i, :], True, True, [xk, sfx + 'KBG'], [pKck])
            kb.cp(U0s[:].rearrange("p a b -> p (a b)"), psU[0:64, :], [pUk, sfx + 'U0s'], [sfx + 'U0s'], eng='act')
            kb.cp(KCs[:].rearrange("p a b -> p (a b)"), psKc[0:128, 0:256], [pKck, sfx + 'KCs'], [sfx + 'KCs'])
            kb.store(G_U0[g], U0s[:], sfx + 'U0s', ('G_U0', g), final=False)
            kb.store(G_KC[g], KCs[:], sfx + 'KCs', ('G_KC', g), final=False)
            kb.store(G_KD[g], KDg[:], sfx + 'KDg', ('G_KD', g), final=False)
            kb.store(G_AT[g], attT[:], sfx + 'attT', ('G_AT', g), final=False)
            yield
            LA = la_tm[:, c, :, :]
            psB, pBk = kb.psum()
            for i in range(4):
                kb.mm(psB[0:64, i * 64:(i + 1) * 64], LA[:, i, :], cms[:, 2 * (i // 2), :], True, True, ['la_tm', 'cm'], [pBk])
            for d in range(2):
                kb.mm(psB[0:64, 256 + d * 128:256 + (d + 1) * 128], cms[:, 2 * d, :], LA[:, 2 * d:2 * d + 2, :].rearrange("p a b -> p (a b)"), True, True, ['la_tm', 'cm'], [pBk])
            psT, pTk = kb.psum()
            kb.mm(psT[0:64, 0:256], ones[0:64, 0:64], LA.rearrange("p a b -> p (a b)"), True, True, ['la_tm', 'ones'], [pTk])
            for i in range(4):
                kb.mm(psT[0:64, 256 + i:257 + i], LA[:, i, :], ones[0:64, 0:1], True, True, ['la_tm', 'ones'], [pTk])
            bcT = psB[0:64, 0:256].rearrange("p (a b) -> p a b", b=64)
            for d in range(2):
                ridx = 32 if d == 0 else 31
                kb.cp(refc[:, 2 * d:2 * d + 2], bcT[:, 2 * d:2 * d + 2, ridx], [pBk, sfx + 'refc'], [sfx + 'refc'])
            kb.tt(dlt[:], bcT, bc_last(refc[:], 64), ALU.subtract, [pBk, sfx + 'refc'], [sfx + 'dlt'])
            kb.act(Ea[:], dlt[:], AF.Exp, [sfx + 'dlt'], [sfx + 'Ea'])
            kb.act(Eb[:], dlt[:], AF.Exp, [sfx + 'dlt'], [sfx + 'Eb'], scale=-1.0)
            kb.act(Ec[:], bcT, AF.Exp, [pBk], [sfx + 'Ec'])
            kb.cp(bct[:].rearrange("p a b -> p (a b)"), psB[0:64, 256:512], [pBk], [sfx + 'bct'], eng='act')
            kb.tt(bct[:].rearrange("p a b -> p (a b)"), psT[0:64, 0:256], bct[:].rearrange("p a b -> p (a b)"), ALU.subtract, [pTk, sfx + 'bct'], [sfx + 'bct'])
            kb.act(bct[:], bct[:], AF.Exp, [sfx + 'bct'], [sfx + 'bct'])
            kb.act(LGL[:, g, :], psT[0:64, 256:260], AF.Exp, [pTk], ['LGL'])
            yield
            for d in range(2):
                dsl = slice(2 * d, 2 * d + 2)
                kb.tt(QQ[:, dsl, :], qlT[:, :, csl], Ea[:, dsl, :], ALU.mult, ['qlT', sfx + 'Ea', sfx + 'QQ'], [sfx + 'QQ'])
                kb.tt(KKl[:, dsl, :], klT[:, :, csl], Eb[:, dsl, :], ALU.mult, ['klT', sfx + 'Eb', sfx + 'KKl'], [sfx + 'KKl'], eng='pool')
                kb.tt(QDl[:, dsl, :], qlT[:, :, csl], Ec[:, dsl, :], ALU.mult, ['qlT', sfx + 'Ec', sfx + 'QDl'], [sfx + 'QDl'])
            psA, pAk = kb.psum()
            for i in range(4):
                kb.mm(psA[0:64, i * 64:(i + 1) * 64], KKl[:, i, :], QQ[:, i, :], True, True, [sfx + 'KKl', sfx + 'QQ'], [pAk])
            kb.tt(ATl_[:], psA[0:64, 0:256].rearrange("p (a b) -> p a b", b=64), INC2, ALU.mult, [pAk, 'cm'], [sfx + 'ATTl'])
            yield
            for d in range(2):
                dsl = slice(2 * d, 2 * d + 2)
                kb.tt(KDl[:, dsl, :], kl_tm[:, c, :, :], bct[:, dsl, :], ALU.mult, ['kl_tm', sfx + 'bct', sfx + 'KDl'], [sfx + 'KDl'])
            kb.store(L_KD[g], KDl[:], sfx + 'KDl', ('L_KD', g), final=False)
            kb.store(L_AT[g], ATl_[:], sfx + 'ATTl', ('L_AT', g), final=False)
            kb.store(L_QD[g], QDl[:], sfx + 'QDl', ('L_QD', g), final=False)
            kb.store(L_V[g], vl_tm[:, c, :, :], 'vl_tm', ('L_V', g), final=False)
        if stage >= 3:
            for c0 in range(0, nch, 2):
                gens = [prep(c0, prep_lanes[0]), prep(c0 + 1, prep_lanes[1])]
                while gens:
                    for gnr in list(gens):
                        try:
                            next(gnr)
                        except StopIteration:
                            gens.remove(gnr)
    kb.mixa_state = dict(G_U0=G_U0, G_KC=G_KC, G_KD=G_KD, G_AT=G_AT, G_QT=G_QT, L_KD=L_KD, L_V=L_V, L_AT=L_AT, L_QD=L_QD, OG=OG, OL=OL, ZT=ZT,
                         GGL=GGL, GEGC=GEGC, LGL=LGL, YT=YT, ident=ident, hnws=hnws, tiles=tiles,
                         g0of=(lambda g: 0 if g < NCTX else NCTX + ((g - NCTX) // 8) * 8))
    if stage >= 4:
        emit_mixa_scan(kb)
    return kb.finish()


def emit_mixa_scan(kb):
    st = kb.mixa_state
    S = kb.S
    G_U0, G_KC, G_KD, G_AT, G_QT = st['G_U0'], st['G_KC'], st['G_KD'], st['G_AT'], st['G_QT']
    L_KD, L_V, L_AT, L_QD, OG, OL, ZT = st['L_KD'], st['L_V'], st['L_AT'], st['L_QD'], st['OG'], st['OL'], st['ZT']
    GGL, GEGC, LGL, YT, ident, hnws = st['GGL'], st['GEGC'], st['LGL'], st['YT'], st['ident'], st['hnws']
    if kb.arena is not None:
        S.barrier()
        kb.aoff = kb.mixa_aoff_scan
    Sg = kb.sb('Sg', [128, 4, 128]); Sl = kb.sb('Sl', [64, 4, 128])
    kb.memset(Sg[:], 0.0, [('Sg', i) for i in range(4)]); kb.memset(Sl[:], 0.0, [('Sl', i) for i in range(4)])
    NB = 2
    U0b = [[kb.sb('U0b%d%d' % (d, s), [64, 2, 128]) for s in range(NB)] for d in range(2)]
    KCb = [[kb.sb('KCb%d%d' % (d, s), [128, 2, 64]) for s in range(NB)] for d in range(2)]
    KDb = [[kb.sb('KDb%d%d' % (d, s), [64, 2, 128]) for s in range(NB)] for d in range(2)]
    ATb = [[kb.sb('ATb%d%d' % (d, s), [64, 2, 64]) for s in range(NB)] for d in range(2)]
    QTb = [[kb.sb('QTb%d%d' % (d, s), [128, 2, 64]) for s in range(NB)] for d in range(2)]
    LKDb = [[kb.sb('LKDb%d%d' % (d, s), [64, 2, 64]) for s in range(NB)] for d in range(2)]
    LVb = [[kb.sb('LVb%d%d' % (d, s), [64, 2, 128]) for s in range(NB)] for d in range(2)]
    LATb = [[kb.sb('LATb%d%d' % (d, s), [64, 2, 64]) for s in range(NB)] for d in range(2)]
    LQDb = [[kb.sb('LQDb%d%d' % (d, s), [64, 2, 64]) for s in range(NB)] for d in range(2)]
    ub = [kb.sb('ub%d' % d, [64, 2, 128]) for d in range(2)]
    qs = [kb.sb('qs%d' % d, [64, 2, 128]) for d in range(2)]
    ob = [kb.sb('ob%d' % d, [64, 2, 128]) for d in range(2)]
    obl = [kb.sb('obl%d' % d, [64, 2, 128]) for d in range(2)]
    order = [list(range(NG)), list(range(NCTX - 1, -1, -1)) + list(range(NG - 1, NCTX - 1, -1))]
    for s in range(NG):
        slot = s % NB
        for d in range(2):
            g = order[d][s]
            emit = g >= NCTX
            dsl = slice(2 * d, 2 * d + 2)
            gk = ('gop', d, slot); lk = ('lop', d, slot)
            S.dma('sp', U0b[d][slot][:], G_U0[g][:, dsl, :], reads=[('G_U0', g)], writes=[gk])
            S.dma('sp', KCb[d][slot][:], G_KC[g][:, dsl, :], reads=[('G_KC', g)], writes=[gk])
            S.dma('sp', KDb[d][slot][:], G_KD[g][:, dsl, :], reads=[('G_KD', g)], writes=[gk])
            if emit:
                S.dma('sp', ATb[d][slot][:], G_AT[g][:, dsl, :], reads=[('G_AT', g)], writes=[gk])
                S.dma('sp', QTb[d][slot][:], G_QT[g], reads=[('G_QT', st['g0of'](g), 0), ('G_QT', st['g0of'](g), 1)], writes=[gk])
            S.dma('sp', LKDb[d][slot][:], L_KD[g][:, dsl, :], reads=[('L_KD', g)], writes=[lk])
            S.dma('sp', LVb[d][slot][:], L_V[g], reads=[('L_V', g)], writes=[lk])
            if emit:
                S.dma('sp', LATb[d][slot][:], L_AT[g][:, dsl, :], reads=[('L_AT', g)], writes=[lk])
                S.dma('sp', LQDb[d][slot][:], L_QD[g][:, dsl, :], reads=[('L_QD', g)], writes=[lk])
            ps1, p1k = kb.psum()
            for hl in range(2):
                i = 2 * d + hl
                kb.mm(ps1[0:64, hl * 128:(hl + 1) * 128], KCb[d][slot][:, hl, :], Sg[:, i, :], True, True, [gk, ('Sg', i)], [p1k])
            if emit:
                pso, pok = kb.psum()
                for hl in range(2):
                    i = 2 * d + hl
                    kb.mm(pso[0:64, hl * 128:(hl + 1) * 128], QTb[d][slot][:, hl, :], Sg[:, i, :], True, True, [gk, ('Sg', i)], [pok])
            kb.tt(ub[d][:].rearrange("p a b -> p (a b)"), U0b[d][slot][:].rearrange("p a b -> p (a b)"), ps1[0:64, 0:256], ALU.subtract, [gk, p1k, ('ub', d)], [('ub', d)])
            if emit:
                for hl in range(2):
                    i = 2 * d + hl
                    kb.ts(qs[d][:, hl, :], pso[0:64, hl * 128:(hl + 1) * 128], GEGC[:, g, i:i + 1], None, ALU.mult, None, [pok, 'GEGC', ('qs', d)], [('qs', d)], eng='pool' if False else 'dve')
                pso2, po2k = kb.psum()
                for hl in range(2):
                    kb.mm(pso2[0:64, hl * 128:(hl + 1) * 128], ATb[d][slot][:, hl, :], ub[d][:, hl, :], True, True, [gk, ('ub', d)], [po2k])
                kb.tt(ob[d][:].rearrange("p a b -> p (a b)"), qs[d][:].rearrange("p a b -> p (a b)"), pso2[0:64, 0:256], ALU.add, [('qs', d), po2k, ('ob', d)], [('ob', d)])
                kb.store(OG[g - NCTX][:, dsl, :], ob[d][:], ('ob', d), ('OG', g, d), final=False)
            pss, psk = kb.psum()
            for hl in range(2):
                kb.mm(pss[0:128, hl * 128:(hl + 1) * 128], KDb[d][slot][:, hl, :], ub[d][:, hl, :], True, True, [gk, ('ub', d)], [psk])
            for hl in range(2):
                i = 2 * d + hl
                kb.stt(Sg[:, i, :], Sg[:, i, :], GGL[:, g, i:i + 1], pss[0:128, hl * 128:(hl + 1) * 128], ALU.mult, ALU.add, [psk, 'GGL', ('Sg', i)], [('Sg', i)])
            if emit:
                pso, pok = kb.psum()
                for hl in range(2):
                    i = 2 * d + hl
                    kb.mm(pso[0:64, hl * 128:(hl + 1) * 128], LQDb[d][slot][:, hl, :], Sl[:, i, :], True, False, [lk, ('Sl', i)], [pok])
                    kb.mm(pso[0:64, hl * 128:(hl + 1) * 128], LATb[d][slot][:, hl, :], LVb[d][slot][:, hl, :], False, True, [lk], [pok])
                kb.cp(obl[d][:].rearrange("p a b -> p (a b)"), pso[0:64, 0:256], [pok, ('obl', d)], [('obl', d)], eng='act')
                kb.store(OL[g - NCTX][:, dsl, :], obl[d][:], ('obl', d), ('OL', g, d), final=False)
            pss, psk = kb.psum()
            for hl in range(2):
                kb.mm(pss[0:64, hl * 128:(hl + 1) * 128], LKDb[d][slot][:, hl, :], LVb[d][slot][:, hl, :], True, True, [lk], [psk])
            for hl in range(2):
                i = 2 * d + hl
                kb.stt(Sl[:, i, :], Sl[:, i, :], LGL[:, g, i:i + 1], pss[0:64, hl * 128:(hl + 1) * 128], ALU.mult, ALU.add, [psk, 'LGL', ('Sl', i)], [('Sl', i)])
    FB = []
    for i in range(2):
        FB.append(dict(Ob=kb.sb('Ob%d' % i, [128, 4, 128]), osum=kb.sb('osum%d' % i, [128, 2, 128]), osq=kb.sb('osq%d' % i, [128, 2, 128]), ss=kb.sb('ss%d' % i, [128, 2]),
                       Zb=kb.sb('Zb%d' % i, [128, 2, 128]), Yb=kb.sb('Yb%d' % i, [128, 2, 128])))
    cnt = 0
    for which, (OD, zoff, yoff, ncol) in enumerate(((OG, 0, kb.yoffs[0], 0), (OL, 2, kb.yoffs[1], 1))):
        nm = 'OG' if which == 0 else 'OL'
        for tg in range(NLAT // 2):
            fb = FB[cnt % 2]; sx = 'f%d' % (cnt % 2); cnt += 1
            Ob, osum, osq, ss, Zb, Yb = fb['Ob'], fb['osum'], fb['osq'], fb['ss'], fb['Zb'], fb['Yb']
            toks = slice(tg * 128, (tg + 1) * 128)
            rk = [(nm, NCTX + 2 * tg + cc, d) for cc in range(2) for d in range(2)]
            S.dma('sp', Ob[:], OD[2 * tg:2 * tg + 2].rearrange("c p i e -> (c p) i e"), reads=rk, writes=['Ob' + sx])
            S.dma('sp', Zb[:], ZT[zoff:zoff + 2, :, toks].rearrange("h e t -> e h t"), reads=[('ZT', zoff + h, (tg * 128 // TT) * TT) for h in range(2)], writes=['Zb' + sx])
            kb.tt(osum[:], Ob[:, 0:2, :], Ob[:, 2:4, :], ALU.add, ['Ob' + sx, 'osum' + sx], ['osum' + sx])
            kb.tt(osq[:], osum[:], osum[:], ALU.mult, ['osum' + sx, 'osq' + sx], ['osq' + sx], eng='pool')
            S.op('dve', (lambda e, ss=ss, osq=osq: e.tensor_reduce(out=ss[:], in_=osq[:], axis=AX.X, op=ALU.add)), ['osq' + sx, 'ss' + sx], ['ss' + sx])
            kb.act(ss[:], ss[:], AF.Sqrt, ['ss' + sx, 'eps'], ['ss' + sx], bias=kb.eps_ap, scale=1.0 / 128)
            kb.recip(ss[:], ss[:], ['ss' + sx], ['ss' + sx])
            kb.tt(osum[:], osum[:], ss[:].unsqueeze(2).to_broadcast([128, 2, 128]), ALU.mult, ['osum' + sx, 'ss' + sx], ['osum' + sx])
            psY, pYk = kb.psum()
            for hl in range(2):
                kb.mm(psY[0:128, hl * 128:(hl + 1) * 128], osum[:, hl, :], ident[:, :], True, True, ['osum' + sx, 'ident'], [pYk])
            kb.stt(Yb[:].rearrange("p a b -> p (a b)"), psY[0:128, 0:256], hnws[:, ncol:ncol + 1], Zb[:].rearrange("p a b -> p (a b)"), ALU.mult, ALU.mult, [pYk, 'hnw', 'Zb' + sx, 'Yb' + sx], ['Yb' + sx])
            kb.store(YT[yoff:yoff + 256, toks].rearrange("(h e) t -> e h t", e=128), Yb[:], 'Yb' + sx, ('YT', which, tg))


def mixa_inputs(inp, b, hh):
    e = 0
    W = inp['ab_w_in'][e]
    offs = np.concatenate([[0], np.cumsum(AB_SIZES)])

    def cols(seg, start, n):
        return list(range(offs[seg] + start, offs[seg] + start + n))
    blocks = []
    for seg in (0, 1, 2):
        for hl in range(2):
            blocks.append(cols(seg, (2 * hh + hl) * 128, 128))
    for hl in range(2):
        blocks.append(cols(3, (2 * hh + hl) * 128, 128))
    for seg in (6, 7):
        for hl in range(2):
            blocks.append(cols(seg, (2 * hh + hl) * 64, 64))
    for seg in (8, 9):
        for hl in range(2):
            blocks.append(cols(seg, (2 * hh + hl) * 128, 128))
    for d in range(2):
        blocks.append(cols(10, d * 16, 16))
    Wp = np.zeros((D, 18 * 128), np.float32)
    for bi, cl in enumerate(blocks):
        Wp[:, bi * 128:bi * 128 + len(cl)] = W[:, cl]
    gcols = [offs[4] + d * 4 + 2 * hh + hl for d in range(2) for hl in range(2)] + [offs[5] + d * 4 + 2 * hh + hl for d in range(2) for hl in range(2)]
    wg = np.ascontiguousarray(W[:, gcols].reshape(KC, 128, 8).transpose(1, 0, 2))
    cw = inp['ab_conv_w'][e]
    taps = np.zeros((128, 6, 3), np.float32)
    for seg in range(3):
        for hl in range(2):
            ch0 = seg * 512 + (2 * hh + hl) * 128
            taps[:, seg * 2 + hl, :] = cw[:, ch0:ch0 + 128].T
    al = np.array([inp['gdn_a_log'][e][d, 2 * hh + hl] for d in range(2) for hl in range(2)], np.float32)
    dtb = np.array([inp['gdn_dt_bias'][e][d, 2 * hh + hl] for d in range(2) for hl in range(2)], np.float32)
    gconst = np.zeros((64, 2, 32), np.float32)
    gconst[:, 0, :] = np.tile(al, 8)[None, :]
    gconst[:, 1, :] = np.tile(dtb, 8)[None, :]
    gwb = np.zeros((17, 2, 128), np.float32)
    hs = slice(2 * hh * 64, (2 * hh + 2) * 64)
    for d in range(2):
        gwb[0:16, d, :] = inp['gla_gate_w'][e][d][:, hs]
        gwb[16, d, :] = inp['gla_gate_b'][e][d][hs]
    cm, _ = mixa_consts()
    return dict(
        XT=np.ascontiguousarray(inp['x'][b].T), CXT=np.ascontiguousarray(inp['ctx'][b].T),
        cT=np.ascontiguousarray(np.stack([fm_vec(inp['c'][b]), fm_vec(inp['c_ctx'])], axis=-1)),
        modw=blk_w(np.ascontiguousarray(inp['mod_w'][0][:, 0:2048]), 8), modb=fm_vec(inp['mod_b'][0][0:2048]), nw=fm_vec(inp['norm1_w'][0]),
        wl=blk_w(Wp, 8), wg=wg, taps=taps, gconst=gconst, gwb=gwb,
        hnw=np.ascontiguousarray(np.stack([inp['gdn_norm_w'][e], inp['gla_norm_w'][e]], axis=-1)), cm=cm, ident=np.eye(128, dtype=np.float32))


LSEQ = 8192
NFFT = 16384
CG = 8
MAGIC = 12582912.0


def hy_consts():
    n = np.arange(128, dtype=np.float64)
    th = 2 * np.pi * np.outer(n, n) / 128.0
    Fre, Fim = np.cos(th), -np.sin(th)
    tw = 2 * np.pi * np.outer(n, n) / NFFT
    Tre, Tim = np.cos(tw), -np.sin(tw)
    F64c = np.zeros((128, 256)); F64c[:64, :128] = Fre[:64]; F64c[:64, 128:] = Fim[:64]
    G1 = np.concatenate([Fre, -Fim], 1); G2 = np.concatenate([Fim, Fre], 1)
    fc = np.concatenate([F64c, Fre, Fim, G1, G2, Tre, Tim], 1).astype(np.float32)
    l = LSEQ
    t = np.linspace(0.0, 1.0, l, dtype=np.float32)[:, None]
    w = (np.float32(2.0 * math.pi / l) * np.arange(l, dtype=np.float32))[:, None]
    f = np.linspace(1e-4, 15, 16, dtype=np.float32)[None, :]
    zp = np.concatenate([t, np.cos(f * w), -np.sin(f * w)], axis=-1).astype(np.float32).T
    deltas = np.abs(np.linspace(math.log(1e-2) / 1.5, math.log(1e-2) / 0.3, D, dtype=np.float32))
    win = (np.exp(-t * deltas[None, :]) + np.float32(0.05)).astype(np.float32).T
    idx = (l - np.arange(l)) % l
    zp2 = np.concatenate([zp, zp[:, idx]], axis=0)
    win2 = np.stack([win, win[:, idx]], axis=0)
    return np.ascontiguousarray(fc), np.ascontiguousarray(zp2), np.ascontiguousarray(win2)


def build_hyc(stage=99, ngroups=None, nb=4, nblk=1, kb=None):
    kb = kb or KB()
    S = kb.S
    L = LSEQ
    NCH = nblk * 128
    if ngroups is None:
        ngroups = NCH // CG
    UT3 = kb.inp('UT3', [3, NCH, nb, L]); tapsd = kb.inp('taps', [128, nblk, 3, 3]); zpd = kb.inp('zp', [66, L]); wind = kb.inp('win', [2, NCH, L])
    w1d = kb.inp('w1', [33, 64]); w23d = kb.inp('w23', [64, 2, 64]); bfrd = kb.inp('bfr', [64, 4]); fod = kb.inp('fo', [64, 4, NCH]); skipd = kb.inp('skip', [128, nblk, 2])
    fcd = kb.inp('fc', [128, 1280])
    Z2T = kb.outp('Z2T', [NCH, nb, L])
    UC = kb.scratch('UC', [3, nb, NCH, L]); HFs = kb.scratch('HFs', [4, NCH, L])
    kb.init_psum(8)
    fc = kb.sb('fc_s', [128, 1280]); kb.load(fc[:], fcd[:, :], 'fc')
    F64c = fc[0:64, 0:256]; F_re = fc[:, 256:384]; F_im = fc[:, 384:512]; G1 = fc[:, 512:768]; G2 = fc[:, 768:1024]; T_re = fc[:, 1024:1152]; T_im = fc[:, 1152:1280]
    taps = kb.sb('taps_s', [128, nblk, 3, 3]); w1 = kb.sb('w1s', [33, 64]); w23 = kb.sb('w23s', [64, 2, 64]); bfr = kb.sb('bfrs', [64, 4]); fo = kb.sb('fos', [64, 4, NCH]); skip = kb.sb('skips', [128, nblk, 2])
    kb.load(taps[:], tapsd[:, :, :, :], 'taps'); kb.load(w1[:], w1d[:, :], 'w1'); kb.load(w23[:], w23d[:, :, :], 'w23'); kb.load(bfr[:], bfrd[:, :], 'bfr'); kb.load(fo[:], fod[:, :, :], 'fo'); kb.load(skip[:], skipd[:, :, :], 'skip')
    frb = kb.sb('frb', [64, 3])
    kb.tt(frb[:], bfr[:, 0:3], bfr[:, 3:4].to_broadcast([64, 3]), ALU.mult, ['bfr'], ['frb'])
    aoff_p3 = kb.aoff
    CW = 2048
    Uin = kb.sb('Uin', [128, CW + 2]); Uout = kb.sb('Uout', [128, CW])
    uckeys = []
    for seg in range(3):
        for b in range(nb):
            for cb in range(nblk):
                chs = slice(cb * 128, (cb + 1) * 128)
                for t0 in range(0, L, CW):
                    lo = max(t0 - 1, 0); hi = min(t0 + CW + 1, L)
                    if t0 == 0:
                        kb.memset(Uin[:, 0:1], 0.0, ['Uin'], eng='dve')
                    if t0 + CW == L:
                        kb.memset(Uin[:, CW + 1:CW + 2], 0.0, ['Uin'], eng='dve')
                    S.dma('sp', Uin[:, lo - (t0 - 1):hi - (t0 - 1)], UT3[seg, chs, b, lo:hi], writes=['Uin'])
                    kb.ts(Uout[:], Uin[:, 1:CW + 1], taps[:, cb, seg, 1:2], None, ALU.mult, None, ['Uin', 'taps', 'Uout'], ['Uout'])
                    kb.stt(Uout[:], Uin[:, 0:CW], taps[:, cb, seg, 0:1], Uout[:], ALU.mult, ALU.add, ['Uin', 'taps', 'Uout'], ['Uout'])
                    kb.stt(Uout[:], Uin[:, 2:CW + 2], taps[:, cb, seg, 2:3], Uout[:], ALU.mult, ALU.add, ['Uin', 'taps', 'Uout'], ['Uout'])
                    kb.store(UC[seg, b, chs, t0:t0 + CW], Uout[:], 'Uout', ('UC', seg, b, cb, t0), final=False)
                    uckeys.append(('UC', seg, b, cb, t0))
    zp = kb.sb('zp_s', [33, TT]); aa = kb.sb('aa', [64, TT]); tq = kb.sb('tq', [64, TT]); hd = [kb.sb('hd%d' % i, [64, TT]) for i in range(2)]
    hft = kb.sb('hft', [128, 2, TT]); wt = kb.sb('wt', [128, TT])

    def sin_layer(ps, pk, li, out, okey):
        kb.act(aa[:], ps[0:64, :TT], AF.Identity, [pk, 'bfr', 'frb'], ['aa'], bias=frb[:, li:li + 1], scale=bfr[:, 3:4])
        kb.ts(tq[:], aa[:], 1.0 / (2 * math.pi), MAGIC, ALU.mult, ALU.add, ['aa'], ['tq'])
        kb.ts(tq[:], tq[:], MAGIC, -2 * math.pi, ALU.subtract, ALU.mult, ['tq'], ['tq'])
        kb.tt(aa[:], aa[:], tq[:], ALU.add, ['aa', 'tq'], ['aa'])
        kb.ts(aa[:], aa[:], 3.141592, -3.141592, ALU.min, ALU.max, ['aa'], ['aa'])
        kb.act(out, aa[:], AF.Sin, ['aa'], [okey])
    hfkeys = []
    zp2 = kb.sb('zp2_s', [33, TT]); hdr = kb.sb('hdr', [64, TT]); wt2 = kb.sb('wt2', [128, TT])
    for ti in range(L // TT):
        tsl = slice(ti * TT, (ti + 1) * TT)
        kb.load(zp[:], zpd[0:33, tsl], 'zp'); kb.load(zp2[:], zpd[33:66, tsl], 'zp2')
        hfin = []
        for (zt_, zk, dst, dk) in ((zp, 'zp', hd[0], 'hd0'), (zp2, 'zp2', hdr, 'hdr')):
            ps, pk = kb.psum()
            kb.mm(ps[0:64, :TT], w1[:, :], zt_[:, :], True, True, ['w1', zk], [pk])
            sin_layer(ps, pk, 0, dst[:], dk)
            ps, pk = kb.psum()
            kb.mm(ps[0:64, :TT], w23[:, 0, :], dst[:], True, True, ['w23', dk], [pk])
            sin_layer(ps, pk, 1, hd[1][:], 'hd1')
            ps, pk = kb.psum()
            kb.mm(ps[0:64, :TT], w23[:, 1, :], hd[1][:], True, True, ['w23', 'hd1'], [pk])
            sin_layer(ps, pk, 2, dst[:], dk)
        for cb in range(nblk):
            chs = slice(cb * 128, (cb + 1) * 128)
            kb.load(wt[:], wind[0, chs, tsl], 'wt'); kb.load(wt2[:], wind[1, chs, tsl], 'wt2')
            for f in range(4):
                hsrc, hsk, wsrc, wsk = (hd[0], 'hd0', wt, 'wt') if f < 2 else (hdr, 'hdr', wt2, 'wt2')
                ps, pk = kb.psum()
                kb.mm(ps[:, :TT], fo[:, f, chs], hsrc[:], True, True, ['fo', hsk], [pk])
                hk = ('hft', f % 2)
                kb.tt(hft[:, f % 2, :], ps[:, :TT], wsrc[:], ALU.mult, [pk, wsk, hk], [hk])
                if ti == 0:
                    if f < 2:
                        kb.tt(hft[:, f % 2, 0:1], hft[:, f % 2, 0:1], skip[:, cb, f:f + 1], ALU.add, [hk, 'skip'], [hk])
                    else:
                        kb.memset(hft[:, f % 2, 0:1], 0.0, [hk], eng='dve')
                kb.store(HFs[f, chs, tsl], hft[:, f % 2, :], hk, ('HFs', f, cb, ti), final=False)
                hfkeys.append(('HFs', f, cb, ti))
    S.barrier()
    kb.aoff = aoff_p3

    def mk_lane(tag, n, ys=False):
        d = dict(tag=tag, n=n, As=kb.sb('As' + tag, [128, n, 2, 128]), Bt=kb.sb('Bt' + tag, [128, n, 3, 128]),
                 tw=[kb.sb('tw%d%s' % (i, tag), [128, n, 128]) for i in range(4)])
        d['Tre'] = T_re.unsqueeze(1).to_broadcast([128, n, 128]); d['Tim'] = T_im.unsqueeze(1).to_broadcast([128, n, 128])
        if ys:
            d['Ys'] = kb.sb('Ys' + tag, [128, n, 2, 128])
        return d
    HG = CG // 2
    LF, LD0, LD1 = mk_lane('F', CG), mk_lane('D0', HG, True), mk_lane('D1', HG, True)
    Xgs = [kb.sb('Xg%d' % i, [64, 3, CG, 128]) for i in range(2)]
    Hx = [kb.sb('Hx%d' % i, [128, CG, 128]) for i in range(2)]
    KSp = [kb.sb('KSp%d' % i, [128, 2, CG, 2, 128]) for i in range(2)]

    def fwd_fft(ln, Xv, xkey, dst, dkey, fmat=None):
        fmat = F64c if fmat is None else fmat
        t = ln['tag']; As = ln['As']; Bt = ln['Bt']; tw = ln['tw']; NL = ln['n']; Tre_b = ln['Tre']; Tim_b = ln['Tim']
        ak, bk = 'As' + t, 'Bt' + t
        tk = ['tw%d%s' % (i, t) for i in range(4)]
        for c0 in range(0, NL, 2):
            ps, pk = kb.psum()
            for c in (c0, c0 + 1):
                kb.mm(ps[:, (c - c0) * 256:(c - c0 + 1) * 256], Xv[:, c, :], fmat, True, True, (list(xkey) if isinstance(xkey, list) else [xkey]) + ['fc'], [pk])
            kb.cp(As[:, c0:c0 + 2, :, :].rearrange("p a b c -> p (a b c)"), ps[:, :], [pk, ak], [ak], eng='act')
            yield
        Are, Aim = As[:, :, 0, :], As[:, :, 1, :]
        kb.tt(tw[0][:], Are, Tre_b, ALU.mult, [ak, 'fc', tk[0]], [tk[0]])
        kb.tt(tw[1][:], Aim, Tim_b, ALU.mult, [ak, 'fc', tk[1]], [tk[1]], eng='pool')
        kb.tt(tw[2][:], Are, Tim_b, ALU.mult, [ak, 'fc', tk[2]], [tk[2]], eng='pool')
        kb.tt(tw[3][:], Aim, Tre_b, ALU.mult, [ak, 'fc', tk[3]], [tk[3]])
        yield
        kb.tt(Bt[:, :, 1, :], tw[0][:], tw[1][:], ALU.subtract, [tk[0], tk[1], bk], [bk])
        kb.tt(Bt[:, :, 2, :], tw[2][:], tw[3][:], ALU.add, [tk[2], tk[3], bk], [bk], eng='pool')
        yield
        kb.ts(Bt[:, :, 0, :], Bt[:, :, 2, :], -1.0, None, ALU.mult, None, [bk], [bk])
        yield
        for c0 in range(0, NL, 2):
            ps, pk = kb.psum()
            o = ps[:, :].rearrange("p (c x) -> p c x", c=2)
            kb.mm(o, F_re, Bt[:, c0:c0 + 2, 1:3, :].rearrange("p c a b -> p c (a b)"), True, False, [bk, 'fc'], [pk])
            kb.mm(o, F_im, Bt[:, c0:c0 + 2, 0:2, :].rearrange("p c a b -> p c (a b)"), False, True, [bk, 'fc'], [pk])
            kb.cp(dst[:, c0:c0 + 2, :, :].rearrange("p a b c -> p (a b c)"), ps[:, :], [pk, dkey], [dkey], eng='act')
            yield

    def conv(ln, zidx, gidx, Kt, kkey, order, Xg, xs, coff):
        t = ln['tag']; As = ln['As']; Bt = ln['Bt']; tw = ln['tw']; NL = ln['n']; Tre_b = ln['Tre']; Tim_b = ln['Tim']; Ys = ln['Ys']
        ak, bk, yk = 'As' + t, 'Bt' + t, 'Ys' + t
        tk = ['tw%d%s' % (i, t) for i in range(4)]
        csl = slice(coff, coff + NL)
        xz, xg_ = ('Xg', xs, zidx, coff), ('Xg', xs, gidx)
        yield from fwd_fft(ln, Xg[:, zidx, csl, :], [('Xg', xs, zidx), xz], As, ak)
        Xre, Xim = As[:, :, 0, :], As[:, :, 1, :]
        Kre, Kim = Kt[:, order, csl, 0, :], Kt[:, order, csl, 1, :]
        kb.tt(tw[0][:], Xre, Kre, ALU.mult, [ak, kkey, tk[0]], [tk[0]])
        kb.tt(tw[1][:], Xim, Kim, ALU.mult, [ak, kkey, tk[1]], [tk[1]], eng='pool')
        kb.tt(tw[2][:], Xre, Kim, ALU.mult, [ak, kkey, tk[2]], [tk[2]], eng='pool')
        kb.tt(tw[3][:], Xim, Kre, ALU.mult, [ak, kkey, tk[3]], [tk[3]])
        yield
        kb.tt(Ys[:, :, 0, :], tw[0][:], tw[1][:], ALU.subtract, [tk[0], tk[1], yk], [yk])
        kb.tt(Ys[:, :, 1, :], tw[2][:], tw[3][:], ALU.add, [tk[2], tk[3], yk], [yk], eng='pool')
        yield
        Cs = As
        for c0 in range(0, NL, 2):
            ps, pk = kb.psum()
            for c in (c0, c0 + 1):
                o = ps[:, (c - c0) * 256:(c - c0 + 1) * 256]
                kb.mm(o, Ys[:, c, 0, :], G1, True, False, [yk, 'fc'], [pk])
                kb.mm(o, Ys[:, c, 1, :], G2, False, True, [yk, 'fc'], [pk])
            kb.cp(Cs[:, c0:c0 + 2, :, :].rearrange("p a b c -> p (a b c)"), ps[:, :], [pk, ak], [ak], eng='act')
            yield
        Cre, Cim = Cs[:, :, 0, :], Cs[:, :, 1, :]
        Cp = Bt
        kb.tt(tw[0][:], Cre, Tre_b, ALU.mult, [ak, 'fc', tk[0]], [tk[0]])
        kb.tt(tw[1][:], Cim, Tim_b, ALU.mult, [ak, 'fc', tk[1]], [tk[1]], eng='pool')
        kb.tt(tw[2][:], Cim, Tre_b, ALU.mult, [ak, 'fc', tk[2]], [tk[2]], eng='pool')
        kb.tt(tw[3][:], Cre, Tim_b, ALU.mult, [ak, 'fc', tk[3]], [tk[3]])
        yield
        kb.tt(Cp[:, :, 0, :], tw[0][:], tw[1][:], ALU.add, [tk[0], tk[1], bk], [bk])
        kb.tt(Cp[:, :, 1, :], tw[2][:], tw[3][:], ALU.subtract, [tk[2], tk[3], bk], [bk], eng='pool')
        yield
        for c0 in range(0, NL, 4):
            ps, pk = kb.psum()
            o = ps[0:64, :].rearrange("p (c x) -> p c x", c=4)
            kb.mm(o, F_re[:, 0:64], Cp[:, c0:c0 + 4, 0, :], True, False, [bk, 'fc'], [pk])
            kb.mm(o, F_im[:, 0:64], Cp[:, c0:c0 + 4, 1, :], False, True, [bk, 'fc'], [pk])
            kb.tt(Xg[:, zidx, coff + c0:coff + c0 + 4, :].rearrange("p a b -> p (a b)"), ps[0:64, :], Xg[:, gidx, coff + c0:coff + c0 + 4, :].rearrange("p a b -> p (a b)"), ALU.mult, [pk, ('Xg', xs, gidx), ('Xg', xs, zidx), xz], [xz])
            yield

    F128c = fc[:, 256:512]

    def filter_lane(gi):
        ch = slice(gi * CG, (gi + 1) * CG)
        Kt = KSp[gi % 2]; kkey = ('KSp', gi % 2)
        for order in range(2):
            hb = Hx[order]; hk = ('Hx', order)
            S.dma('sp', hb[0:64, :, :], HFs[order, ch, :].rearrange("c (a b) -> a c b", b=128), reads=hfkeys, writes=[hk])
            S.dma('sp', hb[64:128, :, :], HFs[2 + order, ch, :].rearrange("c (a b) -> a c b", b=128), reads=hfkeys, writes=[hk])
            yield from fwd_fft(LF, hb, hk, Kt[:, order, :, :, :], kkey, fmat=F128c)
        kb.ts(Kt[:].rearrange("p a b c d -> p (a b c d)"), Kt[:].rearrange("p a b c d -> p (a b c d)"), 1.0 / NFFT, None, ALU.mult, None, [kkey], [kkey])
        yield

    def half_lane(ln, gi, b, Xg, xs, coff, Kt, kkey):
        yield from conv(ln, 0, 1, Kt, kkey, 0, Xg, xs, coff)
        yield from conv(ln, 0, 2, Kt, kkey, 1, Xg, xs, coff)

    def data_lane(gi):
        ch = slice(gi * CG, (gi + 1) * CG)
        Kt = KSp[gi % 2]; kkey = ('KSp', gi % 2)
        for b in range(nb):
            xs = (gi * nb + b) % 2
            Xg = Xgs[xs]
            for seg_ in range(3):
                S.dma('sp', Xg[:, seg_, :, :], UC[seg_, b, ch, :].rearrange("c (a b) -> a c b", b=128), reads=uckeys, writes=[('Xg', xs, seg_)])
            yield from both([half_lane(LD0, gi, b, Xg, xs, 0, Kt, kkey), half_lane(LD1, gi, b, Xg, xs, HG, Kt, kkey)])
            S.dma('pool', Z2T[ch, b, :].rearrange("c (a b) -> a c b", b=128), Xg[:, 0, :, :], reads=[('Xg', xs, 0), ('Xg', xs, 0, 0), ('Xg', xs, 0, HG)], writes=[('Z2T', gi, b)], semkey=('Xgst', xs))
            kb.outkeys.append(('Z2T', gi, b))
            yield

    def both(gens):
        gens = list(gens)
        while gens:
            for gnr in list(gens):
                try:
                    next(gnr)
                except StopIteration:
                    gens.remove(gnr)
            yield

    for step in range(ngroups + 1):
        lanes = []
        if step < ngroups:
            lanes.append(filter_lane(step))
        if step >= 1:
            lanes.append(data_lane(step - 1))
        while lanes:
            for g in list(lanes):
                try:
                    next(g)
                except StopIteration:
                    lanes.remove(g)
    return kb.finish()


def hyc_inputs(inp, c0, nblk):
    o = 0
    nch = nblk * 128
    fc, zp, win = hy_consts()
    cw = inp['hy_conv_w'][o]
    taps = np.zeros((128, nblk, 3, 3), np.float32)
    for cb in range(nblk):
        for seg in range(3):
            ch0 = seg * D + c0 + cb * 128
            taps[:, cb, seg, :] = cw[:, ch0:ch0 + 128].T
    fo_full = inp['hy_filt_out'][o].reshape(64, 2, 2, D)
    fo = np.ascontiguousarray(fo_full[:, :, :, c0:c0 + nch].reshape(64, 4, nch))
    sk = inp['hy_skip'][o][:, c0:c0 + nch]
    skip = np.ascontiguousarray(sk.reshape(2, nblk, 128).transpose(2, 1, 0))
    return dict(
        taps=taps, zp=zp, win=np.ascontiguousarray(win[:, c0:c0 + nch]),
        w1=np.ascontiguousarray(inp['hy_pos_w1'][o]), w23=np.ascontiguousarray(np.stack([inp['hy_pos_w2'][o], inp['hy_pos_w3'][o]], axis=1)),
        bfr=np.ascontiguousarray(np.stack([inp['hy_pos_b1'][o], inp['hy_pos_b2'][o], inp['hy_pos_b3'][o], inp['hy_freq'][o]], axis=1)),
        fo=fo, skip=skip, fc=fc)


ARENA = 53200


def build_fused():
    kb = KB(fused=True, arena_floats=ARENA)
    nc = kb.nc
    L = LSEQ
    XT = nc.dram_tensor('XT', [D, L], F32, kind="ExternalInput").ap()
    CXT = nc.dram_tensor('CXT', [D, 256], F32, kind="ExternalInput").ap()
    Yint = nc.dram_tensor('Yint', [D, L], F32, kind="Internal").ap()
    H0 = nc.dram_tensor('H0int', [D, L], F32, kind="Internal").ap()
    U = nc.dram_tensor('Uint', [3 * D, L], F32, kind="Internal").ap()
    Z2 = nc.dram_tensor('Z2int', [D, 1, L], F32, kind="Internal").ap()
    kb.init_psum(8)
    for hh in range(2):
        kb.next_stage('a%d_' % hh, {'XT': XT, 'CXT': CXT, 'YT': Yint})
        kb.yoffs = (hh * 256, 512 + hh * 256)
        build_mixa(kb=kb)
    kb.next_stage('b_', {'HT': XT, 'YT': Yint, 'HO': H0, 'UT': U})
    build_post(L, last=False, kb=kb)
    kb.next_stage('c_', {'UT3': U.rearrange("(s c) (b t) -> s c b t", s=3, b=1), 'Z2T': Z2})
    build_hyc(nb=1, nblk=8, kb=kb)
    kb.next_stage('d_', {'HT': H0, 'YT': Z2.rearrange("c b t -> c (b t)")})
    kb.dyn_tok = True
    kb.dyn_half = L // 2
    build_post(L // 2, last=True, kb=kb)
    kb.stage_amax = kb.amax
    return kb.finish(force=True)


def fused_inputs(inp, core):
    b, r = core // 2, core % 2
    m = {}
    for hh in range(2):
        a = mixa_inputs(inp, b, hh)
        m['XT'] = a.pop('XT'); m['CXT'] = a.pop('CXT')
        for k, v in a.items():
            m['a%d_%s' % (hh, k)] = v
    mw, mb = inp['mod_w'], inp['mod_b']
    modw = np.concatenate([mw[0][:, 2 * D:6 * D], mw[1][:, 0:2 * D]], axis=1)
    modb = np.concatenate([mb[0][2 * D:6 * D], mb[1][0:2 * D]])
    bd = dict(wo=blk_w(inp['ab_w_out'][0], 8), modw=blk_w(modw, 8), modb=fm_vec(modb), cT=fm_vec(inp['c'][b]),
              nw=np.ascontiguousarray(np.stack([fm_vec(inp['norm2_w'][0]), fm_vec(inp['norm1_w'][1])], axis=-1)),
              w1=blk_w(inp['ffn_w1'][0], 8), w3=blk_w(inp['ffn_w3'][0], 8), w2=blk_w(inp['ffn_w2'][0], 22),
              wn=blk_w(inp['hy_w_in'][0], 8))
    for k, v in bd.items():
        m['b_' + k] = v
    for k, v in hyc_inputs(inp, 0, 8).items():
        m['c_' + k] = v
    dd = dict(wo=blk_w(inp['hy_w_out'][0], 8), modw=blk_w(np.ascontiguousarray(mw[1][:, 2 * D:6 * D]), 8), modb=fm_vec(mb[1][2 * D:6 * D]), cT=fm_vec(inp['c'][b]),
              nw=np.ascontiguousarray(np.stack([fm_vec(inp['norm2_w'][1]), fm_vec(inp['final_norm_w'])], axis=-1)),
              w1=blk_w(inp['ffn_w1'][1], 8), w3=blk_w(inp['ffn_w3'][1], 8), w2=blk_w(inp['ffn_w2'][1], 22))
    for k, v in dd.items():
        m['d_' + k] = v
    return m


def kernel(**inputs):
    inp = {k: np.asarray(v, dtype=np.float32) for k, v in inputs.items()}
    B, L = 4, LSEQ
    cores = list(range(NCORES))
    nc = build_fused()
    shared = {}
    maps = []
    for c in cores:
        m = fused_inputs(inp, c)
        for k in list(m.keys()):
            if k in shared and shared[k].shape == m[k].shape and k not in ('XT', 'CXT') and not k.endswith('cT') and not k.startswith('a'):
                m[k] = shared[k]
            else:
                shared.setdefault(k, m[k])
        maps.append(m)
    res = run_bass_kernel_spmd(nc, maps, core_ids=cores)
    out = np.zeros((B, L, D), np.float32)
    NT = L // 2
    for c in cores:
        b, r = c // 2, c % 2
        out[b, r * NT:(r + 1) * NT, :] = res.results[c]['d_HO'].T
    return out
```
